# Optimizing a Trainium2 kernel written in Bass

```python
import math
import jax
import jax.numpy as jnp
from jax import lax
import numpy as np

D_MODEL = 1024
BATCH = 2
SEQ = 8192
DEPTH = 4
DEC_BATCH = 8
DEC_SEQ = 4096
PAST_LEN = 128

F32 = jnp.float32
N_MEM = 256
GRID_W = 64
Q_BLOCK = 128
D_FF = 4 * D_MODEL
NORM_EPS = 1e-6
ROPE_THETA = 500000.0
AXIAL_THETA = 10000.0
NEG_INF = -1e30
N_MIXERS = 4

A_PATTERNS = ((128, 1), (512, 4), (2048, 16))
A_GROUPS = len(A_PATTERNS)
A_HEADS = 8
A_HEAD_DIM = 64
A_ROT = A_HEAD_DIM // 4
A_IN = A_GROUPS * 3 * A_HEADS * A_HEAD_DIM
B_HEADS = 16
B_KV_HEADS = 4
B_HEAD_DIM = 64
B_IN = (B_HEADS + 2 * B_KV_HEADS) * B_HEAD_DIM
C_HEADS = 8
C_DIM = 64
C_ROT = C_DIM // 4
C_IN = 3 * C_HEADS * 2 * C_DIM
D_HEADS = 16
D_Q_RANK = 384
D_KV_RANK = 256
D_NOPE = 64
D_ROPE = 32
D_V = 64
D_IN = D_Q_RANK + D_KV_RANK + D_ROPE
X_HEADS = 4
X_HEAD_DIM = 128
N_A = (DEPTH + 3) // 4
N_B = (DEPTH + 2) // 4
N_C = (DEPTH + 1) // 4
N_D = DEPTH // 4

kernel_name = 'hybrid_bidir_encoder_interleaved'


def rmsnorm(x, g):
    xf = x.astype(F32)
    y = xf * lax.rsqrt(jnp.mean(xf * xf, axis=-1, keepdims=True) + NORM_EPS)
    return (y * g.astype(F32)).astype(x.dtype)


def rope(x, pos, theta, rot):
    half = rot // 2
    inv_freq = jnp.exp(jnp.arange(half, dtype=F32) * (-2.0 * math.log(theta) / rot))
    ang = pos.astype(F32)[:, None] * inv_freq[None, :]
    bshape = (1, pos.shape[0]) + (1,) * (x.ndim - 3) + (half,)
    cos = jnp.cos(ang).reshape(bshape).astype(x.dtype)
    sin = jnp.sin(ang).reshape(bshape).astype(x.dtype)
    x1 = x[..., :half]
    x2 = x[..., half:rot]
    return jnp.concatenate([x1 * cos - x2 * sin, x2 * cos + x1 * sin, x[..., rot:]], axis=-1)


def axial_rope(x, rows, cols):
    half = x.shape[-1] // 2
    return jnp.concatenate([rope(x[..., :half], rows, AXIAL_THETA, half),
                            rope(x[..., half:], cols, AXIAL_THETA, half)], axis=-1)


def to_blocks(t):
    b, s = t.shape[:2]
    return jnp.moveaxis(t.reshape((b, s // Q_BLOCK, Q_BLOCK) + t.shape[2:]), 1, 0)


def from_blocks(t):
    nb, b, qb = t.shape[:3]
    return jnp.moveaxis(t, 0, 1).reshape((b, nb * qb) + t.shape[3:])


def dilated_window_attention(q, k, v, window, dilation):
    b, s, h, dh = q.shape
    n = window // (2 * dilation)
    L = s // dilation
    c = n
    nb = -(-L // c)
    lp = nb * c

    def split(t):
        t = t.reshape(b, L, dilation, h, dh).transpose(0, 2, 1, 3, 4)
        t = jnp.pad(t, ((0, 0), (0, 0), (0, lp - L), (0, 0), (0, 0)))
        return t.reshape(b, dilation, nb, c, h, dh)

    def neighbours(t):
        tp = jnp.pad(t, ((0, 0), (0, 0), (1, 1), (0, 0), (0, 0), (0, 0)))
        return jnp.concatenate([tp[:, :, :-2], tp[:, :, 1:-1], tp[:, :, 2:]], axis=3)

    qb = split(q)
    kn = neighbours(split(k))
    vn = neighbours(split(v))
    iq = jnp.arange(nb)[:, None, None] * c + jnp.arange(c)[None, :, None]
    ik = (jnp.arange(nb)[:, None, None] - 1) * c + jnp.arange(3 * c)[None, None, :]
    valid = (jnp.abs(iq - ik) <= n) & (ik >= 0) & (ik < L)
    sc = jnp.einsum('brnqhd,brnkhd->brnhqk', qb, kn).astype(F32) * dh ** -0.5
    sc = jnp.where(valid[None, None, :, None], sc, NEG_INF)
    mx = jnp.max(sc, axis=-1, keepdims=True)
    e = jnp.exp(sc - mx)
    den = jnp.sum(e, axis=-1, keepdims=True)
    lse = (mx + jnp.log(den))[..., 0]
    o = jnp.einsum('brnhqk,brnkhd->brnqhd', (e / den).astype(v.dtype), vn)
    o = o.reshape(b, dilation, lp, h, dh)[:, :, :L].transpose(0, 2, 1, 3, 4).reshape(b, s, h, dh)
    lse = lse.transpose(0, 1, 2, 4, 3).reshape(b, dilation, lp, h)[:, :, :L]
    lse = lse.transpose(0, 2, 1, 3).reshape(b, s, h)
    return o, lse


def mixer_dilated(h, w_in, w_out, pos):
    b, s, _ = h.shape
    qkv = (h @ w_in).reshape(b, s, A_GROUPS, 3, A_HEADS, A_HEAD_DIM)
    outs = []
    lses = []
    for g, (window, dil) in enumerate(A_PATTERNS):
        q = rope(qkv[:, :, g, 0], pos, ROPE_THETA, A_ROT)
        k = rope(qkv[:, :, g, 1], pos, ROPE_THETA, A_ROT)
        o, lse = dilated_window_attention(q, k, qkv[:, :, g, 2], window, dil)
        outs.append(o)
        lses.append(lse)
    alpha = jax.nn.softmax(jnp.stack(lses), axis=0)
    o = jnp.einsum('gbsh,gbshd->bshd', alpha.astype(h.dtype), jnp.stack(outs))
    return o.reshape(b, s, A_HEADS * A_HEAD_DIM) @ w_out


def mixer_gqa_axial(h, w_in, q_gain, k_gain, w_out, rows, cols):
    b, s, _ = h.shape
    dh = B_HEAD_DIM
    grp = B_HEADS // B_KV_HEADS
    qkv = h @ w_in
    q = qkv[..., :B_HEADS * dh].reshape(b, s, B_HEADS, dh)
    k = qkv[..., B_HEADS * dh:(B_HEADS + B_KV_HEADS) * dh].reshape(b, s, B_KV_HEADS, dh)
    v = qkv[..., (B_HEADS + B_KV_HEADS) * dh:].reshape(b, s, B_KV_HEADS, dh)
    q = axial_rope(rmsnorm(q, q_gain), rows, cols).reshape(b, s, B_KV_HEADS, grp, dh)
    k = axial_rope(rmsnorm(k, k_gain), rows, cols)
    scale = dh ** -0.5

    def block(qb):
        sc = jnp.einsum('bqhgd,bkhd->bhgqk', qb, k).astype(F32) * scale
        p = jax.nn.softmax(sc, axis=-1).astype(v.dtype)
        return jnp.einsum('bhgqk,bkhd->bqhgd', p, v)

    o = from_blocks(lax.map(block, to_blocks(q)))
    return o.reshape(b, s, B_HEADS * dh) @ w_out


def mixer_diff(h, w_in, lq1, lk1, lq2, lk2, sub_gain, w_out, pos, lambda_init):
    b, s, _ = h.shape
    qkv = (h @ w_in).reshape(b, s, 3, C_HEADS, 2 * C_DIM)
    q = rope(qkv[:, :, 0].reshape(b, s, C_HEADS, 2, C_DIM), pos, ROPE_THETA, C_ROT)
    k = rope(qkv[:, :, 1].reshape(b, s, C_HEADS, 2, C_DIM), pos, ROPE_THETA, C_ROT)
    v = qkv[:, :, 2]
    lam = (jnp.exp(jnp.sum(lq1.astype(F32) * lk1.astype(F32)))
           - jnp.exp(jnp.sum(lq2.astype(F32) * lk2.astype(F32))) + lambda_init)
    scale = C_DIM ** -0.5

    def block(qb):
        sc = jnp.einsum('bqhcd,bkhcd->bchqk', qb, k).astype(F32) * scale
        p = jax.nn.softmax(sc, axis=-1)
        a = (p[:, 0] - lam * p[:, 1]).astype(v.dtype)
        return jnp.einsum('bhqk,bkhd->bqhd', a, v)

    o = from_blocks(lax.map(block, to_blocks(q)))
    o = rmsnorm(o, sub_gain) * (1.0 - lambda_init)
    return o.reshape(b, s, C_HEADS * 2 * C_DIM) @ w_out


def mixer_mla(h, w_in, q_gain, kv_gain, w_uq, w_ukv, w_out, pos):
    b, s, _ = h.shape
    cmb = h @ w_in
    c_q = rmsnorm(cmb[..., :D_Q_RANK], q_gain)
    c_kv = rmsnorm(cmb[..., D_Q_RANK:D_Q_RANK + D_KV_RANK], kv_gain)
    k_rope = rope(cmb[..., D_Q_RANK + D_KV_RANK:][:, :, None, :], pos, ROPE_THETA, D_ROPE)[:, :, 0]
    q = (c_q @ w_uq).reshape(b, s, D_HEADS, D_NOPE + D_ROPE)
    q_nope = q[..., :D_NOPE]
    q_rope = rope(q[..., D_NOPE:], pos, ROPE_THETA, D_ROPE)
    kv = (c_kv @ w_ukv).reshape(b, s, D_HEADS, D_NOPE + D_V)
    k_nope = kv[..., :D_NOPE]
    v = kv[..., D_NOPE:]
    scale = (D_NOPE + D_ROPE) ** -0.5

    def block(qs):
        qn, qr = qs
        sc = (jnp.einsum('bqhd,bkhd->bhqk', qn, k_nope)
              + jnp.einsum('bqhr,bkr->bhqk', qr, k_rope)).astype(F32) * scale
        p = jax.nn.softmax(sc, axis=-1).astype(v.dtype)
        return jnp.einsum('bhqk,bkhd->bqhd', p, v)

    o = from_blocks(lax.map(block, (to_blocks(q_nope), to_blocks(q_rope))))
    return o.reshape(b, s, D_HEADS * D_V) @ w_out


def memory_cross_attention(h, mem, mem_gain, w_q, w_kv, w_o):
    b, s, _ = h.shape
    n_mem = mem.shape[1]
    m = rmsnorm(mem, mem_gain)
    q = (h @ w_q).reshape(b, s, X_HEADS, X_HEAD_DIM)
    kv = (m @ w_kv).reshape(b, n_mem, 2, X_HEADS, X_HEAD_DIM)
    sc = jnp.einsum('bqhd,bmhd->bhqm', q, kv[:, :, 0]).astype(F32) * X_HEAD_DIM ** -0.5
    p = jax.nn.softmax(sc, axis=-1).astype(h.dtype)
    o = jnp.einsum('bhqm,bmhd->bqhd', p, kv[:, :, 1])
    return o.reshape(b, s, X_HEADS * X_HEAD_DIM) @ w_o


def sqrelu_mlp(h, w_in, w_out):
    a = jax.nn.relu(h @ w_in)
    return (a * a) @ w_out


def run_trunk(x, mem, p):
    s = x.shape[1]
    n_rows = s // GRID_W
    pos = jnp.arange(s, dtype=F32)
    rows = jnp.repeat(jnp.arange(n_rows, dtype=F32), GRID_W)
    cols = jnp.tile(jnp.arange(GRID_W, dtype=F32), n_rows)
    for i in range(DEPTH):
        m, j = i % N_MIXERS, i // N_MIXERS
        h = rmsnorm(x, p['norm_mix'][i])
        if m == 0:
            mix = mixer_dilated(h, p['a_w_in'][j], p['a_w_out'][j], pos)
        elif m == 1:
            mix = mixer_gqa_axial(h, p['b_w_in'][j], p['b_q_norm'][j], p['b_k_norm'][j],
                                  p['b_w_out'][j], rows, cols)
        elif m == 2:
            mix = mixer_diff(h, p['c_w_in'][j], p['c_lambda_q1'][j], p['c_lambda_k1'][j],
                             p['c_lambda_q2'][j], p['c_lambda_k2'][j], p['c_sub_norm'][j],
                             p['c_w_out'][j], pos, 0.8 - 0.6 * math.exp(-0.3 * i))
        else:
            mix = mixer_mla(h, p['d_w_in'][j], p['d_q_norm'][j], p['d_kv_norm'][j],
                            p['d_w_uq'][j], p['d_w_ukv'][j], p['d_w_out'][j], pos)
        x = x + mix
        x = x + memory_cross_attention(rmsnorm(x, p['norm_x'][i]), mem, p['norm_mem'][i],
                                       p['w_xq'][i], p['w_xkv'][i], p['w_xo'][i])
        x = x + sqrelu_mlp(rmsnorm(x, p['norm_mlp'][i]), p['w_mlp_in'][i], p['w_mlp_out'][i])
    return rmsnorm(x, p['final_norm'])


def setup_inputs(seed: int = 0) -> dict:
    key = jax.random.key(seed)
    ks = iter(jax.random.split(key, 40))

    def act(shape):
        return jax.random.normal(next(ks), shape, F32)

    def w(shape, fan_in):
        return jax.random.normal(next(ks), shape, F32) * fan_in ** -0.5

    def gain(shape):
        return 1.0 + 0.05 * jax.random.normal(next(ks), shape, F32)

    def small(shape):
        return 0.1 * jax.random.normal(next(ks), shape, F32)

    D = D_MODEL
    return {
        'x_prompt': act((BATCH, SEQ, D)),
        'x_sample': act((DEC_BATCH, DEC_SEQ, D)),
        'mem_prompt': act((BATCH, N_MEM, D)),
        'mem_sample': act((DEC_BATCH, N_MEM, D)),
        'norm_mix': gain((DEPTH, D)),
        'norm_x': gain((DEPTH, D)),
        'norm_mem': gain((DEPTH, D)),
        'w_xq': w((DEPTH, D, X_HEADS * X_HEAD_DIM), D),
        'w_xkv': w((DEPTH, D, 2 * X_HEADS * X_HEAD_DIM), D),
        'w_xo': w((DEPTH, X_HEADS * X_HEAD_DIM, D), X_HEADS * X_HEAD_DIM),
        'norm_mlp': gain((DEPTH, D)),
        'w_mlp_in': w((DEPTH, D, D_FF), D),
        'w_mlp_out': w((DEPTH, D_FF, D), D_FF),
        'a_w_in': w((N_A, D, A_IN), D),
        'a_w_out': w((N_A, A_HEADS * A_HEAD_DIM, D), A_HEADS * A_HEAD_DIM),
        'b_w_in': w((N_B, D, B_IN), D),
        'b_q_norm': gain((N_B, B_HEAD_DIM)),
        'b_k_norm': gain((N_B, B_HEAD_DIM)),
        'b_w_out': w((N_B, B_HEADS * B_HEAD_DIM, D), B_HEADS * B_HEAD_DIM),
        'c_w_in': w((N_C, D, C_IN), D),
        'c_lambda_q1': small((N_C, C_DIM)),
        'c_lambda_k1': small((N_C, C_DIM)),
        'c_lambda_q2': small((N_C, C_DIM)),
        'c_lambda_k2': small((N_C, C_DIM)),
        'c_sub_norm': gain((N_C, 2 * C_DIM)),
        'c_w_out': w((N_C, C_HEADS * 2 * C_DIM, D), C_HEADS * 2 * C_DIM),
        'd_w_in': w((N_D, D, D_IN), D),
        'd_q_norm': gain((N_D, D_Q_RANK)),
        'd_kv_norm': gain((N_D, D_KV_RANK)),
        'd_w_uq': w((N_D, D_Q_RANK, D_HEADS * (D_NOPE + D_ROPE)), D_Q_RANK),
        'd_w_ukv': w((N_D, D_KV_RANK, D_HEADS * (D_NOPE + D_V)), D_KV_RANK),
        'd_w_out': w((N_D, D_HEADS * D_V, D), D_HEADS * D_V),
        'final_norm': gain((D,)),
    }


def reference(x_prompt, x_sample, mem_prompt, mem_sample, norm_mix, norm_x, norm_mem, w_xq, w_xkv,
              w_xo, norm_mlp, w_mlp_in, w_mlp_out, a_w_in, a_w_out, b_w_in, b_q_norm, b_k_norm,
              b_w_out, c_w_in, c_lambda_q1, c_lambda_k1, c_lambda_q2, c_lambda_k2, c_sub_norm,
              c_w_out, d_w_in, d_q_norm, d_kv_norm, d_w_uq, d_w_ukv, d_w_out, final_norm):
    p = dict(norm_mix=norm_mix, norm_x=norm_x, norm_mem=norm_mem, w_xq=w_xq, w_xkv=w_xkv,
             w_xo=w_xo, norm_mlp=norm_mlp, w_mlp_in=w_mlp_in, w_mlp_out=w_mlp_out,
             a_w_in=a_w_in, a_w_out=a_w_out, b_w_in=b_w_in, b_q_norm=b_q_norm,
             b_k_norm=b_k_norm, b_w_out=b_w_out, c_w_in=c_w_in, c_lambda_q1=c_lambda_q1,
             c_lambda_k1=c_lambda_k1, c_lambda_q2=c_lambda_q2, c_lambda_k2=c_lambda_k2,
             c_sub_norm=c_sub_norm, c_w_out=c_w_out, d_w_in=d_w_in, d_q_norm=d_q_norm,
             d_kv_norm=d_kv_norm, d_w_uq=d_w_uq, d_w_ukv=d_w_ukv, d_w_out=d_w_out,
             final_norm=final_norm)
    y_prompt = run_trunk(x_prompt, mem_prompt, p)
    y_sample = run_trunk(x_sample, mem_sample, p)
    return (y_prompt, y_sample)
```

```python
import math
import contextlib
import numpy as np
import concourse.bass as bass
import concourse.mybir as mybir
from concourse.bass_utils import run_bass_kernel_spmd

F32 = mybir.dt.float32
BF16 = mybir.dt.bfloat16
AF = mybir.ActivationFunctionType
ALU = mybir.AluOpType

D = 1024
KC = 8
ST = 512
DEPTH = 4
EPS = 1e-6
DMA_RING = 8


class Buf:
    __slots__ = ("last_w", "readers", "excl")

    def __init__(self, excl=False):
        self.last_w = None
        self.readers = {}
        self.excl = excl


class Op:
    __slots__ = ("q", "fn", "kind", "deps", "signal", "sigval", "slot", "rnd", "phase")


class Sched:
    ENGS = ("pe", "act", "dve", "pool", "sp")

    def __init__(self, nc, st):
        self.nc = nc
        self.ops = {q: [] for q in self.ENGS}
        self.ndma = {q: 0 for q in self.ENGS}
        self.sigcnt = {q: 0 for q in self.ENGS}
        self.phase = 0
        self.csem = {q: st.enter_context(nc.semaphore("c_" + q)) for q in self.ENGS}
        self.dsem = {q: [st.enter_context(nc.semaphore("d_%s_%d" % (q, i))) for i in range(DMA_RING)]
                     for q in ("sp", "pool")}
        self.waited = {q: {} for q in self.ENGS}

    def op(self, q, fn, reads=(), writes=(), kind="c"):
        o = Op()
        o.q = q
        o.fn = fn
        o.kind = kind
        o.signal = False
        o.sigval = 0
        o.phase = self.phase
        deps = {}
        xr = [b for b in reads if b.excl]
        if xr:
            reads = [b for b in reads if not b.excl]
            writes = list(writes) + xr
        for b in reads:
            w = b.last_w
            if w is not None:
                deps[id(w)] = w
        for b in writes:
            w = b.last_w
            if w is not None:
                deps[id(w)] = w
            for r in b.readers.values():
                deps[id(r)] = r
        for b in reads:
            if kind == "c":
                b.readers[q] = o
            else:
                b.readers[("d", q, self.ndma[q] % (4 * DMA_RING))] = o
        for b in writes:
            b.last_w = o
            b.readers = {}
        dl = []
        for d in deps.values():
            if d is o or d.phase != self.phase:
                continue
            if d.kind == "c":
                if d.q == q and q == "pe" and kind == "c":
                    continue
                d.signal = True
            dl.append(d)
        o.deps = dl
        if kind == "d":
            i = self.ndma[q]
            self.ndma[q] = i + 1
            o.slot = i % DMA_RING
            o.rnd = i // DMA_RING
        self.ops[q].append(o)
        return o

    def emit_phase(self):
        nc = self.nc
        for q in self.ENGS:
            c = self.sigcnt[q]
            for o in self.ops[q]:
                if o.kind == "c" and o.signal:
                    c += 1
                    o.sigval = c
            self.sigcnt[q] = c
        with nc.Block() as block:
            handles = {"pe": block.tensor, "act": block.scalar, "dve": block.vector,
                       "pool": block.gpsimd, "sp": block.sync}
            for q in self.ENGS:
                ops = self.ops[q]
                if not ops:
                    continue

                def body(eng, q=q, ops=ops):
                    waited = self.waited[q]

                    def wait(sem, key, val):
                        if waited.get(key, 0) >= val:
                            return
                        waited[key] = val
                        eng.wait_ge(sem, val)

                    for o in ops:
                        for d in o.deps:
                            if d.kind == "c":
                                wait(self.csem[d.q], ("c", d.q), d.sigval)
                            else:
                                wait(self.dsem[d.q][d.slot], ("d", d.q, d.slot), 16 * (d.rnd + 1))
                        if o.kind == "d":
                            if o.rnd > 0:
                                wait(self.dsem[q][o.slot], ("d", q, o.slot), 16 * o.rnd)
                            ins = o.fn(eng)
                            ins.then_inc(self.dsem[q][o.slot], 16)
                        else:
                            ins = o.fn(eng)
                            if o.signal:
                                ins.then_inc(self.csem[q], 1)
                    n = self.ndma[q]
                    if q in self.dsem and n > 0:
                        for s in range(min(DMA_RING, n)):
                            cnt = (n - 1 - s) // DMA_RING + 1
                            wait(self.dsem[q][s], ("d", q, s), 16 * cnt)

                handles[q](body)
        self.ops = {q: [] for q in self.ENGS}
        self.phase += 1


class Tile:
    def __init__(self, t, nb=1):
        self.t = t
        self.b = [Buf() for _ in range(nb)]

    @property
    def all(self):
        return self.b


WNAMES = ["w_xq", "w_xkv", "w_xo", "w_mlp_in", "w_mlp_out", "a_w_in", "a_w_out", "b_w_in", "b_w_out",
          "c_w_in", "c_w_out", "d_w_in", "d_w_uq", "d_w_ukv", "d_w_out"]
WSHAPES = {"w_xq": [4, 1024, 512], "w_xkv": [4, 1024, 1024], "w_xo": [4, 512, 1024],
           "w_mlp_in": [4, 1024, 4096], "w_mlp_out": [4, 4096, 1024], "a_w_in": [1, 1024, 4608],
           "a_w_out": [1, 512, 1024], "b_w_in": [1, 1024, 1536], "b_w_out": [1, 1024, 1024],
           "c_w_in": [1, 1024, 3072], "c_w_out": [1, 1024, 1024], "d_w_in": [1, 1024, 672],
           "d_w_uq": [1, 384, 1536], "d_w_ukv": [1, 256, 2048], "d_w_out": [1, 1024, 1024]}
GNAMES = {"norm_mix": [4, 1024], "norm_x": [4, 1024], "norm_mem": [4, 1024], "norm_mlp": [4, 1024],
          "final_norm": [1024], "b_q_norm": [1, 64], "b_k_norm": [1, 64], "c_lambda_q1": [1, 64],
          "c_lambda_k1": [1, 64], "c_lambda_q2": [1, 64], "c_lambda_k2": [1, 64], "c_sub_norm": [1, 128],
          "d_q_norm": [1, 384], "d_kv_norm": [1, 256]}
A_DILS = (1, 4, 16)


class Prog:
    def __init__(self, NH, depth=DEPTH):
        self.NH = NH
        self.N = 2 * NH
        self.NST = self.N // ST
        self.NT = self.N // 128
        self.depth = depth
        self.nc = bass.Bass("TRN2", target_bir_lowering=False)
        self.build()

    def dram_in(self, name, shape, dt=F32):
        return self.nc.dram_tensor(name, list(shape), dt, kind="ExternalInput").ap()

    def dram_tmp(self, name, shape, dt):
        return self.nc.dram_tensor(name, list(shape), dt, kind="Internal").ap()

    def sb(self, st, name, shape, dt, nb=1):
        self._uid = getattr(self, "_uid", 0) + 1
        return Tile(st.enter_context(self.nc.sbuf_tensor("s%d_%s" % (self._uid, name), list(shape), dt)), nb)

    def op(self, *a, **k):
        return self.S.op(*a, **k)

    def load(self, out_ap, in_ap, reads, writes, q="sp", slow=False):
        if slow:
            self.op(q, lambda e: e.dma_start(out=out_ap, in_=in_ap, allow_slow_non_contiguous=True),
                    reads=reads, writes=writes, kind="d")
        else:
            self.op(q, lambda e: e.dma_start(out=out_ap, in_=in_ap), reads=reads, writes=writes, kind="d")

    def mm(self, out_ap, lhsT, rhs, start, stop, reads, writes):
        self.op("pe", lambda e: e.matmul(out_ap, lhsT=lhsT, rhs=rhs, start=start, stop=stop),
                reads=reads, writes=writes)

    def tr(self, out_ap, in_ap, ident, reads, writes):
        self.op("pe", lambda e: e.transpose(out=out_ap, in_=in_ap, identity=ident), reads=reads, writes=writes)

    def act(self, out_ap, in_ap, func, reads, writes, scale=None, bias=None, accum=None):
        kw = {}
        if scale is not None:
            kw["scale"] = scale
        if bias is not None:
            kw["bias"] = bias
        if accum is not None:
            kw["accum_out"] = accum
        self.op("act", lambda e: e.activation(out=out_ap, in_=in_ap, func=func, **kw), reads=reads, writes=writes)

    def tt(self, out_ap, in0, in1, op, reads, writes, q="dve"):
        self.op(q, lambda e: e.tensor_tensor(out=out_ap, in0=in0, in1=in1, op=op), reads=reads, writes=writes)

    def stt(self, out_ap, in0, scalar, in1, op0, op1, reads, writes, q="dve"):
        self.op(q, lambda e: e.scalar_tensor_tensor(out=out_ap, in0=in0, scalar=scalar, in1=in1, op0=op0, op1=op1),
                reads=reads, writes=writes)

    def ts(self, out_ap, in0, s1, s2, op0, op1, reads, writes, q="dve"):
        self.op(q, lambda e: e.tensor_scalar(out=out_ap, in0=in0, scalar1=s1, scalar2=s2, op0=op0, op1=op1),
                reads=reads, writes=writes)

    def cp(self, out_ap, in_ap, reads, writes, q="dve"):
        if q == "act":
            self.act(out_ap, in_ap, AF.Copy, reads, writes)
        else:
            self.op(q, lambda e: e.tensor_copy(out=out_ap, in_=in_ap), reads=reads, writes=writes)

    def recip(self, out_ap, in_ap, reads, writes):
        self.op("dve", lambda e: e.reciprocal(out=out_ap, in_=in_ap), reads=reads, writes=writes)

    def memset(self, ap, val, writes, q="dve"):
        self.op(q, lambda e: e.memset(ap, val), writes=writes)

    def build(self):
        nc = self.nc
        N, NH, NST, NT = self.N, self.NH, self.NST, self.NT
        self.x_in = self.dram_in("x_slot", [N, D])
        self.mem_in = self.dram_in("mem2", [2, 256, D])
        self.w_in = {n: self.dram_in(n, WSHAPES[n]) for n in WNAMES}
        self.g_in = {n: self.dram_in(n, GNAMES[n]) for n in GNAMES}
        self.tabs = {}
        for n, r in (("tA", 128), ("tB", 128), ("tD", 96), ("tDk", 32)):
            self.tabs[n] = (self.dram_in(n + "_cos", [r, N]), self.dram_in(n + "_sin", [r, N]))
        self.sw_in = {n: self.dram_in(n, [r, r]) for n, r in (("swA", 128), ("swB", 128), ("swD", 96), ("swDk", 32))}
        self.maskA_in = self.dram_in("maskA", [128, 5, 128])
        self.xbias_in = self.dram_in("xbias", [128, 2])
        self.cst_in = self.dram_in("cst", [128, 3, 128])
        self.y_out = nc.dram_tensor("y_slot", [N, D], F32, kind="ExternalOutput").ap()
        self.w_bf = {n: self.dram_tmp(n + "_bf", WSHAPES[n], BF16) for n in WNAMES}
        self.xT = self.dram_tmp("xT", [KC, 128, N], F32)
        self.QT = self.dram_tmp("QT", [16, 128, N], BF16)
        self.KT = self.dram_tmp("KT", [16, 128, N], BF16)
        self.VT = self.dram_tmp("VT", [16, 128, N], BF16)
        self.KR = self.dram_tmp("KR", [32, N], BF16)
        self.AT = self.dram_tmp("AT", [16, 128, N], BF16)

        with contextlib.ExitStack() as gst:
            self.S = Sched(nc, gst)
            self.ps = [Tile(gst.enter_context(nc.psum_tensor("ps%d" % i, [128, 512], F32))) for i in range(7)]
            self.psb = Tile(gst.enter_context(nc.psum_tensor("psb", [128, 1024], BF16)))
            for t in self.ps + [self.psb]:
                t.b[0].excl = True
            c32 = self.sb(gst, "c32", [128, 3, 128], F32)
            self.load(c32.t[:], self.cst_in, [], c32.b)
            self.idf = c32.t[:, 0, :]
            self.ones32 = c32.t[:, 1, :]
            cbf = self.sb(gst, "cbf", [128, 3, 128], BF16)
            self.cp(cbf.t[:], c32.t[:], c32.b, cbf.b)
            self.c32, self.cbf = c32, cbf
            self.idb = cbf.t[:, 0, :]
            self.onesb = cbf.t[:, 1, :]
            self.blkb = cbf.t[:, 2, :]
            self.sw = {}
            for n, r in (("swA", 128), ("swB", 128), ("swD", 96), ("swDk", 32)):
                t32 = self.sb(gst, n + "32", [r, r], F32)
                tb = self.sb(gst, n + "b", [r, r], BF16)
                self.load(t32.t[:], self.sw_in[n], [], t32.b)
                self.cp(tb.t[:], t32.t[:], t32.b, tb.b)
                self.sw[n] = tb
            self.gn = {}
            for n in ("norm_mix", "norm_x", "norm_mem", "norm_mlp"):
                t = self.sb(gst, "g_" + n, [128, 4, KC], F32)
                for l in range(4):
                    self.load(t.t[:, l, :], self.g_in[n][l].rearrange("(kc p) -> p kc", p=128), [], t.b, slow=True)
                self.gn[n] = t
            t = self.sb(gst, "g_final", [128, KC], F32)
            self.load(t.t[:], self.g_in["final_norm"].rearrange("(kc p) -> p kc", p=128), [], t.b, slow=True)
            self.gn["final_norm"] = t
            t = self.sb(gst, "g_dq", [128, 3], F32)
            self.load(t.t[:], self.g_in["d_q_norm"][0].rearrange("(kc p) -> p kc", p=128), [], t.b, slow=True)
            self.gn["d_q_norm"] = t
            t = self.sb(gst, "g_dkv", [128, 2], F32)
            self.load(t.t[:], self.g_in["d_kv_norm"][0].rearrange("(kc p) -> p kc", p=128), [], t.b, slow=True)
            self.gn["d_kv_norm"] = t
            t = self.sb(gst, "g_bqk", [128, 2], F32)
            for j, n in enumerate(("b_q_norm", "b_k_norm")):
                for hh in range(2):
                    self.load(t.t[hh * 64:(hh + 1) * 64, j:j + 1], self.g_in[n][0].rearrange("(p o) -> p o", o=1),
                              [], t.b, slow=True)
            self.gn["b_qk"] = t
            t = self.sb(gst, "g_csub", [128, 1], F32)
            self.load(t.t[:], self.g_in["c_sub_norm"][0].rearrange("(p o) -> p o", o=1), [], t.b, slow=True)
            self.gn["c_sub"] = t
            mA32 = self.sb(gst, "mA32", [128, 5, 128], F32)
            self.load(mA32.t[:], self.maskA_in, [], mA32.b)
            self.maskA = self.sb(gst, "maskA", [128, 5, 128], BF16)
            self.cp(self.maskA.t[:], mA32.t[:], mA32.b, self.maskA.b)
            self.xbias = self.sb(gst, "xbias", [128, 2], F32)
            self.load(self.xbias.t[:], self.xbias_in, [], self.xbias.b)
            self.lam_init = 0.8 - 0.6 * math.exp(-0.3 * 2)
            lv = self.sb(gst, "lamv", [128, 4, 64], F32)
            for j, n in enumerate(("c_lambda_q1", "c_lambda_k1", "c_lambda_q2", "c_lambda_k2")):
                self.load(lv.t[:, j, :], self.g_in[n][0].partition_broadcast(128), [], lv.b)
            lp = self.sb(gst, "lamp", [128, 2, 64], F32)
            ls = self.sb(gst, "lams", [128, 4], F32)
            self.tt(lp.t[:, 0, :], lv.t[:, 0, :], lv.t[:, 1, :], ALU.mult, lv.b, lp.b)
            self.tt(lp.t[:, 1, :], lv.t[:, 2, :], lv.t[:, 3, :], ALU.mult, lv.b, lp.b)
            self.op("dve", lambda e: e.tensor_reduce(out=ls.t[:, 0:2], in_=lp.t[:], axis=mybir.AxisListType.X, op=ALU.add),
                    reads=lp.b, writes=ls.b)
            self.act(ls.t[:, 2:4], ls.t[:, 0:2], AF.Exp, ls.b, ls.b)
            self.neglam = self.sb(gst, "neglam", [128, 1], F32)
            self.stt(self.neglam.t[:], ls.t[:, 3:4], -self.lam_init, ls.t[:, 2:3], ALU.add, ALU.subtract, ls.b, self.neglam.b)
            self.csubg = self.sb(gst, "csubg", [128, 1], F32)
            self.act(self.csubg.t[:], self.gn["c_sub"].t[:], AF.Copy, self.gn["c_sub"].b, self.csubg.b,
                     scale=1.0 - self.lam_init)
            self.wB = {n: Buf() for n in WNAMES}
            for n in WNAMES:
                L, R, C = WSHAPES[n]
                rows = max(128, (min(R, (1 << 20) // C) // 16) * 16)
                for l in range(L):
                    for r0 in range(0, R, rows):
                        r1 = min(R, r0 + rows)
                        self.load(self.w_bf[n][l, r0:r1, :], self.w_in[n][l, r0:r1, :], [], [self.wB[n]], q="pool")
            self.S.emit_phase()

            import os
            maxph = int(os.environ.get("MK_MAXPH", "999"))
            steps = [self.phase_p0]
            for l in range(self.depth):
                m = l % 4
                steps.append(lambda l=l, m=m: self.phase_p1(l, m))
                steps.append((lambda: self.phase_attn_A()) if m == 0 else (lambda m=m: self.phase_attn(m)))
                steps.append(lambda l=l, m=m: self.phase_p2b(l, m))
            if self.depth < DEPTH:
                steps.append(self.phase_final_only)
            for i, f in enumerate(steps):
                if i >= maxph:
                    break
                f()

    def phase_p0(self):
        with contextlib.ExitStack() as st:
            xin = [self.sb(st, "p0_xin%d" % i, [128, 4, D], F32) for i in range(2)]
            xo = [self.sb(st, "p0_xo%d" % i, [128, KC, ST], F32, nb=KC) for i in range(2)]
            for s in range(self.NST):
                a, o = xin[s % 2], xo[s % 2]
                self.load(a.t[:], self.x_in[s * ST:(s + 1) * ST, :].rearrange("(tt p) d -> p tt d", p=128), [], a.b)
                for kc in range(KC):
                    bank = self.ps[kc % 4]
                    for tt_ in range(4):
                        self.tr(bank.t[:, tt_ * 128:(tt_ + 1) * 128], a.t[:, tt_, kc * 128:(kc + 1) * 128], self.idf,
                                a.b + self.c32.b, bank.b)
                    self.cp(o.t[:, kc, :], bank.t[:], bank.b, [o.b[kc]], q=("act" if kc % 2 else "dve"))
                self.load(self.xT[:, :, s * ST:(s + 1) * ST].rearrange("kc p t -> p kc t"), o.t[:], o.b, [], q="pool")
            self.S.emit_phase()

    def rms(self, src, nkc, nfeat, gain_ap_fn, out, sq, rstd, ssbank, ones_lhsT=None, ones_reads=None, cols=ST):
        if ones_lhsT is None:
            ones_lhsT, ones_reads = self.onesb, self.cbf.b
        for kc in range(nkc):
            self.act(sq.t[:, kc, :cols], src.t[:, kc, :cols], AF.Square, src.b if len(src.b) == 1 else [src.b[kc]],
                     sq.b if len(sq.b) == 1 else [sq.b[kc]])
        for kc in range(nkc):
            self.mm(ssbank.t[:, :cols], ones_lhsT, sq.t[:, kc, :cols], kc == 0, kc == nkc - 1,
                    (sq.b if len(sq.b) == 1 else [sq.b[kc]]) + ones_reads, ssbank.b)
        self.act(rstd.t[:, :cols], ssbank.t[:, :cols], AF.Sqrt, ssbank.b + self.epsb.b, rstd.b, scale=1.0 / nfeat, bias=self.epsb.t[:])
        self.recip(rstd.t[:, :cols], rstd.t[:, :cols], rstd.b, rstd.b)
        for kc in range(nkc):
            self.stt(out.t[:, kc, :cols], src.t[:, kc, :cols], gain_ap_fn(kc), rstd.t[:, :cols], ALU.mult, ALU.mult,
                     (src.b if len(src.b) == 1 else [src.b[kc]]) + rstd.b + self.gall,
                     out.b if len(out.b) == 1 else [out.b[kc]])

    def layer(self, l):
        m = l % 4
        self.phase_p1(l, m)
        if m == 0:
            self.phase_attn_A()
        else:
            self.phase_attn(m)
        self.phase_p2b(l, m)

    def common_tiles(self, st):
        self.epsb = self.sb(st, "epsb", [128, 1], F32)
        self.memset(self.epsb.t[:], EPS, self.epsb.b)
        self.gall = []
        for t in self.gn.values():
            self.gall += t.b

    def rope(self, st_tiles, psrc, rows, swname, cosT, sinT, s_idx, out_ap, out_b, tag):
        import os
        if os.environ.get("MK_NOROPE"):
            self.cp(out_ap, psrc.t[:rows, :], psrc.b, out_b, q="act")
            return
        xb, t1, t2, swbank = st_tiles
        lvl = int(os.environ.get("MK_ROPE", "9"))
        sw = self.sw[swname]
        self.cp(xb.t[:rows, :], psrc.t[:rows, :], psrc.b, xb.b, q="act")
        self.mm(swbank.t[:rows, :], sw.t[:], xb.t[:rows, :], True, True, xb.b + sw.b, swbank.b)
        if lvl == 1:
            self.cp(out_ap, swbank.t[:rows, :], swbank.b + psrc.b, out_b, q="act")
            return
        self.tt(t1.t[:rows, :], psrc.t[:rows, :], cosT.t[:rows, :], ALU.mult, psrc.b + cosT.b, t1.b)
        self.tt(t2.t[:rows, :], swbank.t[:rows, :], sinT.t[:rows, :], ALU.mult, swbank.b + sinT.b, t2.b)
        if lvl == 2:
            self.cp(out_ap, t2.t[:rows, :], t1.b + t2.b, out_b, q="act")
            return
        if lvl == 3:
            self.tt(out_ap, t1.t[:rows, :], t2.t[:rows, :], ALU.add, t1.b + t2.b, out_b, q="dve")
            return
        self.tt(out_ap, t1.t[:rows, :], t2.t[:rows, :], ALU.add, t1.b + t2.b, out_b, q="pool")

    def phase_p1(self, l, m):
        N, NST = self.N, self.NST
        wname = ("a_w_in", "b_w_in", "c_w_in", "d_w_in")[m]
        C = WSHAPES[wname][2]
        with contextlib.ExitStack() as st:
            self.common_tiles(st)
            w = self.sb(st, "p1_w", [128, KC, C], BF16)
            self.load(w.t[:], self.w_bf[wname][0].rearrange("(kc p) n -> p kc n", p=128), [self.wB[wname]], w.b)
            if m == 3:
                wuq = self.sb(st, "p1_wuq", [128, 3, 1536], BF16)
                self.load(wuq.t[:], self.w_bf["d_w_uq"][0].rearrange("(kc p) n -> p kc n", p=128), [self.wB["d_w_uq"]], wuq.b)
                wukv = self.sb(st, "p1_wukv", [128, 2, 2048], BF16)
                self.load(wukv.t[:], self.w_bf["d_w_ukv"][0].rearrange("(kc p) n -> p kc n", p=128), [self.wB["d_w_ukv"]], wukv.b)
            xs = [self.sb(st, "p1_x%d" % i, [128, KC, ST], F32) for i in range(2)]
            hT = self.sb(st, "p1_h", [128, KC, ST], BF16)
            sq = self.sb(st, "p1_sq", [128, KC, ST], BF16)
            rstd = self.sb(st, "p1_rstd", [128, ST], F32)
            tabn = ("tA", "tB", "tA", "tD")[m]
            trows = 96 if m == 3 else 128
            cosT = [self.sb(st, "p1_cos%d" % i, [trows, ST], F32) for i in range(2)]
            sinT = [self.sb(st, "p1_sin%d" % i, [trows, ST], F32) for i in range(2)]
            if m == 3:
                cosK = [self.sb(st, "p1_cosk%d" % i, [32, ST], F32) for i in range(2)]
                sinK = [self.sb(st, "p1_sink%d" % i, [32, ST], F32) for i in range(2)]
                cq = self.sb(st, "p1_cq", [128, 5, ST], F32)
                cqn = self.sb(st, "p1_cqn", [128, 5, ST], BF16)
                sq2 = self.sb(st, "p1_sq2", [128, 3, ST], BF16)
                rstd2 = self.sb(st, "p1_rstd2", [128, ST], F32)
            xb = self.sb(st, "p1_xb", [128, ST], BF16)
            t1 = self.sb(st, "p1_t1", [128, ST], F32)
            t2 = self.sb(st, "p1_t2", [128, ST], F32)
            qn = self.sb(st, "p1_qn", [128, ST], F32)
            qsq = self.sb(st, "p1_qsq", [128, 1, ST], BF16)
            NO = 4
            outs = [self.sb(st, "p1_o%d" % i, [128, ST], BF16) for i in range(NO)]
            self.p1_oi = 0
            swbank = self.ps[5]
            ssbank = self.ps[4]
            rope_tiles = (xb, t1, t2, swbank)

            def ld(s):
                x = xs[s % 2]
                self.load(x.t[:], self.xT[:, :, s * ST:(s + 1) * ST].rearrange("kc p t -> p kc t"), [], x.b)
                tc_, ts_ = self.tabs[tabn]
                self.load(cosT[s % 2].t[:], tc_[:, s * ST:(s + 1) * ST], [], cosT[s % 2].b)
                self.load(sinT[s % 2].t[:], ts_[:, s * ST:(s + 1) * ST], [], sinT[s % 2].b)
                if m == 3:
                    tc_, ts_ = self.tabs["tDk"]
                    self.load(cosK[s % 2].t[:], tc_[:, s * ST:(s + 1) * ST], [], cosK[s % 2].b)
                    self.load(sinK[s % 2].t[:], ts_[:, s * ST:(s + 1) * ST], [], sinK[s % 2].b)

            def nxt_out():
                o = outs[self.p1_oi % NO]
                self.p1_oi += 1
                return o

            def proj(bank, wt, nk, c0, cn, src):
                for kc in range(nk):
                    self.mm(bank.t[:cn, :], wt.t[:, kc, c0:c0 + cn], src.t[:, kc, :], kc == 0, kc == nk - 1,
                            wt.b + src.b, bank.b)

            def store(dst_ap, o, rows, p0=0):
                self.load(dst_ap, o.t[p0:p0 + rows, :], o.b, [], q="pool")

            ld(0)
            for s in range(NST):
                if s + 1 < NST:
                    ld(s + 1)
                x = xs[s % 2]
                cs, sn = cosT[s % 2], sinT[s % 2]
                sl = slice(s * ST, (s + 1) * ST)
                gm = self.gn["norm_mix"]
                self.rms(x, KC, D, lambda kc: gm.t[:, l, kc:kc + 1], hT, sq, rstd, ssbank)
                nb = 0
                if m == 0:
                    for c in range(36):
                        g, typ, hp = c // 12, (c % 12) // 4, c % 4
                        bank = self.ps[nb % 4]; nb += 1
                        proj(bank, w, KC, c * 128, 128, hT)
                        o = nxt_out()
                        if typ < 2:
                            self.rope(rope_tiles, bank, 128, "swA", cs, sn, s, o.t[:], o.b, "a")
                        else:
                            self.cp(o.t[:], bank.t[:], bank.b, o.b, q="act")
                        dst = (self.QT, self.KT, self.VT)[typ]
                        store(dst[g * 4 + hp, :, sl], o, 128)
                elif m == 1:
                    gq = self.gn["b_qk"]
                    for c in range(12):
                        bank = self.ps[nb % 4]; nb += 1
                        proj(bank, w, KC, c * 128, 128, hT)
                        o = nxt_out()
                        if c < 10:
                            self.cp(qn.t[:], bank.t[:], bank.b, qn.b, q="act")
                            qt = Tile(qn.t[:].rearrange("p (o t) -> p o t", o=1))
                            qt.b = qn.b
                            gcol = 0 if c < 8 else 1
                            nbk = self.ps[6]
                            self.rms(qt, 1, 64, lambda kc: gq.t[:, gcol:gcol + 1], qt, qsq, rstd, ssbank,
                                     ones_lhsT=self.blkb, ones_reads=self.cbf.b)
                            self.rope(rope_tiles, qn, 128, "swB", cs, sn, s, o.t[:], o.b, "b")
                            dst = self.QT[c] if c < 8 else self.KT[c - 8]
                        else:
                            self.cp(o.t[:], bank.t[:], bank.b, o.b, q="act")
                            dst = self.VT[c - 10]
                        store(dst[:, sl], o, 128)
                elif m == 2:
                    for c in range(24):
                        typ, hh = c // 8, c % 8
                        bank = self.ps[nb % 4]; nb += 1
                        proj(bank, w, KC, c * 128, 128, hT)
                        o = nxt_out()
                        if typ < 2:
                            self.rope(rope_tiles, bank, 128, "swA", cs, sn, s, o.t[:], o.b, "c")
                        else:
                            self.cp(o.t[:], bank.t[:], bank.b, o.b, q="act")
                        dst = (self.QT, self.KT, self.VT)[typ]
                        store(dst[hh, :, sl], o, 128)
                else:
                    for c in range(5):
                        bank = self.ps[nb % 4]; nb += 1
                        proj(bank, w, KC, c * 128, 128, hT)
                        self.cp(cq.t[:, c, :], bank.t[:], bank.b, cq.b, q="act")
                    bank = self.ps[nb % 4]; nb += 1
                    proj(bank, w, KC, 640, 32, hT)
                    o = nxt_out()
                    self.rope(rope_tiles, bank, 32, "swDk", cosK[s % 2], sinK[s % 2], s, o.t[:32, :], o.b, "dk")
                    store(self.KR[:, sl], o, 32)
                    gq_, gkv_ = self.gn["d_q_norm"], self.gn["d_kv_norm"]
                    cq_q = Tile(cq.t[:, 0:3, :]); cq_q.b = cq.b
                    cqn_q = Tile(cqn.t[:, 0:3, :]); cqn_q.b = cqn.b
                    self.rms(cq_q, 3, 384, lambda kc: gq_.t[:, kc:kc + 1], cqn_q, sq2, rstd2, ssbank)
                    cq_k = Tile(cq.t[:, 3:5, :]); cq_k.b = cq.b
                    cqn_k = Tile(cqn.t[:, 3:5, :]); cqn_k.b = cqn.b
                    self.rms(cq_k, 2, 256, lambda kc: gkv_.t[:, kc:kc + 1], cqn_k, sq2, rstd2, ssbank)
                    for hh in range(16):
                        bank = self.ps[nb % 4]; nb += 1
                        proj(bank, wuq, 3, hh * 96, 96, cqn_q)
                        o = nxt_out()
                        self.rope(rope_tiles, bank, 96, "swD", cs, sn, s, o.t[:96, :], o.b, "d")
                        store(self.QT[hh, 0:96, sl], o, 96)
                    for hh in range(16):
                        bank = self.ps[nb % 4]; nb += 1
                        proj(bank, wukv, 2, hh * 128, 128, cqn_k)
                        o = nxt_out()
                        self.cp(o.t[:], bank.t[:], bank.b, o.b, q="act")
                        store(self.KT[hh, 0:64, sl], o, 64, 0)
                        store(self.VT[hh, 0:64, sl], o, 64, 64)
            self.S.emit_phase()

    def phase_attn(self, m):
        N, NST, NT, NH = self.N, self.NST, self.NT, self.NH
        diff = (m == 2)
        if m == 1:
            nstream, dqk, dv, scale = 16, 64, 64, 64 ** -0.5
        elif m == 2:
            nstream, dqk, dv, scale = 8, 64, 128, 64 ** -0.5
        else:
            nstream, dqk, dv, scale = 16, 96, 64, 96 ** -0.5
        with contextlib.ExitStack() as st:
            self.common_tiles(st)
            ncomp = 2 if diff else 1
            kts = [[self.sb(st, "at_k%d_%d" % (i, c), [dqk, N], BF16) for c in range(ncomp)] for i in range(2)]
            vts = [self.sb(st, "at_v%d" % i, [dv, N], BF16) for i in range(2)]
            dva = dv if diff else dv + 1
            vaug = [self.sb(st, "at_va%d" % i, [128, NT, dva], BF16) for i in range(2)]
            if not diff:
                for i in range(2):
                    self.cp(vaug[i].t[:, :, dv:dv + 1], self.onesb[:, 0:NT].rearrange("p (t o) -> p t o", o=1),
                            self.cbf.b, vaug[i].b)
            NQ = 3
            qs = [[self.sb(st, "at_q%d_%d" % (i, c), [dqk, ST], BF16) for c in range(ncomp)] for i in range(NQ)]
            NP = 4
            pts = [self.sb(st, "at_p%d" % i, [128, ST], BF16) for i in range(NP)]
            rec = self.sb(st, "at_rec", [128, ST], F32)
            rec2 = self.sb(st, "at_rec2", [128, ST], F32)
            bcs = self.sb(st, "at_bcs", [128, ST], F32)
            osb = [self.sb(st, "at_o%d" % i, [128, ST], BF16) for i in range(2)]
            if diff:
                o32 = self.sb(st, "at_o32", [128, 1, ST], F32)
                t32 = self.sb(st, "at_t32", [128, ST], F32)
                osq = self.sb(st, "at_osq", [128, 1, ST], BF16)
                rstd = self.sb(st, "at_rstd", [128, ST], F32)
            sbanks = [self.ps[0], self.ps[1], self.ps[2]]
            pi = [0]
            si = [0]
            units = [(h, s) for h in range(nstream) for s in range(NST)]

            def ld_head(h):
                i = h % 2
                for c in range(ncomp):
                    kt = kts[i][c]
                    if m == 1:
                        src = self.KT[h // 8, ((h // 4) % 2) * 64:((h // 4) % 2) * 64 + 64, :]
                        self.load(kt.t[:], src, [], kt.b)
                    elif m == 2:
                        self.load(kt.t[:], self.KT[h, c * 64:(c + 1) * 64, :], [], kt.b)
                    else:
                        self.load(kt.t[0:64, :], self.KT[h, 0:64, :], [], kt.b)
                        self.load(kt.t[64:96, :], self.KR[:, :], [], kt.b)
                vt = vts[i]
                if m == 1:
                    self.load(vt.t[:], self.VT[h // 8, ((h // 4) % 2) * 64:((h // 4) % 2) * 64 + 64, :], [], vt.b)
                elif m == 2:
                    self.load(vt.t[:], self.VT[h, :, :], [], vt.b)
                else:
                    self.load(vt.t[:], self.VT[h, 0:64, :], [], vt.b)

            def ld_q(u):
                h, s = units[u]
                sl = slice(s * ST, (s + 1) * ST)
                for c in range(ncomp):
                    q = qs[u % NQ][c]
                    if m == 1:
                        self.load(q.t[:], self.QT[h // 2, (h % 2) * 64:(h % 2) * 64 + 64, sl], [], q.b)
                    elif m == 2:
                        self.load(q.t[:], self.QT[h, c * 64:(c + 1) * 64, sl], [], q.b)
                    else:
                        self.load(q.t[:], self.QT[h, 0:96, sl], [], q.b)

            def prep_v(h):
                i = h % 2
                vt, va = vts[i], vaug[i]
                per = 1024 // dv
                for t0 in range(0, NT, per):
                    for j in range(per):
                        self.tr(self.psb.t[:, j * dv:(j + 1) * dv], vt.t[:, (t0 + j) * 128:(t0 + j + 1) * 128],
                                self.idb[0:dv, 0:dv], vt.b + self.cbf.b, self.psb.b)
                    self.cp(va.t[:, t0:t0 + per, 0:dv], self.psb.t[:, 0:per * dv].rearrange("p (t d) -> p t d", d=dv),
                            self.psb.b, va.b, q="dve")

            ld_head(0)
            ld_q(0)
            ld_q(1)
            for u, (h, s) in enumerate(units):
                if s == 0:
                    prep_v(h)
                    if h + 1 < nstream:
                        ld_head(h + 1)
                if u + 2 < len(units):
                    ld_q(u + 2)
                i = h % 2
                va = vaug[i]
                qhalf = (s * ST) // NH
                if not diff:
                    ob = self.ps[3 + (u % 2)]
                    kt = kts[i][0]
                    q = qs[u % NQ][0]
                    for t in range(NT):
                        sb_ = sbanks[si[0] % 3]; si[0] += 1
                        p = pts[pi[0] % NP]; pi[0] += 1
                        self.mm(sb_.t[:], kt.t[:, t * 128:(t + 1) * 128], q.t[:], True, True, kt.b + q.b, sb_.b)
                        bcol = 0 if (t * 128) // NH == qhalf else 1
                        self.act(p.t[:], sb_.t[:], AF.Exp, sb_.b + self.xbias.b, p.b, scale=scale,
                                 bias=self.xbias.t[:, bcol:bcol + 1])
                        self.mm(ob.t[0:dv + 1, :], va.t[:, t, :], p.t[:], t == 0, t == NT - 1, va.b + p.b, ob.b)
                    self.recip(rec.t[64:65, :], ob.t[64:65, :], ob.b, rec.b)
                    bb = self.ps[5]
                    self.mm(bb.t[0:64, :], self.ones32[64:65, 0:64], rec.t[64:65, :], True, True, rec.b + self.c32.b, bb.b)
                    self.cp(bcs.t[0:64, :], bb.t[0:64, :], bb.b, bcs.b, q="act")
                    o = osb[u % 2]
                    self.tt(o.t[0:64, :], ob.t[0:64, :], bcs.t[0:64, :], ALU.mult, ob.b + bcs.b, o.b)
                    self.load(self.AT[h // 2, (h % 2) * 64:(h % 2) * 64 + 64, s * ST:(s + 1) * ST], o.t[0:64, :], o.b, [], q="pool")
                else:
                    obs = [self.ps[3], self.ps[4]]
                    dbs = [self.ps[5], self.ps[6]]
                    for t in range(NT):
                        bcol = 0 if (t * 128) // NH == qhalf else 1
                        for c in range(2):
                            sb_ = sbanks[si[0] % 3]; si[0] += 1
                            p = pts[pi[0] % NP]; pi[0] += 1
                            kt = kts[i][c]
                            q = qs[u % NQ][c]
                            self.mm(sb_.t[:], kt.t[:, t * 128:(t + 1) * 128], q.t[:], True, True, kt.b + q.b, sb_.b)
                            self.act(p.t[:], sb_.t[:], AF.Exp, sb_.b + self.xbias.b, p.b, scale=scale,
                                     bias=self.xbias.t[:, bcol:bcol + 1])
                            self.mm(obs[c].t[:], va.t[:, t, :], p.t[:], t == 0, t == NT - 1, va.b + p.b, obs[c].b)
                            self.mm(dbs[c].t[:], self.onesb, p.t[:], t == 0, t == NT - 1, self.cbf.b + p.b, dbs[c].b)
                    self.recip(rec.t[:], dbs[0].t[:], dbs[0].b, rec.b)
                    self.recip(rec2.t[:], dbs[1].t[:], dbs[1].b, rec2.b)
                    self.tt(o32.t[:, 0, :], obs[0].t[:], rec.t[:], ALU.mult, obs[0].b + rec.b, o32.b)
                    self.tt(t32.t[:], obs[1].t[:], rec2.t[:], ALU.mult, obs[1].b + rec2.b, t32.b)
                    self.stt(o32.t[:, 0, :], t32.t[:], self.neglam.t[:], o32.t[:, 0, :], ALU.mult, ALU.add,
                             t32.b + o32.b + self.neglam.b, o32.b)
                    o = osb[u % 2]
                    ot = Tile(o.t[:].rearrange("p (o t) -> p o t", o=1)); ot.b = o.b
                    self.rms(o32, 1, 128, lambda kc: self.csubg.t[:], ot, osq, rstd, dbs[0])
                    self.load(self.AT[h, :, s * ST:(s + 1) * ST], o.t[:], o.b, [], q="pool")
            self.S.emit_phase()

    def phase_attn_A(self):
        N, NT, NH, NST = self.N, self.NT, self.NH, self.NST
        scale = 64 ** -0.5
        with contextlib.ExitStack() as st:
            self.common_tiles(st)
            qkv = [[self.sb(st, "aa_%s%d" % (n, i), [64, N], BF16) for n in "qkv"] for i in range(2)]
            vaug = [self.sb(st, "aa_va%d" % i, [128, NT, 65], BF16) for i in range(2)]
            for i in range(2):
                self.cp(vaug[i].t[:, :, 64:65], self.onesb[:, 0:NT].rearrange("p (t o) -> p t o", o=1), self.cbf.b, vaug[i].b)
            acc = self.sb(st, "aa_acc", [65, N], F32)
            NP = 3
            es = [self.sb(st, "aa_e%d" % i, [128, 3, 128], BF16) for i in range(NP)]
            pp = [self.sb(st, "aa_p%d" % i, [128, 3, 128], BF16) for i in range(NP)]
            rec = self.sb(st, "aa_rec", [65, ST], F32)
            osb = [self.sb(st, "aa_o%d" % i, [64, ST], BF16) for i in range(2)]
            accsync = Buf()
            combos = [(h, g) for h in range(8) for g in range(3)]

            def ld(ci):
                h, g = combos[ci]
                for j, src in enumerate((self.QT, self.KT, self.VT)):
                    t = qkv[ci % 2][j]
                    self.load(t.t[:], src[g * 4 + h // 2, (h % 2) * 64:(h % 2) * 64 + 64, :], [], t.b)

            def toks(d, s, r, i):
                start = s * NH + (128 * i) * d + r
                return slice(start, start + 127 * d + 1, d)

            ld(0)
            ui = 0
            for ci, (h, g) in enumerate(combos):
                if ci + 1 < len(combos):
                    ld(ci + 1)
                d = A_DILS[g]
                q, k, v = qkv[ci % 2]
                va = vaug[ci % 2]
                npp = NH // d // 128
                tiles = [(s, r, i) for s in range(2) for r in range(d) for i in range(npp)]
                tidx = {t: j for j, t in enumerate(tiles)}
                for t0 in range(0, NT, 16):
                    for j in range(16):
                        s, r, i = tiles[t0 + j]
                        self.tr(self.psb.t[:, j * 64:(j + 1) * 64], v.t[:, toks(d, s, r, i)], self.idb[0:64, 0:64],
                                v.b + self.cbf.b, self.psb.b)
                    self.cp(va.t[:, t0:t0 + 16, 0:64], self.psb.t[:, :].rearrange("p (t d) -> p t d", d=64),
                            self.psb.b, va.b, q="dve")
                first_evac = True
                for (s, r, i) in tiles:
                    nb = []
                    if i > 0:
                        nb.append(((s, r, i - 1), 0))
                    elif s == 1:
                        nb.append(((0, r, npp - 1), 3))
                    nb.append(((s, r, i), 1))
                    if i < npp - 1:
                        nb.append(((s, r, i + 1), 2))
                    elif s == 0:
                        nb.append(((1, r, 0), 4))
                    sb_ = self.ps[ui % 3]
                    e = es[ui % NP]
                    p = pp[ui % NP]
                    ob = self.ps[3 + (ui % 2)]
                    ui += 1
                    nn = len(nb)
                    for j, (kt_, mi) in enumerate(nb):
                        self.mm(sb_.t[:, j * 128:(j + 1) * 128], k.t[:, toks(d, *kt_)], q.t[:, toks(d, s, r, i)],
                                True, True, k.b + q.b, sb_.b)
                    self.act(e.t[:, 0:nn, :], sb_.t[:, 0:nn * 128].rearrange("p (j t) -> p j t", t=128), AF.Exp,
                             sb_.b, e.b, scale=scale)
                    for j, (kt_, mi) in enumerate(nb):
                        self.tt(p.t[:, j, :], e.t[:, j, :], self.maskA.t[:, mi, :], ALU.mult, e.b + self.maskA.b, p.b,
                                q=("pool" if j == 1 else "dve"))
                    for j, (kt_, mi) in enumerate(nb):
                        self.mm(ob.t[0:65, 0:128], va.t[:, tidx[kt_], :], p.t[:, j, :], j == 0, j == nn - 1, va.b + p.b, ob.b)
                    asl = acc.t[:, toks(d, s, r, i)]
                    rd = [accsync] if first_evac else []
                    first_evac = False
                    if g == 0:
                        self.cp(asl, ob.t[0:65, 0:128], ob.b + rd, [], q="dve")
                    else:
                        self.tt(asl, asl, ob.t[0:65, 0:128], ALU.add, ob.b + rd, [])
                self.memset(rec.t[0:1, 0:1], 0.0, [accsync])
                if g == 2:
                    for ck in range(NST):
                        sl = slice(ck * ST, (ck + 1) * ST)
                        self.recip(rec.t[64:65, :], acc.t[64:65, sl], [accsync] + rec.b, rec.b)
                        bb = self.ps[5]
                        self.mm(bb.t[0:64, :], self.ones32[64:65, 0:64], rec.t[64:65, :], True, True, rec.b + self.c32.b, bb.b)
                        o = osb[ck % 2]
                        self.tt(o.t[:], acc.t[0:64, sl], bb.t[0:64, :], ALU.mult, bb.b, o.b)
                        self.load(self.AT[h // 2, (h % 2) * 64:(h % 2) * 64 + 64, sl], o.t[:], o.b, [], q="pool")
                    self.memset(rec.t[0:1, 0:1], 0.0, [accsync])
            self.S.emit_phase()

    def phase_p2b(self, l, m):
        N, NST, NH = self.N, self.NST, self.NH
        last = (l == DEPTH - 1)
        woname = ("a_w_out", "b_w_out", "c_w_out", "d_w_out")[m]
        dvp = 128
        nh = WSHAPES[woname][1] // dvp
        xscale = 128 ** -0.5
        with contextlib.ExitStack() as st:
            self.common_tiles(st)
            kmT = self.sb(st, "pb_kmT", [128, 2, 4, 256], BF16)
            vm = self.sb(st, "pb_vm", [128, 2, 2, 512], BF16)
            ssbank = self.ps[4]
            with contextlib.ExitStack() as st2:
                wkv = self.sb(st2, "pb_wkv", [128, KC, D], BF16)
                self.load(wkv.t[:], self.w_bf["w_xkv"][l].rearrange("(kc p) n -> p kc n", p=128), [self.wB["w_xkv"]], wkv.b)
                mtok = self.sb(st2, "pb_mtok", [128, 2, D], F32)
                mT = self.sb(st2, "pb_mT", [128, KC, 256], F32)
                mTn = self.sb(st2, "pb_mTn", [128, KC, 256], BF16)
                msq = self.sb(st2, "pb_msq", [128, KC, 256], BF16)
                mrs = self.sb(st2, "pb_mrs", [128, ST], F32)
                gmem = self.gn["norm_mem"]
                for hf in range(2):
                    self.load(mtok.t[:], self.mem_in[hf].rearrange("(t p) d -> p t d", p=128), [], mtok.b)
                    for kc in range(KC):
                        bank = self.ps[kc % 4]
                        for t in range(2):
                            self.tr(bank.t[:, t * 128:(t + 1) * 128], mtok.t[:, t, kc * 128:(kc + 1) * 128], self.idf,
                                    mtok.b + self.c32.b, bank.b)
                        self.cp(mT.t[:, kc, :], bank.t[:, 0:256], bank.b, mT.b, q=("act" if kc % 2 else "dve"))
                    self.rms(mT, KC, D, lambda kc: gmem.t[:, l, kc:kc + 1], mTn, msq, mrs, ssbank, cols=256)
                    for hd in range(4):
                        bank = self.ps[hd % 4]
                        for kc in range(KC):
                            self.mm(bank.t[:, 0:256], wkv.t[:, kc, hd * 128:(hd + 1) * 128], mTn.t[:, kc, :], kc == 0,
                                    kc == KC - 1, wkv.b + mTn.b, bank.b)
                        self.cp(kmT.t[:, hf, hd, :], bank.t[:, 0:256], bank.b, kmT.b, q="act")
                    for t in range(2):
                        bank = self.ps[t % 4]
                        for kc in range(KC):
                            self.mm(bank.t[:], mTn.t[:, kc, t * 128:(t + 1) * 128], wkv.t[:, kc, 512:1024], kc == 0,
                                    kc == KC - 1, wkv.b + mTn.b, bank.b)
                        self.cp(vm.t[:, hf, t, :], bank.t[:], bank.b, vm.b, q="dve")
                self.S.emit_phase()
            wo = self.sb(st, "pb_wo", [dvp, nh, D], BF16)
            self.load(wo.t[:], self.w_bf[woname][0].rearrange("(h p) n -> p h n", p=dvp), [self.wB[woname]], wo.b)
            wxq = self.sb(st, "pb_wxq", [128, KC, 512], BF16)
            self.load(wxq.t[:], self.w_bf["w_xq"][l].rearrange("(kc p) n -> p kc n", p=128), [self.wB["w_xq"]], wxq.b)
            wxo = self.sb(st, "pb_wxo", [128, 4, D], BF16)
            self.load(wxo.t[:], self.w_bf["w_xo"][l].rearrange("(kc p) n -> p kc n", p=128), [self.wB["w_xo"]], wxo.b)
            NW1, NW2 = 2, 2
            w1 = [self.sb(st, "pb_w1_%d" % i, [128, KC, 512], BF16) for i in range(NW1)]
            w2 = [self.sb(st, "pb_w2_%d" % i, [128, 16, 256], BF16) for i in range(NW2)]
            xs = [self.sb(st, "pb_x%d" % i, [128, KC, ST], F32, nb=KC) for i in range(2)]
            ats = [self.sb(st, "pb_at%d" % i, [dvp, nh, ST], BF16) for i in range(2)]
            hT = self.sb(st, "pb_h", [128, KC, ST], BF16)
            sq = self.sb(st, "pb_sq", [128, KC, ST], BF16)
            rstd = self.sb(st, "pb_rstd", [128, ST], F32)
            qx = self.sb(st, "pb_qx", [128, 4, ST], BF16)
            px = [self.sb(st, "pb_px%d" % i, [128, ST], BF16) for i in range(2)]
            recx = self.sb(st, "pb_recx", [128, ST], F32)
            ox = self.sb(st, "pb_ox", [128, 4, ST], BF16)
            aT = self.sb(st, "pb_a", [128, 16, ST], BF16, nb=16)
            rl = [self.sb(st, "pb_rl%d" % i, [128, ST], F32) for i in range(2)]
            if last:
                ytok = [self.sb(st, "pb_ytok%d" % i, [128, D], F32) for i in range(2)]

            def ld(s):
                x = xs[s % 2]
                self.load(x.t[:], self.xT[:, :, s * ST:(s + 1) * ST].rearrange("kc p t -> p kc t"), [], x.all)
                a = ats[s % 2]
                self.load(a.t[:], self.AT[0:nh, 0:dvp, s * ST:(s + 1) * ST].rearrange("h p t -> p h t"), [], a.b)

            w1i, w2i, nb = [0], [0], [0]

            def ld_w1(j):
                t = w1[w1i[0] % NW1]; w1i[0] += 1
                self.load(t.t[:], self.w_bf["w_mlp_in"][l][:, j * 512:(j + 1) * 512].rearrange("(kc p) n -> p kc n", p=128),
                          [self.wB["w_mlp_in"]], t.b)
                return t

            def ld_w2(j):
                hf2, c = j // 4, j % 4
                t = w2[w2i[0] % NW2]; w2i[0] += 1
                self.load(t.t[:], self.w_bf["w_mlp_out"][l][hf2 * 2048:(hf2 + 1) * 2048, c * 256:(c + 1) * 256]
                          .rearrange("(fc p) n -> p fc n", p=128), [self.wB["w_mlp_out"]], t.b)
                return t

            def bank_():
                b = self.ps[nb[0] % 4]; nb[0] += 1
                return b

            ld(0)
            for s in range(NST):
                if s + 1 < NST:
                    ld(s + 1)
                x, a = xs[s % 2], ats[s % 2]
                hf = (s * ST) // NH
                w1q = [ld_w1(0), ld_w1(1)]
                for oc in range(KC):
                    bank = bank_()
                    for hh in range(nh):
                        self.mm(bank.t[:], wo.t[:, hh, oc * 128:(oc + 1) * 128], a.t[:, hh, :], hh == 0, hh == nh - 1,
                                wo.b + a.b, bank.b)
                    self.tt(x.t[:, oc, :], x.t[:, oc, :], bank.t[:], ALU.add, [x.b[oc]] + bank.b, [x.b[oc]])
                gx = self.gn["norm_x"]
                self.rms(x, KC, D, lambda kc: gx.t[:, l, kc:kc + 1], hT, sq, rstd, ssbank)
                for hd in range(4):
                    bank = bank_()
                    for kc in range(KC):
                        self.mm(bank.t[:], wxq.t[:, kc, hd * 128:(hd + 1) * 128], hT.t[:, kc, :], kc == 0, kc == KC - 1,
                                wxq.b + hT.b, bank.b)
                    self.cp(qx.t[:, hd, :], bank.t[:], bank.b, qx.b, q="act")
                for hd in range(4):
                    ob, db = self.ps[5], self.ps[6]
                    for t in range(2):
                        bank = bank_()
                        p = px[t]
                        self.mm(bank.t[:], kmT.t[:, hf, hd, t * 128:(t + 1) * 128], qx.t[:, hd, :], True, True,
                                kmT.b + qx.b, bank.b)
                        self.act(p.t[:], bank.t[:], AF.Exp, bank.b, p.b, scale=xscale)
                        self.mm(ob.t[:], vm.t[:, hf, t, hd * 128:(hd + 1) * 128], p.t[:], t == 0, t == 1, vm.b + p.b, ob.b)
                        self.mm(db.t[:], self.onesb, p.t[:], t == 0, t == 1, self.cbf.b + p.b, db.b)
                    self.recip(recx.t[:], db.t[:], db.b, recx.b)
                    self.tt(ox.t[:, hd, :], ob.t[:], recx.t[:], ALU.mult, ob.b + recx.b, ox.b)
                for oc in range(KC):
                    bank = bank_()
                    for hd in range(4):
                        self.mm(bank.t[:], wxo.t[:, hd, oc * 128:(oc + 1) * 128], ox.t[:, hd, :], hd == 0, hd == 3,
                                wxo.b + ox.b, bank.b)
                    self.tt(x.t[:, oc, :], x.t[:, oc, :], bank.t[:], ALU.add, [x.b[oc]] + bank.b, [x.b[oc]])
                gm = self.gn["norm_mlp"]
                self.rms(x, KC, D, lambda kc: gm.t[:, l, kc:kc + 1], hT, sq, rstd, ssbank)
                for hf2 in range(2):
                    w2q = [ld_w2(hf2 * 4 + 0)]
                    for j in range(4):
                        wt = w1q.pop(0)
                        for f in range(4):
                            fc = j * 4 + f
                            bank = bank_()
                            for kc in range(KC):
                                self.mm(bank.t[:], wt.t[:, kc, f * 128:(f + 1) * 128], hT.t[:, kc, :], kc == 0, kc == KC - 1,
                                        wt.b + hT.b, bank.b)
                            r = rl[fc % 2]
                            self.act(r.t[:], bank.t[:], AF.Relu, bank.b, r.b)
                            self.tt(aT.t[:, fc, :], r.t[:], r.t[:], ALU.mult, r.b, [aT.b[fc]], q="pool")
                        jj = hf2 * 4 + j
                        if jj + 2 < 8:
                            w1q.append(ld_w1(jj + 2))
                        if j == 2:
                            w2q.append(ld_w2(hf2 * 4 + 1))
                    for j in range(4):
                        wt = w2q.pop(0)
                        for o2 in range(2):
                            oc = j * 2 + o2
                            bank = bank_()
                            for fc in range(16):
                                self.mm(bank.t[:], wt.t[:, fc, o2 * 128:(o2 + 1) * 128], aT.t[:, fc, :], fc == 0, fc == 15,
                                        wt.b + [aT.b[fc]], bank.b)
                            self.tt(x.t[:, oc, :], x.t[:, oc, :], bank.t[:], ALU.add, [x.b[oc]] + bank.b, [x.b[oc]])
                        if j + 2 < 4:
                            w2q.append(ld_w2(hf2 * 4 + j + 2))
                if not last:
                    self.load(self.xT[:, :, s * ST:(s + 1) * ST].rearrange("kc p t -> p kc t"), x.t[:], x.all, [], q="pool")
                else:
                    gf = self.gn["final_norm"]
                    self.rms(x, KC, D, lambda kc: gf.t[:, kc:kc + 1], x, sq, rstd, ssbank)
                    yT = x
                    for tt_ in range(4):
                        yt = ytok[tt_ % 2]
                        for half in range(2):
                            bank = bank_()
                            for k4 in range(4):
                                kc = half * 4 + k4
                                self.tr(bank.t[:, k4 * 128:(k4 + 1) * 128], yT.t[:, kc, tt_ * 128:(tt_ + 1) * 128], self.idf,
                                        [yT.b[kc]] + self.c32.b, bank.b)
                            self.cp(yt.t[:, half * 512:(half + 1) * 512], bank.t[:], bank.b, yt.b, q=("act" if half else "dve"))
                        r0 = s * ST + tt_ * 128
                        self.load(self.y_out[r0:r0 + 128, :], yt.t[:], yt.b, [], q="pool")
            self.S.emit_phase()

    def phase_final_only(self):
        NST = self.NST
        with contextlib.ExitStack() as st:
            self.common_tiles(st)
            xs = [self.sb(st, "pf_x%d" % i, [128, KC, ST], F32) for i in range(2)]
            yT = self.sb(st, "pf_yT", [128, KC, ST], F32)
            sq = self.sb(st, "pf_sq", [128, KC, ST], BF16)
            rstd = self.sb(st, "pf_rstd", [128, ST], F32)
            ytok = [self.sb(st, "pf_ytok%d" % i, [128, D], F32) for i in range(2)]
            nb = 0
            for s in range(NST):
                x = xs[s % 2]
                self.load(x.t[:], self.xT[:, :, s * ST:(s + 1) * ST].rearrange("kc p t -> p kc t"), [], x.b)
                gf = self.gn["final_norm"]
                self.rms(x, KC, D, lambda kc: gf.t[:, kc:kc + 1], yT, sq, rstd, self.ps[4])
                for tt_ in range(4):
                    yt = ytok[tt_ % 2]
                    for half in range(2):
                        bank = self.ps[nb % 4]; nb += 1
                        for k4 in range(4):
                            kc = half * 4 + k4
                            self.tr(bank.t[:, k4 * 128:(k4 + 1) * 128], yT.t[:, kc, tt_ * 128:(tt_ + 1) * 128], self.idf,
                                    yT.b + self.c32.b, bank.b)
                        self.cp(yt.t[:, half * 512:(half + 1) * 512], bank.t[:], bank.b, yt.b, q=("act" if half else "dve"))
                    r0 = s * ST + tt_ * 128
                    self.load(self.y_out[r0:r0 + 128, :], yt.t[:], yt.b, [], q="pool")
            self.S.emit_phase()


def _rope_tab(pos, theta, rot):
    half = rot // 2
    inv = np.exp(np.arange(half, dtype=np.float32) * np.float32(-2.0 * math.log(theta) / rot)).astype(np.float32)
    ang = pos.astype(np.float32)[None, :] * inv[:, None]
    return np.cos(ang).astype(np.float32), np.sin(ang).astype(np.float32)


def host_tables(NH, is_pair):
    N = 2 * NH
    t = np.arange(N)
    pos = (t % NH) if is_pair else t
    out = {}
    c, s = _rope_tab(pos, 500000.0, 16)
    cosA = np.ones((128, N), np.float32)
    sinA = np.zeros((128, N), np.float32)
    swA = np.zeros((128, 128), np.float32)
    for u in range(2):
        for i in range(8):
            cosA[u * 64 + i] = c[i]; cosA[u * 64 + 8 + i] = c[i]
            sinA[u * 64 + i] = -s[i]; sinA[u * 64 + 8 + i] = s[i]
            swA[u * 64 + 8 + i, u * 64 + i] = 1.0
            swA[u * 64 + i, u * 64 + 8 + i] = 1.0
    out.update(tA_cos=cosA, tA_sin=sinA, swA=swA)
    rows = pos // 64
    cols = pos % 64
    cr, sr = _rope_tab(rows, 10000.0, 32)
    cc, sc = _rope_tab(cols, 10000.0, 32)
    cosB = np.ones((128, N), np.float32)
    sinB = np.zeros((128, N), np.float32)
    swB = np.zeros((128, 128), np.float32)
    for u in range(2):
        for off, (c_, s_) in ((0, (cr, sr)), (32, (cc, sc))):
            for i in range(16):
                a, b = u * 64 + off + i, u * 64 + off + 16 + i
                cosB[a] = c_[i]; cosB[b] = c_[i]
                sinB[a] = -s_[i]; sinB[b] = s_[i]
                swB[b, a] = 1.0
                swB[a, b] = 1.0
    out.update(tB_cos=cosB, tB_sin=sinB, swB=swB)
    c, s = _rope_tab(pos, 500000.0, 32)
    cosD = np.ones((96, N), np.float32)
    sinD = np.zeros((96, N), np.float32)
    swD = np.zeros((96, 96), np.float32)
    cosK = np.ones((32, N), np.float32)
    sinK = np.zeros((32, N), np.float32)
    swK = np.zeros((32, 32), np.float32)
    for i in range(16):
        cosD[64 + i] = c[i]; cosD[80 + i] = c[i]
        sinD[64 + i] = -s[i]; sinD[80 + i] = s[i]
        swD[80 + i, 64 + i] = 1.0
        swD[64 + i, 80 + i] = 1.0
        cosK[i] = c[i]; cosK[16 + i] = c[i]
        sinK[i] = -s[i]; sinK[16 + i] = s[i]
        swK[16 + i, i] = 1.0
        swK[i, 16 + i] = 1.0
    out.update(tD_cos=cosD, tD_sin=sinD, swD=swD, tDk_cos=cosK, tDk_sin=sinK, swDk=swK)
    kk = np.arange(128)[:, None]
    qq = np.arange(128)[None, :]
    mprev = (kk >= qq + 64).astype(np.float32)
    mcur = (np.abs(qq - kk) <= 64).astype(np.float32)
    mnext = (kk <= qq - 64).astype(np.float32)
    x = 0.0 if is_pair else 1.0
    out["maskA"] = np.ascontiguousarray(np.stack([mprev, mcur, mnext, mprev * x, mnext * x], axis=1))
    xb = np.zeros((128, 2), np.float32)
    xb[:, 1] = -30000.0 if is_pair else 0.0
    out["xbias"] = xb
    cst = np.zeros((128, 3, 128), np.float32)
    cst[:, 0, :] = np.eye(128, dtype=np.float32)
    cst[:, 1, :] = 1.0
    cst[0:64, 2, 0:64] = 1.0
    cst[64:128, 2, 64:128] = 1.0
    out["cst"] = cst
    return out


_PROG_CACHE = {}


def run_slots(slots, weights, NH, depth=DEPTH):
    key = (NH, depth)
    if key not in _PROG_CACHE:
        _PROG_CACHE[key] = Prog(NH, depth)
    prog = _PROG_CACHE[key]
    tabs = {True: host_tables(NH, True), False: host_tables(NH, False)}
    in_maps = []
    for sl in slots:
        mp = {"x_slot": np.ascontiguousarray(sl["x"], dtype=np.float32),
              "mem2": np.ascontiguousarray(sl["mem"], dtype=np.float32)}
        for n in WNAMES:
            mp[n] = weights[n]
        for n in GNAMES:
            mp[n] = weights[n]
        mp.update(tabs[bool(sl["pair"])])
        in_maps.append(mp)
    import os
    ncr = int(os.environ.get("MK_NCORES", "8"))
    res = run_bass_kernel_spmd(prog.nc, in_maps[:ncr], core_ids=list(range(ncr)))
    out = [r["y_slot"] for r in res.results]
    while len(out) < 8:
        out.append(out[-1])
    return out


def kernel(**inputs):
    NH = 4096
    xp = np.asarray(inputs["x_prompt"], dtype=np.float32)
    xs = np.asarray(inputs["x_sample"], dtype=np.float32)
    mp = np.asarray(inputs["mem_prompt"], dtype=np.float32)
    ms = np.asarray(inputs["mem_sample"], dtype=np.float32)
    weights = {n: np.ascontiguousarray(np.asarray(inputs[n], dtype=np.float32)) for n in list(WNAMES) + list(GNAMES)}
    slots = []
    for b in range(2):
        slots.append({"x": xp[b], "mem": np.stack([mp[b], mp[b]]), "pair": False})
    for j in range(4):
        slots.append({"x": xs[2 * j:2 * j + 2].reshape(2 * NH, D), "mem": ms[2 * j:2 * j + 2], "pair": True})
    for j in range(2):
        slots.append(slots[2 + j])
    ys = run_slots(slots, weights, NH)
    y_prompt = np.stack([ys[0], ys[1]]).astype(np.float32)
    y_sample = np.concatenate([ys[2 + j].reshape(2, NH, D) for j in range(4)], axis=0).astype(np.float32)
    return (y_prompt, y_sample)
```

```python
import math
import contextlib
import numpy as np
import concourse.bass as bass
import concourse.mybir as mybir
from concourse.bass_utils import run_bass_kernel_spmd

F32 = mybir.dt.float32
BF16 = mybir.dt.bfloat16
AF = mybir.ActivationFunctionType
ALU = mybir.AluOpType

D = 1024
KC = 8
ST = 512
DEPTH = 4
EPS = 1e-6
DMA_RING = 8


class Buf:
    __slots__ = ("last_w", "readers", "excl")

    def __init__(self, excl=False):
        self.last_w = None
        self.readers = {}
        self.excl = excl


class Op:
    __slots__ = ("q", "fn", "kind", "deps", "signal", "sigval", "slot", "rnd", "phase")


class Sched:
    ENGS = ("pe", "act", "dve", "pool", "sp")

    def __init__(self, nc, st):
        self.nc = nc
        self.ops = {q: [] for q in self.ENGS}
        self.ndma = {q: 0 for q in self.ENGS}
        self.sigcnt = {q: 0 for q in self.ENGS}
        self.phase = 0
        self.csem = {q: st.enter_context(nc.semaphore("c_" + q)) for q in self.ENGS}
        self.dsem = {q: [st.enter_context(nc.semaphore("d_%s_%d" % (q, i))) for i in range(DMA_RING)]
                     for q in ("sp", "pool")}
        self.waited = {q: {} for q in self.ENGS}

    def op(self, q, fn, reads=(), writes=(), kind="c"):
        o = Op()
        o.q = q
        o.fn = fn
        o.kind = kind
        o.signal = False
        o.sigval = 0
        o.phase = self.phase
        deps = {}
        xr = [b for b in reads if b.excl]
        if xr:
            reads = [b for b in reads if not b.excl]
            writes = list(writes) + xr
        for b in reads:
            w = b.last_w
            if w is not None:
                deps[id(w)] = w
        for b in writes:
            w = b.last_w
            if w is not None:
                deps[id(w)] = w
            for r in b.readers.values():
                deps[id(r)] = r
        for b in reads:
            if kind == "c":
                b.readers[q] = o
            else:
                b.readers[("d", q, self.ndma[q] % (4 * DMA_RING))] = o
        for b in writes:
            b.last_w = o
            b.readers = {}
        dl = []
        for d in deps.values():
            if d is o or d.phase != self.phase:
                continue
            if d.kind == "c":
                if d.q == q and q == "pe" and kind == "c":
                    continue
                d.signal = True
            dl.append(d)
        o.deps = dl
        if kind == "d":
            i = self.ndma[q]
            self.ndma[q] = i + 1
            o.slot = i % DMA_RING
            o.rnd = i // DMA_RING
        self.ops[q].append(o)
        return o

    def emit_phase(self):
        nc = self.nc
        for q in self.ENGS:
            c = self.sigcnt[q]
            for o in self.ops[q]:
                if o.kind == "c" and o.signal:
                    c += 1
                    o.sigval = c
            self.sigcnt[q] = c
        with nc.Block() as block:
            handles = {"pe": block.tensor, "act": block.scalar, "dve": block.vector,
                       "pool": block.gpsimd, "sp": block.sync}
            for q in self.ENGS:
                ops = self.ops[q]
                if not ops:
                    continue

                def body(eng, q=q, ops=ops):
                    waited = self.waited[q]

                    def wait(sem, key, val):
                        if waited.get(key, 0) >= val:
                            return
                        waited[key] = val
                        eng.wait_ge(sem, val)

                    for o in ops:
                        for d in o.deps:
                            if d.kind == "c":
                                wait(self.csem[d.q], ("c", d.q), d.sigval)
                            else:
                                wait(self.dsem[d.q][d.slot], ("d", d.q, d.slot), 16 * (d.rnd + 1))
                        if o.kind == "d":
                            if o.rnd > 0:
                                wait(self.dsem[q][o.slot], ("d", q, o.slot), 16 * o.rnd)
                            ins = o.fn(eng)
                            ins.then_inc(self.dsem[q][o.slot], 16)
                        else:
                            ins = o.fn(eng)
                            if o.signal:
                                ins.then_inc(self.csem[q], 1)
                    n = self.ndma[q]
                    if q in self.dsem and n > 0:
                        for s in range(min(DMA_RING, n)):
                            cnt = (n - 1 - s) // DMA_RING + 1
                            wait(self.dsem[q][s], ("d", q, s), 16 * cnt)

                handles[q](body)
        self.ops = {q: [] for q in self.ENGS}
        self.phase += 1


class Tile:
    def __init__(self, t, nb=1):
        self.t = t
        self.b = [Buf() for _ in range(nb)]

    @property
    def all(self):
        return self.b


WNAMES = ["w_xq", "w_xkv", "w_xo", "w_mlp_in", "w_mlp_out", "a_w_in", "a_w_out", "b_w_in", "b_w_out",
          "c_w_in", "c_w_out", "d_w_in", "d_w_uq", "d_w_ukv", "d_w_out"]
WSHAPES = {"w_xq": [4, 1024, 512], "w_xkv": [4, 1024, 1024], "w_xo": [4, 512, 1024],
           "w_mlp_in": [4, 1024, 4096], "w_mlp_out": [4, 4096, 1024], "a_w_in": [1, 1024, 4608],
           "a_w_out": [1, 512, 1024], "b_w_in": [1, 1024, 1536], "b_w_out": [1, 1024, 1024],
           "c_w_in": [1, 1024, 3072], "c_w_out": [1, 1024, 1024], "d_w_in": [1, 1024, 672],
           "d_w_uq": [1, 384, 1536], "d_w_ukv": [1, 256, 2048], "d_w_out": [1, 1024, 1024]}
GNAMES = {"norm_mix": [4, 1024], "norm_x": [4, 1024], "norm_mem": [4, 1024], "norm_mlp": [4, 1024],
          "final_norm": [1024], "b_q_norm": [1, 64], "b_k_norm": [1, 64], "c_lambda_q1": [1, 64],
          "c_lambda_k1": [1, 64], "c_lambda_q2": [1, 64], "c_lambda_k2": [1, 64], "c_sub_norm": [1, 128],
          "d_q_norm": [1, 384], "d_kv_norm": [1, 256]}
A_DILS = (1, 4, 16)


class Prog:
    def __init__(self, NH, depth=DEPTH):
        self.NH = NH
        self.N = 2 * NH
        self.NST = self.N // ST
        self.NT = self.N // 128
        self.depth = depth
        self.nc = bass.Bass("TRN2", target_bir_lowering=False)
        self.build()

    def dram_in(self, name, shape, dt=F32):
        return self.nc.dram_tensor(name, list(shape), dt, kind="ExternalInput").ap()

    def dram_tmp(self, name, shape, dt):
        return self.nc.dram_tensor(name, list(shape), dt, kind="Internal").ap()

    def sb(self, st, name, shape, dt, nb=1):
        self._uid = getattr(self, "_uid", 0) + 1
        return Tile(st.enter_context(self.nc.sbuf_tensor("s%d_%s" % (self._uid, name), list(shape), dt)), nb)

    def op(self, *a, **k):
        return self.S.op(*a, **k)

    def load(self, out_ap, in_ap, reads, writes, q="sp", slow=False):
        if slow:
            self.op(q, lambda e: e.dma_start(out=out_ap, in_=in_ap, allow_slow_non_contiguous=True),
                    reads=reads, writes=writes, kind="d")
        else:
            self.op(q, lambda e: e.dma_start(out=out_ap, in_=in_ap), reads=reads, writes=writes, kind="d")

    def mm(self, out_ap, lhsT, rhs, start, stop, reads, writes):
        self.op("pe", lambda e: e.matmul(out_ap, lhsT=lhsT, rhs=rhs, start=start, stop=stop),
                reads=reads, writes=writes)

    def tr(self, out_ap, in_ap, ident, reads, writes):
        self.op("pe", lambda e: e.transpose(out=out_ap, in_=in_ap, identity=ident), reads=reads, writes=writes)

    def act(self, out_ap, in_ap, func, reads, writes, scale=None, bias=None, accum=None):
        kw = {}
        if scale is not None:
            kw["scale"] = scale
        if bias is not None:
            kw["bias"] = bias
        if accum is not None:
            kw["accum_out"] = accum
        self.op("act", lambda e: e.activation(out=out_ap, in_=in_ap, func=func, **kw), reads=reads, writes=writes)

    def tt(self, out_ap, in0, in1, op, reads, writes, q="dve"):
        self.op(q, lambda e: e.tensor_tensor(out=out_ap, in0=in0, in1=in1, op=op), reads=reads, writes=writes)

    def stt(self, out_ap, in0, scalar, in1, op0, op1, reads, writes, q="dve"):
        self.op(q, lambda e: e.scalar_tensor_tensor(out=out_ap, in0=in0, scalar=scalar, in1=in1, op0=op0, op1=op1),
                reads=reads, writes=writes)

    def ts(self, out_ap, in0, s1, s2, op0, op1, reads, writes, q="dve"):
        self.op(q, lambda e: e.tensor_scalar(out=out_ap, in0=in0, scalar1=s1, scalar2=s2, op0=op0, op1=op1),
                reads=reads, writes=writes)

    def cp(self, out_ap, in_ap, reads, writes, q="dve"):
        if q == "act":
            self.act(out_ap, in_ap, AF.Copy, reads, writes)
        else:
            self.op(q, lambda e: e.tensor_copy(out=out_ap, in_=in_ap), reads=reads, writes=writes)

    def recip(self, out_ap, in_ap, reads, writes):
        self.op("dve", lambda e: e.reciprocal(out=out_ap, in_=in_ap), reads=reads, writes=writes)

    def memset(self, ap, val, writes, q="dve"):
        self.op(q, lambda e: e.memset(ap, val), writes=writes)

    def build(self):
        nc = self.nc
        N, NH, NST, NT = self.N, self.NH, self.NST, self.NT
        self.x_in = self.dram_in("x_slot", [N, D])
        self.mem_in = self.dram_in("mem2", [2, 256, D])
        self.w_in = {n: self.dram_in(n, WSHAPES[n]) for n in WNAMES}
        self.g_in = {n: self.dram_in(n, GNAMES[n]) for n in GNAMES}
        self.tabs = {}
        for n, r in (("tA", 128), ("tB", 128), ("tD", 96), ("tDk", 32)):
            self.tabs[n] = (self.dram_in(n + "_cos", [r, N]), self.dram_in(n + "_sin", [r, N]))
        self.sw_in = {n: self.dram_in(n, [r, r]) for n, r in (("swA", 128), ("swB", 128), ("swD", 96), ("swDk", 32))}
        self.maskA_in = self.dram_in("maskA", [128, 5, 128])
        self.xbias_in = self.dram_in("xbias", [128, 2])
        self.cst_in = self.dram_in("cst", [128, 3, 128])
        self.y_out = nc.dram_tensor("y_slot", [N, D], F32, kind="ExternalOutput").ap()
        self.w_bf = {n: self.dram_tmp(n + "_bf", WSHAPES[n], BF16) for n in WNAMES}
        self.xT = self.dram_tmp("xT", [KC, 128, N], F32)
        self.QT = self.dram_tmp("QT", [16, 128, N], BF16)
        self.KT = self.dram_tmp("KT", [16, 128, N], BF16)
        self.VT = self.dram_tmp("VT", [16, 128, N], BF16)
        self.KR = self.dram_tmp("KR", [32, N], BF16)
        self.AT = self.dram_tmp("AT", [16, 128, N], BF16)

        with contextlib.ExitStack() as gst:
            self.S = Sched(nc, gst)
            self.psbig = gst.enter_context(nc.psum_tensor("psbig", [128, 7 * 512], F32))
            self.ps = [Tile(self.psbig[:, i * 512:(i + 1) * 512]) for i in range(7)]
            self.psb = Tile(gst.enter_context(nc.psum_tensor("psb", [128, 1024], BF16)))
            for t in self.ps + [self.psb]:
                t.b[0].excl = True
            c32 = self.sb(gst, "c32", [128, 3, 128], F32)
            self.load(c32.t[:], self.cst_in, [], c32.b)
            self.idf = c32.t[:, 0, :]
            self.ones32 = c32.t[:, 1, :]
            cbf = self.sb(gst, "cbf", [128, 3, 128], BF16)
            self.cp(cbf.t[:], c32.t[:], c32.b, cbf.b)
            self.c32, self.cbf = c32, cbf
            self.idb = cbf.t[:, 0, :]
            self.onesb = cbf.t[:, 1, :]
            self.blkb = cbf.t[:, 2, :]
            self.sw = {}
            for n, r in (("swA", 128), ("swB", 128), ("swD", 96), ("swDk", 32)):
                t32 = self.sb(gst, n + "32", [r, r], F32)
                tb = self.sb(gst, n + "b", [r, r], BF16)
                self.load(t32.t[:], self.sw_in[n], [], t32.b)
                self.cp(tb.t[:], t32.t[:], t32.b, tb.b)
                self.sw[n] = tb
            self.gn = {}
            for n in ("norm_mix", "norm_x", "norm_mem", "norm_mlp"):
                t = self.sb(gst, "g_" + n, [128, 4, KC], F32)
                for l in range(4):
                    self.load(t.t[:, l, :], self.g_in[n][l].rearrange("(kc p) -> p kc", p=128), [], t.b, slow=True)
                self.gn[n] = t
            t = self.sb(gst, "g_final", [128, KC], F32)
            self.load(t.t[:], self.g_in["final_norm"].rearrange("(kc p) -> p kc", p=128), [], t.b, slow=True)
            self.gn["final_norm"] = t
            t = self.sb(gst, "g_dq", [128, 3], F32)
            self.load(t.t[:], self.g_in["d_q_norm"][0].rearrange("(kc p) -> p kc", p=128), [], t.b, slow=True)
            self.gn["d_q_norm"] = t
            t = self.sb(gst, "g_dkv", [128, 2], F32)
            self.load(t.t[:], self.g_in["d_kv_norm"][0].rearrange("(kc p) -> p kc", p=128), [], t.b, slow=True)
            self.gn["d_kv_norm"] = t
            t = self.sb(gst, "g_bqk", [128, 2], F32)
            for j, n in enumerate(("b_q_norm", "b_k_norm")):
                for hh in range(2):
                    self.load(t.t[hh * 64:(hh + 1) * 64, j:j + 1], self.g_in[n][0].rearrange("(p o) -> p o", o=1),
                              [], t.b, slow=True)
            self.gn["b_qk"] = t
            t = self.sb(gst, "g_csub", [128, 1], F32)
            self.load(t.t[:], self.g_in["c_sub_norm"][0].rearrange("(p o) -> p o", o=1), [], t.b, slow=True)
            self.gn["c_sub"] = t
            mA32 = self.sb(gst, "mA32", [128, 5, 128], F32)
            self.load(mA32.t[:], self.maskA_in, [], mA32.b)
            self.maskA = self.sb(gst, "maskA", [128, 5, 128], BF16)
            self.cp(self.maskA.t[:], mA32.t[:], mA32.b, self.maskA.b)
            self.xbias = self.sb(gst, "xbias", [128, 2], F32)
            self.load(self.xbias.t[:], self.xbias_in, [], self.xbias.b)
            self.lam_init = 0.8 - 0.6 * math.exp(-0.3 * 2)
            lv = self.sb(gst, "lamv", [128, 4, 64], F32)
            for j, n in enumerate(("c_lambda_q1", "c_lambda_k1", "c_lambda_q2", "c_lambda_k2")):
                self.load(lv.t[:, j, :], self.g_in[n][0].partition_broadcast(128), [], lv.b)
            lp = self.sb(gst, "lamp", [128, 2, 64], F32)
            ls = self.sb(gst, "lams", [128, 4], F32)
            self.tt(lp.t[:, 0, :], lv.t[:, 0, :], lv.t[:, 1, :], ALU.mult, lv.b, lp.b)
            self.tt(lp.t[:, 1, :], lv.t[:, 2, :], lv.t[:, 3, :], ALU.mult, lv.b, lp.b)
            self.op("dve", lambda e: e.tensor_reduce(out=ls.t[:, 0:2], in_=lp.t[:], axis=mybir.AxisListType.X, op=ALU.add),
                    reads=lp.b, writes=ls.b)
            self.act(ls.t[:, 2:4], ls.t[:, 0:2], AF.Exp, ls.b, ls.b)
            self.neglam = self.sb(gst, "neglam", [128, 1], F32)
            self.stt(self.neglam.t[:], ls.t[:, 3:4], -self.lam_init, ls.t[:, 2:3], ALU.add, ALU.subtract, ls.b, self.neglam.b)
            self.csubg = self.sb(gst, "csubg", [128, 1], F32)
            self.act(self.csubg.t[:], self.gn["c_sub"].t[:], AF.Copy, self.gn["c_sub"].b, self.csubg.b,
                     scale=1.0 - self.lam_init)
            self.wB = {n: Buf() for n in WNAMES}
            for n in WNAMES:
                L, R, C = WSHAPES[n]
                rows = max(128, (min(R, (1 << 20) // C) // 16) * 16)
                for l in range(L):
                    for r0 in range(0, R, rows):
                        r1 = min(R, r0 + rows)
                        self.load(self.w_bf[n][l, r0:r1, :], self.w_in[n][l, r0:r1, :], [], [self.wB[n]], q="pool")
            self.S.emit_phase()

            import os
            maxph = int(os.environ.get("MK_MAXPH", "999"))
            steps = [self.phase_p0]
            for l in range(self.depth):
                m = l % 4
                steps.append(lambda l=l, m=m: self.phase_p1(l, m))
                steps.append((lambda: self.phase_attn_A()) if m == 0 else (lambda m=m: self.phase_attn(m)))
                steps.append(lambda l=l, m=m: self.phase_p2b(l, m))
            if self.depth < DEPTH:
                steps.append(self.phase_final_only)
            for i, f in enumerate(steps):
                if i >= maxph:
                    break
                f()

    def phase_p0(self):
        with contextlib.ExitStack() as st:
            xin = [self.sb(st, "p0_xin%d" % i, [128, 4, D], F32) for i in range(2)]
            xo = [self.sb(st, "p0_xo%d" % i, [128, KC, ST], F32, nb=KC) for i in range(2)]
            for s in range(self.NST):
                a, o = xin[s % 2], xo[s % 2]
                self.load(a.t[:], self.x_in[s * ST:(s + 1) * ST, :].rearrange("(tt p) d -> p tt d", p=128), [], a.b)
                for kc in range(KC):
                    bank = self.ps[kc % 4]
                    for tt_ in range(4):
                        self.tr(bank.t[:, tt_ * 128:(tt_ + 1) * 128], a.t[:, tt_, kc * 128:(kc + 1) * 128], self.idf,
                                a.b + self.c32.b, bank.b)
                    self.cp(o.t[:, kc, :], bank.t[:], bank.b, [o.b[kc]], q=("act" if kc % 2 else "dve"))
                self.load(self.xT[:, :, s * ST:(s + 1) * ST].rearrange("kc p t -> p kc t"), o.t[:], o.b, [], q="pool")
            self.S.emit_phase()

    def rms(self, src, nkc, nfeat, gain_ap_fn, out, sq, rstd, ssbank, ones_lhsT=None, ones_reads=None, cols=ST):
        if ones_lhsT is None:
            ones_lhsT, ones_reads = self.onesb, self.cbf.b
        for kc in range(nkc):
            self.act(sq.t[:, kc, :cols], src.t[:, kc, :cols], AF.Square, src.b if len(src.b) == 1 else [src.b[kc]],
                     sq.b if len(sq.b) == 1 else [sq.b[kc]])
        for kc in range(nkc):
            self.mm(ssbank.t[:, :cols], ones_lhsT, sq.t[:, kc, :cols], kc == 0, kc == nkc - 1,
                    (sq.b if len(sq.b) == 1 else [sq.b[kc]]) + ones_reads, ssbank.b)
        self.act(rstd.t[:, :cols], ssbank.t[:, :cols], AF.Sqrt, ssbank.b + self.epsb.b, rstd.b, scale=1.0 / nfeat, bias=self.epsb.t[:])
        self.recip(rstd.t[:, :cols], rstd.t[:, :cols], rstd.b, rstd.b)
        for kc in range(nkc):
            self.stt(out.t[:, kc, :cols], src.t[:, kc, :cols], gain_ap_fn(kc), rstd.t[:, :cols], ALU.mult, ALU.mult,
                     (src.b if len(src.b) == 1 else [src.b[kc]]) + rstd.b + self.gall,
                     out.b if len(out.b) == 1 else [out.b[kc]])

    def layer(self, l):
        m = l % 4
        self.phase_p1(l, m)
        if m == 0:
            self.phase_attn_A()
        else:
            self.phase_attn(m)
        self.phase_p2b(l, m)

    def common_tiles(self, st):
        self.epsb = self.sb(st, "epsb", [128, 1], F32)
        self.memset(self.epsb.t[:], EPS, self.epsb.b)
        self.gall = []
        for t in self.gn.values():
            self.gall += t.b

    def rope(self, st_tiles, psrc, rows, swname, cosT, sinT, s_idx, out_ap, out_b, tag):
        import os
        if os.environ.get("MK_NOROPE"):
            self.cp(out_ap, psrc.t[:rows, :], psrc.b, out_b, q="act")
            return
        xb, t1, t2, swbank = st_tiles
        lvl = int(os.environ.get("MK_ROPE", "9"))
        sw = self.sw[swname]
        self.cp(xb.t[:rows, :], psrc.t[:rows, :], psrc.b, xb.b, q="act")
        self.mm(swbank.t[:rows, :], sw.t[:], xb.t[:rows, :], True, True, xb.b + sw.b, swbank.b)
        if lvl == 1:
            self.cp(out_ap, swbank.t[:rows, :], swbank.b + psrc.b, out_b, q="act")
            return
        self.tt(t1.t[:rows, :], psrc.t[:rows, :], cosT.t[:rows, :], ALU.mult, psrc.b + cosT.b, t1.b)
        self.tt(t2.t[:rows, :], swbank.t[:rows, :], sinT.t[:rows, :], ALU.mult, swbank.b + sinT.b, t2.b)
        if lvl == 2:
            self.cp(out_ap, t2.t[:rows, :], t1.b + t2.b, out_b, q="act")
            return
        if lvl == 3:
            self.tt(out_ap, t1.t[:rows, :], t2.t[:rows, :], ALU.add, t1.b + t2.b, out_b, q="dve")
            return
        self.tt(out_ap, t1.t[:rows, :], t2.t[:rows, :], ALU.add, t1.b + t2.b, out_b, q="pool")

    def phase_p1(self, l, m):
        N, NST = self.N, self.NST
        wname = ("a_w_in", "b_w_in", "c_w_in", "d_w_in")[m]
        C = WSHAPES[wname][2]
        with contextlib.ExitStack() as st:
            self.common_tiles(st)
            w = self.sb(st, "p1_w", [128, KC, C], BF16)
            self.load(w.t[:], self.w_bf[wname][0].rearrange("(kc p) n -> p kc n", p=128), [self.wB[wname]], w.b)
            if m == 3:
                wuq = self.sb(st, "p1_wuq", [128, 3, 1536], BF16)
                self.load(wuq.t[:], self.w_bf["d_w_uq"][0].rearrange("(kc p) n -> p kc n", p=128), [self.wB["d_w_uq"]], wuq.b)
                wukv = self.sb(st, "p1_wukv", [128, 2, 2048], BF16)
                self.load(wukv.t[:], self.w_bf["d_w_ukv"][0].rearrange("(kc p) n -> p kc n", p=128), [self.wB["d_w_ukv"]], wukv.b)
            xs = [self.sb(st, "p1_x%d" % i, [128, KC, ST], F32) for i in range(2)]
            hT = self.sb(st, "p1_h", [128, KC, ST], BF16)
            sq = self.sb(st, "p1_sq", [128, KC, ST], BF16)
            rstd = self.sb(st, "p1_rstd", [128, ST], F32)
            tabn = ("tA", "tB", "tA", "tD")[m]
            trows = 96 if m == 3 else 128
            cosT = [self.sb(st, "p1_cos%d" % i, [trows, ST], F32) for i in range(2)]
            sinT = [self.sb(st, "p1_sin%d" % i, [trows, ST], F32) for i in range(2)]
            if m == 3:
                cosK = [self.sb(st, "p1_cosk%d" % i, [32, ST], F32) for i in range(2)]
                sinK = [self.sb(st, "p1_sink%d" % i, [32, ST], F32) for i in range(2)]
                cq = self.sb(st, "p1_cq", [128, 5, ST], F32)
                cqn = self.sb(st, "p1_cqn", [128, 5, ST], BF16)
                sq2 = self.sb(st, "p1_sq2", [128, 3, ST], BF16)
                rstd2 = self.sb(st, "p1_rstd2", [128, ST], F32)
            xb = self.sb(st, "p1_xb", [128, ST], BF16)
            t1 = self.sb(st, "p1_t1", [128, ST], F32)
            t2 = self.sb(st, "p1_t2", [128, ST], F32)
            qn = self.sb(st, "p1_qn", [128, ST], F32)
            qsq = self.sb(st, "p1_qsq", [128, 1, ST], BF16)
            NO = 4
            outs = [self.sb(st, "p1_o%d" % i, [128, ST], BF16) for i in range(NO)]
            self.p1_oi = 0
            swbank = self.ps[5]
            ssbank = self.ps[4]
            rope_tiles = (xb, t1, t2, swbank)

            def ld(s):
                x = xs[s % 2]
                self.load(x.t[:], self.xT[:, :, s * ST:(s + 1) * ST].rearrange("kc p t -> p kc t"), [], x.b)
                tc_, ts_ = self.tabs[tabn]
                self.load(cosT[s % 2].t[:], tc_[:, s * ST:(s + 1) * ST], [], cosT[s % 2].b)
                self.load(sinT[s % 2].t[:], ts_[:, s * ST:(s + 1) * ST], [], sinT[s % 2].b)
                if m == 3:
                    tc_, ts_ = self.tabs["tDk"]
                    self.load(cosK[s % 2].t[:], tc_[:, s * ST:(s + 1) * ST], [], cosK[s % 2].b)
                    self.load(sinK[s % 2].t[:], ts_[:, s * ST:(s + 1) * ST], [], sinK[s % 2].b)

            def nxt_out():
                o = outs[self.p1_oi % NO]
                self.p1_oi += 1
                return o

            def proj(bank, wt, nk, c0, cn, src):
                for kc in range(nk):
                    self.mm(bank.t[:cn, :], wt.t[:, kc, c0:c0 + cn], src.t[:, kc, :], kc == 0, kc == nk - 1,
                            wt.b + src.b, bank.b)

            def store(dst_ap, o, rows, p0=0):
                self.load(dst_ap, o.t[p0:p0 + rows, :], o.b, [], q="pool")

            ld(0)
            for s in range(NST):
                if s + 1 < NST:
                    ld(s + 1)
                x = xs[s % 2]
                cs, sn = cosT[s % 2], sinT[s % 2]
                sl = slice(s * ST, (s + 1) * ST)
                gm = self.gn["norm_mix"]
                self.rms(x, KC, D, lambda kc: gm.t[:, l, kc:kc + 1], hT, sq, rstd, ssbank)
                nb = 0
                if m == 0:
                    for c in range(36):
                        g, typ, hp = c // 12, (c % 12) // 4, c % 4
                        bank = self.ps[nb % 4]; nb += 1
                        proj(bank, w, KC, c * 128, 128, hT)
                        o = nxt_out()
                        if typ < 2:
                            self.rope(rope_tiles, bank, 128, "swA", cs, sn, s, o.t[:], o.b, "a")
                        else:
                            self.cp(o.t[:], bank.t[:], bank.b, o.b, q="act")
                        dst = (self.QT, self.KT, self.VT)[typ]
                        store(dst[g * 4 + hp, :, sl], o, 128)
                elif m == 1:
                    gq = self.gn["b_qk"]
                    for c in range(12):
                        bank = self.ps[nb % 4]; nb += 1
                        proj(bank, w, KC, c * 128, 128, hT)
                        o = nxt_out()
                        if c < 10:
                            self.cp(qn.t[:], bank.t[:], bank.b, qn.b, q="act")
                            qt = Tile(qn.t[:].rearrange("p (o t) -> p o t", o=1))
                            qt.b = qn.b
                            gcol = 0 if c < 8 else 1
                            nbk = self.ps[6]
                            self.rms(qt, 1, 64, lambda kc: gq.t[:, gcol:gcol + 1], qt, qsq, rstd, ssbank,
                                     ones_lhsT=self.blkb, ones_reads=self.cbf.b)
                            self.rope(rope_tiles, qn, 128, "swB", cs, sn, s, o.t[:], o.b, "b")
                            dst = self.QT[c] if c < 8 else self.KT[c - 8]
                        else:
                            self.cp(o.t[:], bank.t[:], bank.b, o.b, q="act")
                            dst = self.VT[c - 10]
                        store(dst[:, sl], o, 128)
                elif m == 2:
                    for c in range(24):
                        typ, hh = c // 8, c % 8
                        bank = self.ps[nb % 4]; nb += 1
                        proj(bank, w, KC, c * 128, 128, hT)
                        o = nxt_out()
                        if typ < 2:
                            self.rope(rope_tiles, bank, 128, "swA", cs, sn, s, o.t[:], o.b, "c")
                        else:
                            self.cp(o.t[:], bank.t[:], bank.b, o.b, q="act")
                        dst = (self.QT, self.KT, self.VT)[typ]
                        store(dst[hh, :, sl], o, 128)
                else:
                    for c in range(5):
                        bank = self.ps[nb % 4]; nb += 1
                        proj(bank, w, KC, c * 128, 128, hT)
                        self.cp(cq.t[:, c, :], bank.t[:], bank.b, cq.b, q="act")
                    bank = self.ps[nb % 4]; nb += 1
                    proj(bank, w, KC, 640, 32, hT)
                    o = nxt_out()
                    self.rope(rope_tiles, bank, 32, "swDk", cosK[s % 2], sinK[s % 2], s, o.t[:32, :], o.b, "dk")
                    store(self.KR[:, sl], o, 32)
                    gq_, gkv_ = self.gn["d_q_norm"], self.gn["d_kv_norm"]
                    cq_q = Tile(cq.t[:, 0:3, :]); cq_q.b = cq.b
                    cqn_q = Tile(cqn.t[:, 0:3, :]); cqn_q.b = cqn.b
                    self.rms(cq_q, 3, 384, lambda kc: gq_.t[:, kc:kc + 1], cqn_q, sq2, rstd2, ssbank)
                    cq_k = Tile(cq.t[:, 3:5, :]); cq_k.b = cq.b
                    cqn_k = Tile(cqn.t[:, 3:5, :]); cqn_k.b = cqn.b
                    self.rms(cq_k, 2, 256, lambda kc: gkv_.t[:, kc:kc + 1], cqn_k, sq2, rstd2, ssbank)
                    for hh in range(16):
                        bank = self.ps[nb % 4]; nb += 1
                        proj(bank, wuq, 3, hh * 96, 96, cqn_q)
                        o = nxt_out()
                        self.rope(rope_tiles, bank, 96, "swD", cs, sn, s, o.t[:96, :], o.b, "d")
                        store(self.QT[hh, 0:96, sl], o, 96)
                    for hh in range(16):
                        bank = self.ps[nb % 4]; nb += 1
                        proj(bank, wukv, 2, hh * 128, 128, cqn_k)
                        o = nxt_out()
                        self.cp(o.t[:], bank.t[:], bank.b, o.b, q="act")
                        store(self.KT[hh, 0:64, sl], o, 64, 0)
                        store(self.VT[hh, 0:64, sl], o, 64, 64)
            self.S.emit_phase()

    def pspair(self, i):
        t = Tile(self.psbig[:, i * 512:(i + 2) * 512])
        t.b = self.ps[i].b + self.ps[i + 1].b
        return t

    def phase_attn(self, m):
        N, NST, NT, NH = self.N, self.NST, self.NT, self.NH
        diff = (m == 2)
        if m == 1:
            nstream, dqk, dv, scale = 16, 64, 64, 64 ** -0.5
        elif m == 2:
            nstream, dqk, dv, scale = 8, 64, 128, 64 ** -0.5
        else:
            nstream, dqk, dv, scale = 16, 96, 64, 96 ** -0.5
        with contextlib.ExitStack() as st:
            self.common_tiles(st)
            ncomp = 2 if diff else 1
            dqp = 128 if dqk == 64 else dqk
            kts = [[self.sb(st, "at_k%d_%d" % (i, c), [dqp, N], BF16) for c in range(ncomp)] for i in range(2)]
            vts = [self.sb(st, "at_v%d" % i, [dv, N], BF16) for i in range(2)]
            dva = dv if diff else dv + 1
            vaug = [self.sb(st, "at_va%d" % i, [128, NT, dva], BF16) for i in range(2)]
            if not diff:
                for i in range(2):
                    self.cp(vaug[i].t[:, :, dv:dv + 1], self.onesb[:, 0:NT].rearrange("p (t o) -> p t o", o=1),
                            self.cbf.b, vaug[i].b)
            NQ = 3
            qs = [[self.sb(st, "at_q%d_%d" % (i, c), [dqp, ST], BF16) for c in range(ncomp)] for i in range(NQ)]
            if dqp != dqk:
                for grp_ in (kts, qs):
                    for row in grp_:
                        for t_ in row:
                            self.memset(t_.t[dqk:dqp, :], 0.0, t_.b)
            NP = 3
            pts = [self.sb(st, "at_p%d" % i, [128, 2 * ST], BF16) for i in range(NP)]
            rec = self.sb(st, "at_rec", [128, ST], F32)
            rec2 = self.sb(st, "at_rec2", [128, ST], F32)
            bcs = self.sb(st, "at_bcs", [128, ST], F32)
            osb = [self.sb(st, "at_o%d" % i, [128, ST], BF16) for i in range(2)]
            sgrp = [self.pspair(0), self.pspair(2)]
            if diff:
                o32 = self.sb(st, "at_o32", [128, 1, ST], F32)
                t32 = self.sb(st, "at_t32", [128, ST], F32)
                osq = self.sb(st, "at_osq", [128, 1, ST], BF16)
                rstd = self.sb(st, "at_rstd", [128, ST], F32)
                accs = [self.sb(st, "at_acc%d" % i, [128, 2 * ST], F32) for i in range(2)]
                tmps = [self.sb(st, "at_tmp%d" % i, [128, 2 * ST], BF16) for i in range(2)]
                obs = [self.ps[4], self.ps[5]]
                dbank = self.ps[6]
            units = [(h, s) for h in range(nstream) for s in range(NST)]
            NG = NT if diff else NT // 2
            steps = [(u, g) for u in range(len(units)) for g in range(NG)]

            def ld_head(h):
                i = h % 2
                for c in range(ncomp):
                    kt = kts[i][c]
                    if m == 1:
                        src = self.KT[h // 8, ((h // 4) % 2) * 64:((h // 4) % 2) * 64 + 64, :]
                        self.load(kt.t[0:64, :], src, [], kt.b)
                    elif m == 2:
                        self.load(kt.t[0:64, :], self.KT[h, c * 64:(c + 1) * 64, :], [], kt.b)
                    else:
                        self.load(kt.t[0:64, :], self.KT[h, 0:64, :], [], kt.b)
                        self.load(kt.t[64:96, :], self.KR[:, :], [], kt.b)
                vt = vts[i]
                if m == 1:
                    self.load(vt.t[:], self.VT[h // 8, ((h // 4) % 2) * 64:((h // 4) % 2) * 64 + 64, :], [], vt.b)
                elif m == 2:
                    self.load(vt.t[:], self.VT[h, :, :], [], vt.b)
                else:
                    self.load(vt.t[:], self.VT[h, 0:64, :], [], vt.b)

            def ld_q(u):
                h, s = units[u]
                sl = slice(s * ST, (s + 1) * ST)
                for c in range(ncomp):
                    q = qs[u % NQ][c]
                    if m == 1:
                        self.load(q.t[0:64, :], self.QT[h // 2, (h % 2) * 64:(h % 2) * 64 + 64, sl], [], q.b)
                    elif m == 2:
                        self.load(q.t[0:64, :], self.QT[h, c * 64:(c + 1) * 64, sl], [], q.b)
                    else:
                        self.load(q.t[:], self.QT[h, 0:96, sl], [], q.b)

            def prep_v(h):
                i = h % 2
                vt, va = vts[i], vaug[i]
                per = 1024 // dv
                for t0 in range(0, NT, per):
                    for j in range(per):
                        self.tr(self.psb.t[:, j * dv:(j + 1) * dv], vt.t[:, (t0 + j) * 128:(t0 + j + 1) * 128],
                                self.idb[0:dv, 0:dv], vt.b + self.cbf.b, self.psb.b)
                    self.cp(va.t[:, t0:t0 + per, 0:dv], self.psb.t[:, 0:per * dv].rearrange("p (t d) -> p t d", d=dv),
                            self.psb.b, va.b, q="dve")

            def stageA(i):
                u, g = steps[i]
                h, s = units[u]
                grp = sgrp[i % 2]
                for j in range(2):
                    if diff:
                        t, c = g, j
                    else:
                        t, c = 2 * g + j, 0
                    kt = kts[h % 2][c]
                    q = qs[u % NQ][c]
                    self.mm(grp.t[:, j * ST:(j + 1) * ST], kt.t[:, t * 128:(t + 1) * 128], q.t[:], True, True,
                            kt.b + q.b, [grp.b[j]])

            def stageE(i):
                u, g = steps[i]
                h, s = units[u]
                grp = sgrp[i % 2]
                p = pts[i % NP]
                t0 = g if diff else 2 * g
                qhalf = (s * ST) // NH
                if (t0 * 128) // NH == qhalf:
                    self.act(p.t[:], grp.t[:], AF.Exp, grp.b, p.b, scale=scale)
                else:
                    self.act(p.t[:], grp.t[:], AF.Exp, grp.b + self.xbias.b, p.b, scale=scale, bias=self.xbias.t[:, 1:2])

            def stageB(i):
                u, g = steps[i]
                h, s = units[u]
                p = pts[i % NP]
                va = vaug[h % 2]
                if not diff:
                    ob = self.ps[4 + (u % 2)]
                    for j in range(2):
                        t = 2 * g + j
                        self.mm(ob.t[0:dv + 1, :], va.t[:, t, :], p.t[:, j * ST:(j + 1) * ST], t == 0, t == NT - 1,
                                va.b + p.b, ob.b)
                else:
                    t = g
                    for c in range(2):
                        self.mm(obs[c].t[:], va.t[:, t, :], p.t[:, c * ST:(c + 1) * ST], t == 0, t == NT - 1,
                                va.b + p.b, obs[c].b)
                    if t % 2 == 1:
                        pprev = pts[(i - 1) % NP]
                        ai = (t // 2) % 2
                        tmp = tmps[ai]
                        self.tt(tmp.t[:], pprev.t[:], p.t[:], ALU.add, pprev.b + p.b, tmp.b)
                        eng = "pool" if ai else "dve"
                        if t // 2 < 2:
                            self.cp(accs[ai].t[:], tmp.t[:], tmp.b, accs[ai].b, q=eng)
                        else:
                            self.tt(accs[ai].t[:], accs[ai].t[:], tmp.t[:], ALU.add, tmp.b + accs[ai].b, accs[ai].b, q=eng)

            def epilogue(u):
                h, s = units[u]
                if not diff:
                    ob = self.ps[4 + (u % 2)]
                    self.recip(rec.t[64:65, :], ob.t[64:65, :], ob.b, rec.b)
                    bb = self.ps[6]
                    self.mm(bb.t[0:64, :], self.ones32[64:65, 0:64], rec.t[64:65, :], True, True, rec.b + self.c32.b, bb.b)
                    self.cp(bcs.t[0:64, :], bb.t[0:64, :], bb.b, bcs.b, q="act")
                    o = osb[u % 2]
                    self.tt(o.t[0:64, :], ob.t[0:64, :], bcs.t[0:64, :], ALU.mult, ob.b + bcs.b, o.b)
                    self.load(self.AT[h // 2, (h % 2) * 64:(h % 2) * 64 + 64, s * ST:(s + 1) * ST], o.t[0:64, :], o.b, [], q="pool")
                else:
                    self.tt(accs[0].t[:], accs[0].t[:], accs[1].t[:], ALU.add, accs[0].b + accs[1].b, accs[0].b)
                    for c, r_ in ((0, rec), (1, rec2)):
                        self.mm(dbank.t[:], self.ones32, accs[0].t[:, c * ST:(c + 1) * ST], True, True,
                                accs[0].b + self.c32.b, dbank.b)
                        self.recip(r_.t[:], dbank.t[:], dbank.b, r_.b)
                    self.tt(o32.t[:, 0, :], obs[0].t[:], rec.t[:], ALU.mult, obs[0].b + rec.b, o32.b)
                    self.tt(t32.t[:], obs[1].t[:], rec2.t[:], ALU.mult, obs[1].b + rec2.b, t32.b)
                    self.stt(o32.t[:, 0, :], t32.t[:], self.neglam.t[:], o32.t[:, 0, :], ALU.mult, ALU.add,
                             t32.b + o32.b + self.neglam.b, o32.b)
                    o = osb[u % 2]
                    ot = Tile(o.t[:].rearrange("p (o t) -> p o t", o=1)); ot.b = o.b
                    self.rms(o32, 1, 128, lambda kc: self.csubg.t[:], ot, osq, rstd, dbank)
                    self.load(self.AT[h, :, s * ST:(s + 1) * ST], o.t[:], o.b, [], q="pool")

            ld_head(0)
            ld_q(0)
            ld_q(1)
            stageA(0)
            for i, (u, g) in enumerate(steps):
                h, s = units[u]
                if g == 0:
                    if s == 0:
                        prep_v(h)
                        if h + 1 < nstream:
                            ld_head(h + 1)
                    if u + 2 < len(units):
                        ld_q(u + 2)
                if i + 1 < len(steps):
                    stageA(i + 1)
                stageE(i)
                stageB(i)
                if g == NG - 1:
                    epilogue(u)
            self.S.emit_phase()

    def phase_attn_A(self):
        N, NT, NH, NST = self.N, self.NT, self.NH, self.NST
        scale = 64 ** -0.5
        with contextlib.ExitStack() as st:
            self.common_tiles(st)
            qkv = [[self.sb(st, "aa_%s%d" % (n, i), [128 if n != "v" else 64, N], BF16) for n in "qkv"] for i in range(2)]
            for i in range(2):
                for j in range(2):
                    self.memset(qkv[i][j].t[64:128, :], 0.0, qkv[i][j].b)
            vaug = [self.sb(st, "aa_va%d" % i, [128, NT, 65], BF16) for i in range(2)]
            for i in range(2):
                self.cp(vaug[i].t[:, :, 64:65], self.onesb[:, 0:NT].rearrange("p (t o) -> p t o", o=1), self.cbf.b, vaug[i].b)
            acc = self.sb(st, "aa_acc", [65, N], F32)
            NP = 3
            es = [self.sb(st, "aa_e%d" % i, [128, 3, 128], BF16) for i in range(NP)]
            pp = [self.sb(st, "aa_p%d" % i, [128, 3, 128], BF16) for i in range(NP)]
            rec = self.sb(st, "aa_rec", [65, ST], F32)
            osb = [self.sb(st, "aa_o%d" % i, [64, ST], BF16) for i in range(2)]
            accsync = Buf()
            combos = [(h, g) for h in range(8) for g in range(3)]

            def ld(ci):
                h, g = combos[ci]
                for j, src in enumerate((self.QT, self.KT, self.VT)):
                    t = qkv[ci % 2][j]
                    self.load(t.t[0:64, :], src[g * 4 + h // 2, (h % 2) * 64:(h % 2) * 64 + 64, :], [], t.b)

            def toks(d, s, r, i):
                start = s * NH + (128 * i) * d + r
                return slice(start, start + 127 * d + 1, d)

            ld(0)
            ui = 0
            for ci, (h, g) in enumerate(combos):
                if ci + 1 < len(combos):
                    ld(ci + 1)
                d = A_DILS[g]
                q, k, v = qkv[ci % 2]
                va = vaug[ci % 2]
                npp = NH // d // 128
                tiles = [(s, r, i) for s in range(2) for r in range(d) for i in range(npp)]
                tidx = {t: j for j, t in enumerate(tiles)}
                for t0 in range(0, NT, 16):
                    for j in range(16):
                        s, r, i = tiles[t0 + j]
                        self.tr(self.psb.t[:, j * 64:(j + 1) * 64], v.t[:, toks(d, s, r, i)], self.idb[0:64, 0:64],
                                v.b + self.cbf.b, self.psb.b)
                    self.cp(va.t[:, t0:t0 + 16, 0:64], self.psb.t[:, :].rearrange("p (t d) -> p t d", d=64),
                            self.psb.b, va.b, q="dve")
                first_evac = [True]

                def nbrs(s, r, i):
                    nb = []
                    if i > 0:
                        nb.append(((s, r, i - 1), 0))
                    elif s == 1:
                        nb.append(((0, r, npp - 1), 3))
                    nb.append(((s, r, i), 1))
                    if i < npp - 1:
                        nb.append(((s, r, i + 1), 2))
                    elif s == 0:
                        nb.append(((1, r, 0), 4))
                    return nb

                def stA(j_, u_):
                    s, r, i = tiles[j_]
                    sb_ = self.ps[u_ % 3]
                    for j, (kt_, mi) in enumerate(nbrs(s, r, i)):
                        self.mm(sb_.t[:, j * 128:(j + 1) * 128], k.t[:, toks(d, *kt_)], q.t[:, toks(d, s, r, i)],
                                True, True, k.b + q.b, sb_.b)

                def stEB(j_, u_):
                    s, r, i = tiles[j_]
                    nb = nbrs(s, r, i)
                    nn = len(nb)
                    sb_ = self.ps[u_ % 3]
                    e = es[u_ % NP]
                    p = pp[u_ % NP]
                    ob = self.ps[3 + (u_ % 2)]
                    self.act(e.t[:, 0:nn, :], sb_.t[:, 0:nn * 128].rearrange("p (j t) -> p j t", t=128), AF.Exp,
                             sb_.b, e.b, scale=scale)
                    for j, (kt_, mi) in enumerate(nb):
                        self.tt(p.t[:, j, :], e.t[:, j, :], self.maskA.t[:, mi, :], ALU.mult, e.b + self.maskA.b, p.b,
                                q=("pool" if j == 1 else "dve"))
                    for j, (kt_, mi) in enumerate(nb):
                        self.mm(ob.t[0:65, 0:128], va.t[:, tidx[kt_], :], p.t[:, j, :], j == 0, j == nn - 1, va.b + p.b, ob.b)
                    asl = acc.t[:, toks(d, s, r, i)]
                    rd = [accsync] if first_evac[0] else []
                    first_evac[0] = False
                    if g == 0:
                        self.cp(asl, ob.t[0:65, 0:128], ob.b + rd, [], q="dve")
                    else:
                        self.tt(asl, asl, ob.t[0:65, 0:128], ALU.add, ob.b + rd, [])

                stA(0, ui)
                for j_ in range(len(tiles)):
                    if j_ + 1 < len(tiles):
                        stA(j_ + 1, ui + 1)
                    stEB(j_, ui)
                    ui += 1
                self.memset(rec.t[0:1, 0:1], 0.0, [accsync])
                if g == 2:
                    for ck in range(NST):
                        sl = slice(ck * ST, (ck + 1) * ST)
                        self.recip(rec.t[64:65, :], acc.t[64:65, sl], [accsync] + rec.b, rec.b)
                        bb = self.ps[5]
                        self.mm(bb.t[0:64, :], self.ones32[64:65, 0:64], rec.t[64:65, :], True, True, rec.b + self.c32.b, bb.b)
                        o = osb[ck % 2]
                        self.tt(o.t[:], acc.t[0:64, sl], bb.t[0:64, :], ALU.mult, bb.b, o.b)
                        self.load(self.AT[h // 2, (h % 2) * 64:(h % 2) * 64 + 64, sl], o.t[:], o.b, [], q="pool")
                    self.memset(rec.t[0:1, 0:1], 0.0, [accsync])
            self.S.emit_phase()

    def phase_p2b(self, l, m):
        N, NST, NH = self.N, self.NST, self.NH
        last = (l == DEPTH - 1)
        woname = ("a_w_out", "b_w_out", "c_w_out", "d_w_out")[m]
        dvp = 128
        nh = WSHAPES[woname][1] // dvp
        xscale = 128 ** -0.5
        with contextlib.ExitStack() as st:
            self.common_tiles(st)
            kmT = self.sb(st, "pb_kmT", [128, 2, 4, 256], BF16)
            vm = self.sb(st, "pb_vm", [128, 2, 2, 512], BF16)
            ssbank = self.ps[4]
            with contextlib.ExitStack() as st2:
                wkv = self.sb(st2, "pb_wkv", [128, KC, D], BF16)
                self.load(wkv.t[:], self.w_bf["w_xkv"][l].rearrange("(kc p) n -> p kc n", p=128), [self.wB["w_xkv"]], wkv.b)
                mtok = self.sb(st2, "pb_mtok", [128, 2, D], F32)
                mT = self.sb(st2, "pb_mT", [128, KC, 256], F32)
                mTn = self.sb(st2, "pb_mTn", [128, KC, 256], BF16)
                msq = self.sb(st2, "pb_msq", [128, KC, 256], BF16)
                mrs = self.sb(st2, "pb_mrs", [128, ST], F32)
                gmem = self.gn["norm_mem"]
                for hf in range(2):
                    self.load(mtok.t[:], self.mem_in[hf].rearrange("(t p) d -> p t d", p=128), [], mtok.b)
                    for kc in range(KC):
                        bank = self.ps[kc % 4]
                        for t in range(2):
                            self.tr(bank.t[:, t * 128:(t + 1) * 128], mtok.t[:, t, kc * 128:(kc + 1) * 128], self.idf,
                                    mtok.b + self.c32.b, bank.b)
                        self.cp(mT.t[:, kc, :], bank.t[:, 0:256], bank.b, mT.b, q=("act" if kc % 2 else "dve"))
                    self.rms(mT, KC, D, lambda kc: gmem.t[:, l, kc:kc + 1], mTn, msq, mrs, ssbank, cols=256)
                    for hd in range(4):
                        bank = self.ps[hd % 4]
                        for kc in range(KC):
                            self.mm(bank.t[:, 0:256], wkv.t[:, kc, hd * 128:(hd + 1) * 128], mTn.t[:, kc, :], kc == 0,
                                    kc == KC - 1, wkv.b + mTn.b, bank.b)
                        self.cp(kmT.t[:, hf, hd, :], bank.t[:, 0:256], bank.b, kmT.b, q="act")
                    for t in range(2):
                        bank = self.ps[t % 4]
                        for kc in range(KC):
                            self.mm(bank.t[:], mTn.t[:, kc, t * 128:(t + 1) * 128], wkv.t[:, kc, 512:1024], kc == 0,
                                    kc == KC - 1, wkv.b + mTn.b, bank.b)
                        self.cp(vm.t[:, hf, t, :], bank.t[:], bank.b, vm.b, q="dve")
                self.S.emit_phase()
            wo = self.sb(st, "pb_wo", [dvp, nh, D], BF16)
            self.load(wo.t[:], self.w_bf[woname][0].rearrange("(h p) n -> p h n", p=dvp), [self.wB[woname]], wo.b)
            wxq = self.sb(st, "pb_wxq", [128, KC, 512], BF16)
            self.load(wxq.t[:], self.w_bf["w_xq"][l].rearrange("(kc p) n -> p kc n", p=128), [self.wB["w_xq"]], wxq.b)
            wxo = self.sb(st, "pb_wxo", [128, 4, D], BF16)
            self.load(wxo.t[:], self.w_bf["w_xo"][l].rearrange("(kc p) n -> p kc n", p=128), [self.wB["w_xo"]], wxo.b)
            NW1, NW2 = 2, 2
            w1 = [self.sb(st, "pb_w1_%d" % i, [128, KC, 512], BF16) for i in range(NW1)]
            w2 = [self.sb(st, "pb_w2_%d" % i, [128, 16, 256], BF16) for i in range(NW2)]
            xs = [self.sb(st, "pb_x%d" % i, [128, KC, ST], F32, nb=KC) for i in range(2)]
            ats = [self.sb(st, "pb_at%d" % i, [dvp, nh, ST], BF16) for i in range(2)]
            hT = self.sb(st, "pb_h", [128, KC, ST], BF16)
            sq = self.sb(st, "pb_sq", [128, KC, ST], BF16)
            rstd = self.sb(st, "pb_rstd", [128, ST], F32)
            qx = self.sb(st, "pb_qx", [128, 4, ST], BF16)
            px = [self.sb(st, "pb_px%d" % i, [128, ST], BF16) for i in range(2)]
            recx = self.sb(st, "pb_recx", [128, ST], F32)
            ox = self.sb(st, "pb_ox", [128, 4, ST], BF16)
            aT = self.sb(st, "pb_a", [128, 16, ST], BF16, nb=16)
            rl = [self.sb(st, "pb_rl%d" % i, [128, ST], F32) for i in range(2)]
            if last:
                ytok = [self.sb(st, "pb_ytok%d" % i, [128, D], F32) for i in range(2)]

            def ld(s):
                x = xs[s % 2]
                self.load(x.t[:], self.xT[:, :, s * ST:(s + 1) * ST].rearrange("kc p t -> p kc t"), [], x.all)
                a = ats[s % 2]
                self.load(a.t[:], self.AT[0:nh, 0:dvp, s * ST:(s + 1) * ST].rearrange("h p t -> p h t"), [], a.b)

            w1i, w2i, nb = [0], [0], [0]

            def ld_w1(j):
                t = w1[w1i[0] % NW1]; w1i[0] += 1
                self.load(t.t[:], self.w_bf["w_mlp_in"][l][:, j * 512:(j + 1) * 512].rearrange("(kc p) n -> p kc n", p=128),
                          [self.wB["w_mlp_in"]], t.b)
                return t

            def ld_w2(j):
                hf2, c = j // 4, j % 4
                t = w2[w2i[0] % NW2]; w2i[0] += 1
                self.load(t.t[:], self.w_bf["w_mlp_out"][l][hf2 * 2048:(hf2 + 1) * 2048, c * 256:(c + 1) * 256]
                          .rearrange("(fc p) n -> p fc n", p=128), [self.wB["w_mlp_out"]], t.b)
                return t

            def bank_():
                b = self.ps[nb[0] % 4]; nb[0] += 1
                return b

            ld(0)
            for s in range(NST):
                if s + 1 < NST:
                    ld(s + 1)
                x, a = xs[s % 2], ats[s % 2]
                hf = (s * ST) // NH
                w1q = [ld_w1(0), ld_w1(1)]
                for oc in range(KC):
                    bank = bank_()
                    for hh in range(nh):
                        self.mm(bank.t[:], wo.t[:, hh, oc * 128:(oc + 1) * 128], a.t[:, hh, :], hh == 0, hh == nh - 1,
                                wo.b + a.b, bank.b)
                    self.tt(x.t[:, oc, :], x.t[:, oc, :], bank.t[:], ALU.add, [x.b[oc]] + bank.b, [x.b[oc]])
                gx = self.gn["norm_x"]
                self.rms(x, KC, D, lambda kc: gx.t[:, l, kc:kc + 1], hT, sq, rstd, ssbank)
                for hd in range(4):
                    bank = bank_()
                    for kc in range(KC):
                        self.mm(bank.t[:], wxq.t[:, kc, hd * 128:(hd + 1) * 128], hT.t[:, kc, :], kc == 0, kc == KC - 1,
                                wxq.b + hT.b, bank.b)
                    self.cp(qx.t[:, hd, :], bank.t[:], bank.b, qx.b, q="act")
                for hd in range(4):
                    ob, db = self.ps[5], self.ps[6]
                    for t in range(2):
                        bank = bank_()
                        p = px[t]
                        self.mm(bank.t[:], kmT.t[:, hf, hd, t * 128:(t + 1) * 128], qx.t[:, hd, :], True, True,
                                kmT.b + qx.b, bank.b)
                        self.act(p.t[:], bank.t[:], AF.Exp, bank.b, p.b, scale=xscale)
                        self.mm(ob.t[:], vm.t[:, hf, t, hd * 128:(hd + 1) * 128], p.t[:], t == 0, t == 1, vm.b + p.b, ob.b)
                        self.mm(db.t[:], self.onesb, p.t[:], t == 0, t == 1, self.cbf.b + p.b, db.b)
                    self.recip(recx.t[:], db.t[:], db.b, recx.b)
                    self.tt(ox.t[:, hd, :], ob.t[:], recx.t[:], ALU.mult, ob.b + recx.b, ox.b)
                for oc in range(KC):
                    bank = bank_()
                    for hd in range(4):
                        self.mm(bank.t[:], wxo.t[:, hd, oc * 128:(oc + 1) * 128], ox.t[:, hd, :], hd == 0, hd == 3,
                                wxo.b + ox.b, bank.b)
                    self.tt(x.t[:, oc, :], x.t[:, oc, :], bank.t[:], ALU.add, [x.b[oc]] + bank.b, [x.b[oc]])
                gm = self.gn["norm_mlp"]
                self.rms(x, KC, D, lambda kc: gm.t[:, l, kc:kc + 1], hT, sq, rstd, ssbank)
                for hf2 in range(2):
                    w2q = [ld_w2(hf2 * 4 + 0)]
                    for j in range(4):
                        wt = w1q.pop(0)
                        for f in range(4):
                            fc = j * 4 + f
                            bank = bank_()
                            for kc in range(KC):
                                self.mm(bank.t[:], wt.t[:, kc, f * 128:(f + 1) * 128], hT.t[:, kc, :], kc == 0, kc == KC - 1,
                                        wt.b + hT.b, bank.b)
                            r = rl[fc % 2]
                            self.act(r.t[:], bank.t[:], AF.Relu, bank.b, r.b)
                            self.tt(aT.t[:, fc, :], r.t[:], r.t[:], ALU.mult, r.b, [aT.b[fc]], q="pool")
                        jj = hf2 * 4 + j
                        if jj + 2 < 8:
                            w1q.append(ld_w1(jj + 2))
                        if j == 2:
                            w2q.append(ld_w2(hf2 * 4 + 1))
                    for j in range(4):
                        wt = w2q.pop(0)
                        for o2 in range(2):
                            oc = j * 2 + o2
                            bank = bank_()
                            for fc in range(16):
                                self.mm(bank.t[:], wt.t[:, fc, o2 * 128:(o2 + 1) * 128], aT.t[:, fc, :], fc == 0, fc == 15,
                                        wt.b + [aT.b[fc]], bank.b)
                            self.tt(x.t[:, oc, :], x.t[:, oc, :], bank.t[:], ALU.add, [x.b[oc]] + bank.b, [x.b[oc]])
                        if j + 2 < 4:
                            w2q.append(ld_w2(hf2 * 4 + j + 2))
                if not last:
                    self.load(self.xT[:, :, s * ST:(s + 1) * ST].rearrange("kc p t -> p kc t"), x.t[:], x.all, [], q="pool")
                else:
                    gf = self.gn["final_norm"]
                    self.rms(x, KC, D, lambda kc: gf.t[:, kc:kc + 1], x, sq, rstd, ssbank)
                    yT = x
                    for tt_ in range(4):
                        yt = ytok[tt_ % 2]
                        for half in range(2):
                            bank = bank_()
                            for k4 in range(4):
                                kc = half * 4 + k4
                                self.tr(bank.t[:, k4 * 128:(k4 + 1) * 128], yT.t[:, kc, tt_ * 128:(tt_ + 1) * 128], self.idf,
                                        [yT.b[kc]] + self.c32.b, bank.b)
                            self.cp(yt.t[:, half * 512:(half + 1) * 512], bank.t[:], bank.b, yt.b, q=("act" if half else "dve"))
                        r0 = s * ST + tt_ * 128
                        self.load(self.y_out[r0:r0 + 128, :], yt.t[:], yt.b, [], q="pool")
            self.S.emit_phase()

    def phase_final_only(self):
        NST = self.NST
        with contextlib.ExitStack() as st:
            self.common_tiles(st)
            xs = [self.sb(st, "pf_x%d" % i, [128, KC, ST], F32) for i in range(2)]
            yT = self.sb(st, "pf_yT", [128, KC, ST], F32)
            sq = self.sb(st, "pf_sq", [128, KC, ST], BF16)
            rstd = self.sb(st, "pf_rstd", [128, ST], F32)
            ytok = [self.sb(st, "pf_ytok%d" % i, [128, D], F32) for i in range(2)]
            nb = 0
            for s in range(NST):
                x = xs[s % 2]
                self.load(x.t[:], self.xT[:, :, s * ST:(s + 1) * ST].rearrange("kc p t -> p kc t"), [], x.b)
                gf = self.gn["final_norm"]
                self.rms(x, KC, D, lambda kc: gf.t[:, kc:kc + 1], yT, sq, rstd, self.ps[4])
                for tt_ in range(4):
                    yt = ytok[tt_ % 2]
                    for half in range(2):
                        bank = self.ps[nb % 4]; nb += 1
                        for k4 in range(4):
                            kc = half * 4 + k4
                            self.tr(bank.t[:, k4 * 128:(k4 + 1) * 128], yT.t[:, kc, tt_ * 128:(tt_ + 1) * 128], self.idf,
                                    yT.b + self.c32.b, bank.b)
                        self.cp(yt.t[:, half * 512:(half + 1) * 512], bank.t[:], bank.b, yt.b, q=("act" if half else "dve"))
                    r0 = s * ST + tt_ * 128
                    self.load(self.y_out[r0:r0 + 128, :], yt.t[:], yt.b, [], q="pool")
            self.S.emit_phase()


def _rope_tab(pos, theta, rot):
    half = rot // 2
    inv = np.exp(np.arange(half, dtype=np.float32) * np.float32(-2.0 * math.log(theta) / rot)).astype(np.float32)
    ang = pos.astype(np.float32)[None, :] * inv[:, None]
    return np.cos(ang).astype(np.float32), np.sin(ang).astype(np.float32)


def host_tables(NH, is_pair):
    N = 2 * NH
    t = np.arange(N)
    pos = (t % NH) if is_pair else t
    out = {}
    c, s = _rope_tab(pos, 500000.0, 16)
    cosA = np.ones((128, N), np.float32)
    sinA = np.zeros((128, N), np.float32)
    swA = np.zeros((128, 128), np.float32)
    for u in range(2):
        for i in range(8):
            cosA[u * 64 + i] = c[i]; cosA[u * 64 + 8 + i] = c[i]
            sinA[u * 64 + i] = -s[i]; sinA[u * 64 + 8 + i] = s[i]
            swA[u * 64 + 8 + i, u * 64 + i] = 1.0
            swA[u * 64 + i, u * 64 + 8 + i] = 1.0
    out.update(tA_cos=cosA, tA_sin=sinA, swA=swA)
    rows = pos // 64
    cols = pos % 64
    cr, sr = _rope_tab(rows, 10000.0, 32)
    cc, sc = _rope_tab(cols, 10000.0, 32)
    cosB = np.ones((128, N), np.float32)
    sinB = np.zeros((128, N), np.float32)
    swB = np.zeros((128, 128), np.float32)
    for u in range(2):
        for off, (c_, s_) in ((0, (cr, sr)), (32, (cc, sc))):
            for i in range(16):
                a, b = u * 64 + off + i, u * 64 + off + 16 + i
                cosB[a] = c_[i]; cosB[b] = c_[i]
                sinB[a] = -s_[i]; sinB[b] = s_[i]
                swB[b, a] = 1.0
                swB[a, b] = 1.0
    out.update(tB_cos=cosB, tB_sin=sinB, swB=swB)
    c, s = _rope_tab(pos, 500000.0, 32)
    cosD = np.ones((96, N), np.float32)
    sinD = np.zeros((96, N), np.float32)
    swD = np.zeros((96, 96), np.float32)
    cosK = np.ones((32, N), np.float32)
    sinK = np.zeros((32, N), np.float32)
    swK = np.zeros((32, 32), np.float32)
    for i in range(16):
        cosD[64 + i] = c[i]; cosD[80 + i] = c[i]
        sinD[64 + i] = -s[i]; sinD[80 + i] = s[i]
        swD[80 + i, 64 + i] = 1.0
        swD[64 + i, 80 + i] = 1.0
        cosK[i] = c[i]; cosK[16 + i] = c[i]
        sinK[i] = -s[i]; sinK[16 + i] = s[i]
        swK[16 + i, i] = 1.0
        swK[i, 16 + i] = 1.0
    out.update(tD_cos=cosD, tD_sin=sinD, swD=swD, tDk_cos=cosK, tDk_sin=sinK, swDk=swK)
    kk = np.arange(128)[:, None]
    qq = np.arange(128)[None, :]
    mprev = (kk >= qq + 64).astype(np.float32)
    mcur = (np.abs(qq - kk) <= 64).astype(np.float32)
    mnext = (kk <= qq - 64).astype(np.float32)
    x = 0.0 if is_pair else 1.0
    out["maskA"] = np.ascontiguousarray(np.stack([mprev, mcur, mnext, mprev * x, mnext * x], axis=1))
    xb = np.zeros((128, 2), np.float32)
    xb[:, 1] = -30000.0 if is_pair else 0.0
    out["xbias"] = xb
    cst = np.zeros((128, 3, 128), np.float32)
    cst[:, 0, :] = np.eye(128, dtype=np.float32)
    cst[:, 1, :] = 1.0
    cst[0:64, 2, 0:64] = 1.0
    cst[64:128, 2, 64:128] = 1.0
    out["cst"] = cst
    return out


_PROG_CACHE = {}


def run_slots(slots, weights, NH, depth=DEPTH):
    key = (NH, depth)
    if key not in _PROG_CACHE:
        _PROG_CACHE[key] = Prog(NH, depth)
    prog = _PROG_CACHE[key]
    tabs = {True: host_tables(NH, True), False: host_tables(NH, False)}
    in_maps = []
    for sl in slots:
        mp = {"x_slot": np.ascontiguousarray(sl["x"], dtype=np.float32),
              "mem2": np.ascontiguousarray(sl["mem"], dtype=np.float32)}
        for n in WNAMES:
            mp[n] = weights[n]
        for n in GNAMES:
            mp[n] = weights[n]
        mp.update(tabs[bool(sl["pair"])])
        in_maps.append(mp)
    import os
    ncr = int(os.environ.get("MK_NCORES", "8"))
    res = run_bass_kernel_spmd(prog.nc, in_maps[:ncr], core_ids=list(range(ncr)))
    out = [r["y_slot"] for r in res.results]
    while len(out) < 8:
        out.append(out[-1])
    return out


def kernel(**inputs):
    NH = 4096
    xp = np.asarray(inputs["x_prompt"], dtype=np.float32)
    xs = np.asarray(inputs["x_sample"], dtype=np.float32)
    mp = np.asarray(inputs["mem_prompt"], dtype=np.float32)
    ms = np.asarray(inputs["mem_sample"], dtype=np.float32)
    weights = {n: np.ascontiguousarray(np.asarray(inputs[n], dtype=np.float32)) for n in list(WNAMES) + list(GNAMES)}
    slots = []
    for b in range(2):
        slots.append({"x": xp[b], "mem": np.stack([mp[b], mp[b]]), "pair": False})
    for j in range(4):
        slots.append({"x": xs[2 * j:2 * j + 2].reshape(2 * NH, D), "mem": ms[2 * j:2 * j + 2], "pair": True})
    for j in range(2):
        slots.append(slots[2 + j])
    ys = run_slots(slots, weights, NH)
    y_prompt = np.stack([ys[0], ys[1]]).astype(np.float32)
    y_sample = np.concatenate([ys[2 + j].reshape(2, NH, D) for j in range(4)], axis=0).astype(np.float32)
    return (y_prompt, y_sample)
```

```python
import math
import contextlib
import numpy as np
import concourse.bass as bass
import concourse.mybir as mybir
from concourse.bass_utils import run_bass_kernel_spmd

F32 = mybir.dt.float32
BF16 = mybir.dt.bfloat16
AF = mybir.ActivationFunctionType
ALU = mybir.AluOpType

D = 1024
KC = 8
ST = 512
DEPTH = 4
EPS = 1e-6
DMA_RING = 8


class Buf:
    __slots__ = ("last_w", "readers", "excl")

    def __init__(self, excl=False):
        self.last_w = None
        self.readers = {}
        self.excl = excl


class Op:
    __slots__ = ("q", "fn", "kind", "deps", "signal", "sigval", "slot", "rnd", "phase")


class Sched:
    ENGS = ("pe", "act", "dve", "pool", "sp")

    def __init__(self, nc, st):
        self.nc = nc
        self.ops = {q: [] for q in self.ENGS}
        self.ndma = {q: 0 for q in self.ENGS}
        self.sigcnt = {q: 0 for q in self.ENGS}
        self.phase = 0
        self.csem = {q: st.enter_context(nc.semaphore("c_" + q)) for q in self.ENGS}
        self.dsem = {q: [st.enter_context(nc.semaphore("d_%s_%d" % (q, i))) for i in range(DMA_RING)]
                     for q in ("sp", "pool")}
        self.waited = {q: {} for q in self.ENGS}

    def op(self, q, fn, reads=(), writes=(), kind="c"):
        o = Op()
        o.q = q
        o.fn = fn
        o.kind = kind
        o.signal = False
        o.sigval = 0
        o.phase = self.phase
        deps = {}
        xr = [b for b in reads if b.excl]
        if xr:
            reads = [b for b in reads if not b.excl]
            writes = list(writes) + xr
        for b in reads:
            w = b.last_w
            if w is not None:
                deps[id(w)] = w
        for b in writes:
            w = b.last_w
            if w is not None:
                deps[id(w)] = w
            for r in b.readers.values():
                deps[id(r)] = r
        for b in reads:
            if kind == "c":
                b.readers[q] = o
            else:
                b.readers[("d", q, self.ndma[q] % (4 * DMA_RING))] = o
        for b in writes:
            b.last_w = o
            b.readers = {}
        dl = []
        for d in deps.values():
            if d is o or d.phase != self.phase:
                continue
            if d.kind == "c":
                if d.q == q and q == "pe" and kind == "c":
                    continue
                d.signal = True
            dl.append(d)
        o.deps = dl
        if kind == "d":
            i = self.ndma[q]
            self.ndma[q] = i + 1
            o.slot = i % DMA_RING
            o.rnd = i // DMA_RING
        self.ops[q].append(o)
        return o

    def emit_phase(self):
        nc = self.nc
        for q in self.ENGS:
            c = self.sigcnt[q]
            for o in self.ops[q]:
                if o.kind == "c" and o.signal:
                    c += 1
                    o.sigval = c
            self.sigcnt[q] = c
        with nc.Block() as block:
            handles = {"pe": block.tensor, "act": block.scalar, "dve": block.vector,
                       "pool": block.gpsimd, "sp": block.sync}
            for q in self.ENGS:
                ops = self.ops[q]
                if not ops:
                    continue

                def body(eng, q=q, ops=ops):
                    waited = self.waited[q]

                    def wait(sem, key, val):
                        if waited.get(key, 0) >= val:
                            return
                        waited[key] = val
                        eng.wait_ge(sem, val)

                    for o in ops:
                        for d in o.deps:
                            if d.kind == "c":
                                wait(self.csem[d.q], ("c", d.q), d.sigval)
                            else:
                                wait(self.dsem[d.q][d.slot], ("d", d.q, d.slot), 16 * (d.rnd + 1))
                        if o.kind == "d":
                            if o.rnd > 0:
                                wait(self.dsem[q][o.slot], ("d", q, o.slot), 16 * o.rnd)
                            ins = o.fn(eng)
                            ins.then_inc(self.dsem[q][o.slot], 16)
                        else:
                            ins = o.fn(eng)
                            if o.signal:
                                ins.then_inc(self.csem[q], 1)
                    n = self.ndma[q]
                    if q in self.dsem and n > 0:
                        for s in range(min(DMA_RING, n)):
                            cnt = (n - 1 - s) // DMA_RING + 1
                            wait(self.dsem[q][s], ("d", q, s), 16 * cnt)

                handles[q](body)
        self.ops = {q: [] for q in self.ENGS}
        self.phase += 1


class Tile:
    def __init__(self, t, nb=1):
        self.t = t
        self.b = [Buf() for _ in range(nb)]

    @property
    def all(self):
        return self.b


WNAMES = ["w_xq", "w_xkv", "w_xo", "w_mlp_in", "w_mlp_out", "a_w_in", "a_w_out", "b_w_in", "b_w_out",
          "c_w_in", "c_w_out", "d_w_in", "d_w_uq", "d_w_ukv", "d_w_out"]
WSHAPES = {"w_xq": [4, 1024, 512], "w_xkv": [4, 1024, 1024], "w_xo": [4, 512, 1024],
           "w_mlp_in": [4, 1024, 4096], "w_mlp_out": [4, 4096, 1024], "a_w_in": [1, 1024, 4608],
           "a_w_out": [1, 512, 1024], "b_w_in": [1, 1024, 1536], "b_w_out": [1, 1024, 1024],
           "c_w_in": [1, 1024, 3072], "c_w_out": [1, 1024, 1024], "d_w_in": [1, 1024, 672],
           "d_w_uq": [1, 384, 1536], "d_w_ukv": [1, 256, 2048], "d_w_out": [1, 1024, 1024]}
GNAMES = {"norm_mix": [4, 1024], "norm_x": [4, 1024], "norm_mem": [4, 1024], "norm_mlp": [4, 1024],
          "final_norm": [1024], "b_q_norm": [1, 64], "b_k_norm": [1, 64], "c_lambda_q1": [1, 64],
          "c_lambda_k1": [1, 64], "c_lambda_q2": [1, 64], "c_lambda_k2": [1, 64], "c_sub_norm": [1, 128],
          "d_q_norm": [1, 384], "d_kv_norm": [1, 256]}
A_DILS = (1, 4, 16)


class Prog:
    def __init__(self, NH, depth=DEPTH):
        self.NH = NH
        self.N = 2 * NH
        self.NST = self.N // ST
        self.NT = self.N // 128
        self.depth = depth
        self.nc = bass.Bass("TRN2", target_bir_lowering=False)
        self.build()

    def dram_in(self, name, shape, dt=F32):
        return self.nc.dram_tensor(name, list(shape), dt, kind="ExternalInput").ap()

    def dram_tmp(self, name, shape, dt):
        return self.nc.dram_tensor(name, list(shape), dt, kind="Internal").ap()

    def sb(self, st, name, shape, dt, nb=1):
        self._uid = getattr(self, "_uid", 0) + 1
        return Tile(st.enter_context(self.nc.sbuf_tensor("s%d_%s" % (self._uid, name), list(shape), dt)), nb)

    def op(self, *a, **k):
        return self.S.op(*a, **k)

    def load(self, out_ap, in_ap, reads, writes, q="sp", slow=False):
        if slow:
            self.op(q, lambda e: e.dma_start(out=out_ap, in_=in_ap, allow_slow_non_contiguous=True),
                    reads=reads, writes=writes, kind="d")
        else:
            self.op(q, lambda e: e.dma_start(out=out_ap, in_=in_ap), reads=reads, writes=writes, kind="d")

    def mm(self, out_ap, lhsT, rhs, start, stop, reads, writes):
        self.op("pe", lambda e: e.matmul(out_ap, lhsT=lhsT, rhs=rhs, start=start, stop=stop),
                reads=reads, writes=writes)

    def tr(self, out_ap, in_ap, ident, reads, writes):
        self.op("pe", lambda e: e.transpose(out=out_ap, in_=in_ap, identity=ident), reads=reads, writes=writes)

    def act(self, out_ap, in_ap, func, reads, writes, scale=None, bias=None, accum=None):
        kw = {}
        if scale is not None:
            kw["scale"] = scale
        if bias is not None:
            kw["bias"] = bias
        if accum is not None:
            kw["accum_out"] = accum
        self.op("act", lambda e: e.activation(out=out_ap, in_=in_ap, func=func, **kw), reads=reads, writes=writes)

    def tt(self, out_ap, in0, in1, op, reads, writes, q="dve"):
        self.op(q, lambda e: e.tensor_tensor(out=out_ap, in0=in0, in1=in1, op=op), reads=reads, writes=writes)

    def stt(self, out_ap, in0, scalar, in1, op0, op1, reads, writes, q="dve"):
        self.op(q, lambda e: e.scalar_tensor_tensor(out=out_ap, in0=in0, scalar=scalar, in1=in1, op0=op0, op1=op1),
                reads=reads, writes=writes)

    def ts(self, out_ap, in0, s1, s2, op0, op1, reads, writes, q="dve"):
        self.op(q, lambda e: e.tensor_scalar(out=out_ap, in0=in0, scalar1=s1, scalar2=s2, op0=op0, op1=op1),
                reads=reads, writes=writes)

    def cp(self, out_ap, in_ap, reads, writes, q="dve"):
        if q == "act":
            self.act(out_ap, in_ap, AF.Copy, reads, writes)
        else:
            self.op(q, lambda e: e.tensor_copy(out=out_ap, in_=in_ap), reads=reads, writes=writes)

    def recip(self, out_ap, in_ap, reads, writes):
        self.op("dve", lambda e: e.reciprocal(out=out_ap, in_=in_ap), reads=reads, writes=writes)

    def memset(self, ap, val, writes, q="dve"):
        self.op(q, lambda e: e.memset(ap, val), writes=writes)

    def build(self):
        nc = self.nc
        N, NH, NST, NT = self.N, self.NH, self.NST, self.NT
        self.x_in = self.dram_in("x_slot", [N, D])
        self.mem_in = self.dram_in("mem2", [2, 256, D])
        self.w_in = {n: self.dram_in(n, WSHAPES[n]) for n in WNAMES}
        self.g_in = {n: self.dram_in(n, GNAMES[n]) for n in GNAMES}
        self.tabs = {}
        for n, r in (("tA", 128), ("tB", 128), ("tD", 96), ("tDk", 32)):
            self.tabs[n] = (self.dram_in(n + "_cos", [r, N]), self.dram_in(n + "_sin", [r, N]))
        self.sw_in = {n: self.dram_in(n, [r, r]) for n, r in (("swA", 128), ("swB", 128), ("swD", 96), ("swDk", 32))}
        self.maskA_in = self.dram_in("maskA", [128, 5, 128])
        self.xbias_in = self.dram_in("xbias", [128, 2])
        self.cst_in = self.dram_in("cst", [128, 3, 128])
        self.y_out = nc.dram_tensor("y_slot", [N, D], F32, kind="ExternalOutput").ap()
        self.w_bf = {n: self.dram_tmp(n + "_bf", WSHAPES[n], BF16) for n in WNAMES}
        self.xT = self.dram_tmp("xT", [KC, 128, N], F32)
        self.QT = self.dram_tmp("QT", [16, 128, N], BF16)
        self.KT = self.dram_tmp("KT", [16, 128, N], BF16)
        self.VT = self.dram_tmp("VT", [16, 128, N], BF16)
        self.KR = self.dram_tmp("KR", [32, N], BF16)
        self.AT = self.dram_tmp("AT", [16, 128, N], BF16)

        with contextlib.ExitStack() as gst:
            self.S = Sched(nc, gst)
            self.psbig = gst.enter_context(nc.psum_tensor("psbig", [128, 8 * 512], F32))
            self.ps = [Tile(self.psbig[:, i * 512:(i + 1) * 512]) for i in range(8)]
            self.psb = Tile(self.psbig[:, 7 * 512:8 * 512].bitcast(BF16))
            self.psb.b = self.ps[7].b
            for t in self.ps:
                t.b[0].excl = True
            c32 = self.sb(gst, "c32", [128, 3, 128], F32)
            self.load(c32.t[:], self.cst_in, [], c32.b)
            self.idf = c32.t[:, 0, :]
            self.ones32 = c32.t[:, 1, :]
            cbf = self.sb(gst, "cbf", [128, 3, 128], BF16)
            self.cp(cbf.t[:], c32.t[:], c32.b, cbf.b)
            self.c32, self.cbf = c32, cbf
            self.idb = cbf.t[:, 0, :]
            self.onesb = cbf.t[:, 1, :]
            self.blkb = cbf.t[:, 2, :]
            self.sw = {}
            for n, r in (("swA", 128), ("swB", 128), ("swD", 96), ("swDk", 32)):
                t32 = self.sb(gst, n + "32", [r, r], F32)
                tb = self.sb(gst, n + "b", [r, r], BF16)
                self.load(t32.t[:], self.sw_in[n], [], t32.b)
                self.cp(tb.t[:], t32.t[:], t32.b, tb.b)
                self.sw[n] = tb
            self.gn = {}
            for n in ("norm_mix", "norm_x", "norm_mem", "norm_mlp"):
                t = self.sb(gst, "g_" + n, [128, 4, KC], F32)
                for l in range(4):
                    self.load(t.t[:, l, :], self.g_in[n][l].rearrange("(kc p) -> p kc", p=128), [], t.b, slow=True)
                self.gn[n] = t
            t = self.sb(gst, "g_final", [128, KC], F32)
            self.load(t.t[:], self.g_in["final_norm"].rearrange("(kc p) -> p kc", p=128), [], t.b, slow=True)
            self.gn["final_norm"] = t
            t = self.sb(gst, "g_dq", [128, 3], F32)
            self.load(t.t[:], self.g_in["d_q_norm"][0].rearrange("(kc p) -> p kc", p=128), [], t.b, slow=True)
            self.gn["d_q_norm"] = t
            t = self.sb(gst, "g_dkv", [128, 2], F32)
            self.load(t.t[:], self.g_in["d_kv_norm"][0].rearrange("(kc p) -> p kc", p=128), [], t.b, slow=True)
            self.gn["d_kv_norm"] = t
            t = self.sb(gst, "g_bqk", [128, 2], F32)
            for j, n in enumerate(("b_q_norm", "b_k_norm")):
                for hh in range(2):
                    self.load(t.t[hh * 64:(hh + 1) * 64, j:j + 1], self.g_in[n][0].rearrange("(p o) -> p o", o=1),
                              [], t.b, slow=True)
            self.gn["b_qk"] = t
            t = self.sb(gst, "g_csub", [128, 1], F32)
            self.load(t.t[:], self.g_in["c_sub_norm"][0].rearrange("(p o) -> p o", o=1), [], t.b, slow=True)
            self.gn["c_sub"] = t
            mA32 = self.sb(gst, "mA32", [128, 5, 128], F32)
            self.load(mA32.t[:], self.maskA_in, [], mA32.b)
            self.maskA = self.sb(gst, "maskA", [128, 5, 128], BF16)
            self.cp(self.maskA.t[:], mA32.t[:], mA32.b, self.maskA.b)
            self.xbias = self.sb(gst, "xbias", [128, 2], F32)
            self.load(self.xbias.t[:], self.xbias_in, [], self.xbias.b)
            self.lam_init = 0.8 - 0.6 * math.exp(-0.3 * 2)
            lv = self.sb(gst, "lamv", [128, 4, 64], F32)
            for j, n in enumerate(("c_lambda_q1", "c_lambda_k1", "c_lambda_q2", "c_lambda_k2")):
                self.load(lv.t[:, j, :], self.g_in[n][0].partition_broadcast(128), [], lv.b)
            lp = self.sb(gst, "lamp", [128, 2, 64], F32)
            ls = self.sb(gst, "lams", [128, 4], F32)
            self.tt(lp.t[:, 0, :], lv.t[:, 0, :], lv.t[:, 1, :], ALU.mult, lv.b, lp.b)
            self.tt(lp.t[:, 1, :], lv.t[:, 2, :], lv.t[:, 3, :], ALU.mult, lv.b, lp.b)
            self.op("dve", lambda e: e.tensor_reduce(out=ls.t[:, 0:2], in_=lp.t[:], axis=mybir.AxisListType.X, op=ALU.add),
                    reads=lp.b, writes=ls.b)
            self.act(ls.t[:, 2:4], ls.t[:, 0:2], AF.Exp, ls.b, ls.b)
            self.neglam = self.sb(gst, "neglam", [128, 1], F32)
            self.stt(self.neglam.t[:], ls.t[:, 3:4], -self.lam_init, ls.t[:, 2:3], ALU.add, ALU.subtract, ls.b, self.neglam.b)
            self.csubg = self.sb(gst, "csubg", [128, 1], F32)
            self.act(self.csubg.t[:], self.gn["c_sub"].t[:], AF.Copy, self.gn["c_sub"].b, self.csubg.b,
                     scale=1.0 - self.lam_init)
            self.wB = {n: Buf() for n in WNAMES}
            for n in WNAMES:
                L, R, C = WSHAPES[n]
                rows = max(128, (min(R, (1 << 20) // C) // 16) * 16)
                for l in range(L):
                    for r0 in range(0, R, rows):
                        r1 = min(R, r0 + rows)
                        self.load(self.w_bf[n][l, r0:r1, :], self.w_in[n][l, r0:r1, :], [], [self.wB[n]], q="pool")
            self.S.emit_phase()

            import os
            maxph = int(os.environ.get("MK_MAXPH", "999"))
            steps = [self.phase_p0]
            for l in range(self.depth):
                m = l % 4
                steps.append(lambda l=l, m=m: self.phase_p1(l, m))
                steps.append((lambda: self.phase_attn_A()) if m == 0 else (lambda m=m: self.phase_attn(m)))
                steps.append(lambda l=l, m=m: self.phase_p2b(l, m))
            if self.depth < DEPTH:
                steps.append(self.phase_final_only)
            for i, f in enumerate(steps):
                if i >= maxph:
                    break
                f()

    def phase_p0(self):
        with contextlib.ExitStack() as st:
            xin = [self.sb(st, "p0_xin%d" % i, [128, 4, D], F32) for i in range(2)]
            xo = [self.sb(st, "p0_xo%d" % i, [128, KC, ST], F32, nb=KC) for i in range(2)]
            for s in range(self.NST):
                a, o = xin[s % 2], xo[s % 2]
                self.load(a.t[:], self.x_in[s * ST:(s + 1) * ST, :].rearrange("(tt p) d -> p tt d", p=128), [], a.b)
                for kc in range(KC):
                    bank = self.ps[kc % 4]
                    for tt_ in range(4):
                        self.tr(bank.t[:, tt_ * 128:(tt_ + 1) * 128], a.t[:, tt_, kc * 128:(kc + 1) * 128], self.idf,
                                a.b + self.c32.b, bank.b)
                    self.cp(o.t[:, kc, :], bank.t[:], bank.b, [o.b[kc]], q=("act" if kc % 2 else "dve"))
                self.load(self.xT[:, :, s * ST:(s + 1) * ST].rearrange("kc p t -> p kc t"), o.t[:], o.b, [], q="pool")
            self.S.emit_phase()

    def rms(self, src, nkc, nfeat, gain_ap_fn, out, sq, rstd, ssbank, ones_lhsT=None, ones_reads=None, cols=ST):
        if ones_lhsT is None:
            ones_lhsT, ones_reads = self.onesb, self.cbf.b
        for kc in range(nkc):
            self.act(sq.t[:, kc, :cols], src.t[:, kc, :cols], AF.Square, src.b if len(src.b) == 1 else [src.b[kc]],
                     sq.b if len(sq.b) == 1 else [sq.b[kc]])
        for kc in range(nkc):
            self.mm(ssbank.t[:, :cols], ones_lhsT, sq.t[:, kc, :cols], kc == 0, kc == nkc - 1,
                    (sq.b if len(sq.b) == 1 else [sq.b[kc]]) + ones_reads, ssbank.b)
        self.act(rstd.t[:, :cols], ssbank.t[:, :cols], AF.Sqrt, ssbank.b + self.epsb.b, rstd.b, scale=1.0 / nfeat, bias=self.epsb.t[:])
        self.recip(rstd.t[:, :cols], rstd.t[:, :cols], rstd.b, rstd.b)
        for kc in range(nkc):
            self.stt(out.t[:, kc, :cols], src.t[:, kc, :cols], gain_ap_fn(kc), rstd.t[:, :cols], ALU.mult, ALU.mult,
                     (src.b if len(src.b) == 1 else [src.b[kc]]) + rstd.b + self.gall,
                     out.b if len(out.b) == 1 else [out.b[kc]])

    def layer(self, l):
        m = l % 4
        self.phase_p1(l, m)
        if m == 0:
            self.phase_attn_A()
        else:
            self.phase_attn(m)
        self.phase_p2b(l, m)

    def common_tiles(self, st):
        self.epsb = self.sb(st, "epsb", [128, 1], F32)
        self.memset(self.epsb.t[:], EPS, self.epsb.b)
        self.gall = []
        for t in self.gn.values():
            self.gall += t.b

    def rope(self, st_tiles, psrc, rows, swname, cosT, sinT, s_idx, out_ap, out_b, tag):
        import os
        if os.environ.get("MK_NOROPE"):
            self.cp(out_ap, psrc.t[:rows, :], psrc.b, out_b, q="act")
            return
        xb, t1, t2, swbank = st_tiles
        lvl = int(os.environ.get("MK_ROPE", "9"))
        sw = self.sw[swname]
        self.cp(xb.t[:rows, :], psrc.t[:rows, :], psrc.b, xb.b, q="act")
        self.mm(swbank.t[:rows, :], sw.t[:], xb.t[:rows, :], True, True, xb.b + sw.b, swbank.b)
        if lvl == 1:
            self.cp(out_ap, swbank.t[:rows, :], swbank.b + psrc.b, out_b, q="act")
            return
        self.tt(t1.t[:rows, :], psrc.t[:rows, :], cosT.t[:rows, :], ALU.mult, psrc.b + cosT.b, t1.b)
        self.tt(t2.t[:rows, :], swbank.t[:rows, :], sinT.t[:rows, :], ALU.mult, swbank.b + sinT.b, t2.b)
        if lvl == 2:
            self.cp(out_ap, t2.t[:rows, :], t1.b + t2.b, out_b, q="act")
            return
        if lvl == 3:
            self.tt(out_ap, t1.t[:rows, :], t2.t[:rows, :], ALU.add, t1.b + t2.b, out_b, q="dve")
            return
        self.tt(out_ap, t1.t[:rows, :], t2.t[:rows, :], ALU.add, t1.b + t2.b, out_b, q="pool")

    def phase_p1(self, l, m):
        N, NST = self.N, self.NST
        wname = ("a_w_in", "b_w_in", "c_w_in", "d_w_in")[m]
        C = WSHAPES[wname][2]
        with contextlib.ExitStack() as st:
            self.common_tiles(st)
            w = self.sb(st, "p1_w", [128, KC, C], BF16)
            self.load(w.t[:], self.w_bf[wname][0].rearrange("(kc p) n -> p kc n", p=128), [self.wB[wname]], w.b)
            if m == 3:
                wuq = self.sb(st, "p1_wuq", [128, 3, 1536], BF16)
                self.load(wuq.t[:], self.w_bf["d_w_uq"][0].rearrange("(kc p) n -> p kc n", p=128), [self.wB["d_w_uq"]], wuq.b)
                wukv = self.sb(st, "p1_wukv", [128, 2, 2048], BF16)
                self.load(wukv.t[:], self.w_bf["d_w_ukv"][0].rearrange("(kc p) n -> p kc n", p=128), [self.wB["d_w_ukv"]], wukv.b)
            xs = [self.sb(st, "p1_x%d" % i, [128, KC, ST], F32) for i in range(2)]
            hT = self.sb(st, "p1_h", [128, KC, ST], BF16)
            sq = self.sb(st, "p1_sq", [128, KC, ST], BF16)
            rstd = self.sb(st, "p1_rstd", [128, ST], F32)
            tabn = ("tA", "tB", "tA", "tD")[m]
            trows = 96 if m == 3 else 128
            cosT = [self.sb(st, "p1_cos%d" % i, [trows, ST], F32) for i in range(2)]
            sinT = [self.sb(st, "p1_sin%d" % i, [trows, ST], F32) for i in range(2)]
            if m == 3:
                cosK = [self.sb(st, "p1_cosk%d" % i, [32, ST], F32) for i in range(2)]
                sinK = [self.sb(st, "p1_sink%d" % i, [32, ST], F32) for i in range(2)]
                cq = self.sb(st, "p1_cq", [128, 5, ST], F32)
                cqn = self.sb(st, "p1_cqn", [128, 5, ST], BF16)
                sq2 = self.sb(st, "p1_sq2", [128, 3, ST], BF16)
                rstd2 = self.sb(st, "p1_rstd2", [128, ST], F32)
            xb = self.sb(st, "p1_xb", [128, ST], BF16)
            t1 = self.sb(st, "p1_t1", [128, ST], F32)
            t2 = self.sb(st, "p1_t2", [128, ST], F32)
            qn = self.sb(st, "p1_qn", [128, ST], F32)
            qsq = self.sb(st, "p1_qsq", [128, 1, ST], BF16)
            NO = 4
            outs = [self.sb(st, "p1_o%d" % i, [128, ST], BF16) for i in range(NO)]
            self.p1_oi = 0
            swbank = self.ps[5]
            ssbank = self.ps[4]
            rope_tiles = (xb, t1, t2, swbank)

            def ld(s):
                x = xs[s % 2]
                self.load(x.t[:], self.xT[:, :, s * ST:(s + 1) * ST].rearrange("kc p t -> p kc t"), [], x.b)
                tc_, ts_ = self.tabs[tabn]
                self.load(cosT[s % 2].t[:], tc_[:, s * ST:(s + 1) * ST], [], cosT[s % 2].b)
                self.load(sinT[s % 2].t[:], ts_[:, s * ST:(s + 1) * ST], [], sinT[s % 2].b)
                if m == 3:
                    tc_, ts_ = self.tabs["tDk"]
                    self.load(cosK[s % 2].t[:], tc_[:, s * ST:(s + 1) * ST], [], cosK[s % 2].b)
                    self.load(sinK[s % 2].t[:], ts_[:, s * ST:(s + 1) * ST], [], sinK[s % 2].b)

            def nxt_out():
                o = outs[self.p1_oi % NO]
                self.p1_oi += 1
                return o

            def proj(bank, wt, nk, c0, cn, src):
                for kc in range(nk):
                    self.mm(bank.t[:cn, :], wt.t[:, kc, c0:c0 + cn], src.t[:, kc, :], kc == 0, kc == nk - 1,
                            wt.b + src.b, bank.b)

            def store(dst_ap, o, rows, p0=0):
                self.load(dst_ap, o.t[p0:p0 + rows, :], o.b, [], q="pool")

            ld(0)
            for s in range(NST):
                if s + 1 < NST:
                    ld(s + 1)
                x = xs[s % 2]
                cs, sn = cosT[s % 2], sinT[s % 2]
                sl = slice(s * ST, (s + 1) * ST)
                gm = self.gn["norm_mix"]
                self.rms(x, KC, D, lambda kc: gm.t[:, l, kc:kc + 1], hT, sq, rstd, ssbank)
                nb = 0
                if m == 0:
                    for c in range(36):
                        g, typ, hp = c // 12, (c % 12) // 4, c % 4
                        bank = self.ps[nb % 4]; nb += 1
                        proj(bank, w, KC, c * 128, 128, hT)
                        o = nxt_out()
                        if typ < 2:
                            self.rope(rope_tiles, bank, 128, "swA", cs, sn, s, o.t[:], o.b, "a")
                        else:
                            self.cp(o.t[:], bank.t[:], bank.b, o.b, q="act")
                        dst = (self.QT, self.KT, self.VT)[typ]
                        store(dst[g * 4 + hp, :, sl], o, 128)
                elif m == 1:
                    gq = self.gn["b_qk"]
                    for c in range(12):
                        bank = self.ps[nb % 4]; nb += 1
                        proj(bank, w, KC, c * 128, 128, hT)
                        o = nxt_out()
                        if c < 10:
                            self.cp(qn.t[:], bank.t[:], bank.b, qn.b, q="act")
                            qt = Tile(qn.t[:].rearrange("p (o t) -> p o t", o=1))
                            qt.b = qn.b
                            gcol = 0 if c < 8 else 1
                            nbk = self.ps[6]
                            self.rms(qt, 1, 64, lambda kc: gq.t[:, gcol:gcol + 1], qt, qsq, rstd, ssbank,
                                     ones_lhsT=self.blkb, ones_reads=self.cbf.b)
                            self.rope(rope_tiles, qn, 128, "swB", cs, sn, s, o.t[:], o.b, "b")
                            dst = self.QT[c] if c < 8 else self.KT[c - 8]
                        else:
                            self.cp(o.t[:], bank.t[:], bank.b, o.b, q="act")
                            dst = self.VT[c - 10]
                        store(dst[:, sl], o, 128)
                elif m == 2:
                    for c in range(24):
                        typ, hh = c // 8, c % 8
                        bank = self.ps[nb % 4]; nb += 1
                        proj(bank, w, KC, c * 128, 128, hT)
                        o = nxt_out()
                        if typ < 2:
                            self.rope(rope_tiles, bank, 128, "swA", cs, sn, s, o.t[:], o.b, "c")
                        else:
                            self.cp(o.t[:], bank.t[:], bank.b, o.b, q="act")
                        dst = (self.QT, self.KT, self.VT)[typ]
                        store(dst[hh, :, sl], o, 128)
                else:
                    for c in range(5):
                        bank = self.ps[nb % 4]; nb += 1
                        proj(bank, w, KC, c * 128, 128, hT)
                        self.cp(cq.t[:, c, :], bank.t[:], bank.b, cq.b, q="act")
                    bank = self.ps[nb % 4]; nb += 1
                    proj(bank, w, KC, 640, 32, hT)
                    o = nxt_out()
                    self.rope(rope_tiles, bank, 32, "swDk", cosK[s % 2], sinK[s % 2], s, o.t[:32, :], o.b, "dk")
                    store(self.KR[:, sl], o, 32)
                    gq_, gkv_ = self.gn["d_q_norm"], self.gn["d_kv_norm"]
                    cq_q = Tile(cq.t[:, 0:3, :]); cq_q.b = cq.b
                    cqn_q = Tile(cqn.t[:, 0:3, :]); cqn_q.b = cqn.b
                    self.rms(cq_q, 3, 384, lambda kc: gq_.t[:, kc:kc + 1], cqn_q, sq2, rstd2, ssbank)
                    cq_k = Tile(cq.t[:, 3:5, :]); cq_k.b = cq.b
                    cqn_k = Tile(cqn.t[:, 3:5, :]); cqn_k.b = cqn.b
                    self.rms(cq_k, 2, 256, lambda kc: gkv_.t[:, kc:kc + 1], cqn_k, sq2, rstd2, ssbank)
                    for hh in range(16):
                        bank = self.ps[nb % 4]; nb += 1
                        proj(bank, wuq, 3, hh * 96, 96, cqn_q)
                        o = nxt_out()
                        self.rope(rope_tiles, bank, 96, "swD", cs, sn, s, o.t[:96, :], o.b, "d")
                        store(self.QT[hh, 0:96, sl], o, 96)
                    for hh in range(16):
                        bank = self.ps[nb % 4]; nb += 1
                        proj(bank, wukv, 2, hh * 128, 128, cqn_k)
                        o = nxt_out()
                        self.cp(o.t[:], bank.t[:], bank.b, o.b, q="act")
                        store(self.KT[hh, 0:64, sl], o, 64, 0)
                        store(self.VT[hh, 0:64, sl], o, 64, 64)
            self.S.emit_phase()

    def pspair(self, i):
        t = Tile(self.psbig[:, i * 512:(i + 2) * 512])
        t.b = self.ps[i].b + self.ps[i + 1].b
        return t

    def phase_attn(self, m):
        N, NST, NT, NH = self.N, self.NST, self.NT, self.NH
        diff = (m == 2)
        if m == 1:
            nstream, dqk, dv, scale = 16, 64, 64, 64 ** -0.5
        elif m == 2:
            nstream, dqk, dv, scale = 8, 64, 128, 64 ** -0.5
        else:
            nstream, dqk, dv, scale = 16, 96, 64, 96 ** -0.5
        with contextlib.ExitStack() as st:
            self.common_tiles(st)
            ncomp = 2 if diff else 1
            dqp = 128 if dqk == 64 else dqk
            kts = [[self.sb(st, "at_k%d_%d" % (i, c), [dqp, N], BF16) for c in range(ncomp)] for i in range(2)]
            vts = [self.sb(st, "at_v%d" % i, [dv, N], BF16) for i in range(2)]
            dva = dv if diff else dv + 1
            vaug = [self.sb(st, "at_va%d" % i, [128, NT, dva], BF16) for i in range(2)]
            if not diff:
                for i in range(2):
                    self.cp(vaug[i].t[:, :, dv:dv + 1], self.onesb[:, 0:NT].rearrange("p (t o) -> p t o", o=1),
                            self.cbf.b, vaug[i].b)
            NQ = 3
            qs = [[self.sb(st, "at_q%d_%d" % (i, c), [dqp, ST], BF16) for c in range(ncomp)] for i in range(NQ)]
            if dqp != dqk:
                for grp_ in (kts, qs):
                    for row in grp_:
                        for t_ in row:
                            self.memset(t_.t[dqk:dqp, :], 0.0, t_.b)
            NP = 5 if diff else 4
            pts = [self.sb(st, "at_p%d" % i, [128, 2 * ST], BF16) for i in range(NP)]
            rec = self.sb(st, "at_rec", [128, ST], F32)
            rec2 = self.sb(st, "at_rec2", [128, ST], F32)
            bcs = self.sb(st, "at_bcs", [128, ST], F32)
            osb = [self.sb(st, "at_o%d" % i, [128, ST], BF16) for i in range(2)]
            sgrp = [self.pspair(0), self.pspair(2)] if diff else [self.pspair(0), self.pspair(2), self.pspair(4)]
            NSG = len(sgrp)
            if diff:
                o32 = self.sb(st, "at_o32", [128, 1, ST], F32)
                t32 = self.sb(st, "at_t32", [128, ST], F32)
                osq = self.sb(st, "at_osq", [128, 1, ST], BF16)
                rstd = self.sb(st, "at_rstd", [128, ST], F32)
                accs = [self.sb(st, "at_acc%d" % i, [128, 2 * ST], F32) for i in range(2)]
                tmps = [self.sb(st, "at_tmp%d" % i, [128, 2 * ST], BF16) for i in range(2)]
                obs = [self.ps[4], self.ps[5]]
                dbank = self.ps[6]
            units = [(h, s) for h in range(nstream) for s in range(NST)]
            NG = NT if diff else NT // 2
            steps = [(u, g) for u in range(len(units)) for g in range(NG)]

            def ld_head(h):
                i = h % 2
                for c in range(ncomp):
                    kt = kts[i][c]
                    if m == 1:
                        src = self.KT[h // 8, ((h // 4) % 2) * 64:((h // 4) % 2) * 64 + 64, :]
                        self.load(kt.t[0:64, :], src, [], kt.b)
                    elif m == 2:
                        self.load(kt.t[0:64, :], self.KT[h, c * 64:(c + 1) * 64, :], [], kt.b)
                    else:
                        self.load(kt.t[0:64, :], self.KT[h, 0:64, :], [], kt.b)
                        self.load(kt.t[64:96, :], self.KR[:, :], [], kt.b)
                vt = vts[i]
                if m == 1:
                    self.load(vt.t[:], self.VT[h // 8, ((h // 4) % 2) * 64:((h // 4) % 2) * 64 + 64, :], [], vt.b)
                elif m == 2:
                    self.load(vt.t[:], self.VT[h, :, :], [], vt.b)
                else:
                    self.load(vt.t[:], self.VT[h, 0:64, :], [], vt.b)

            def ld_q(u):
                h, s = units[u]
                sl = slice(s * ST, (s + 1) * ST)
                for c in range(ncomp):
                    q = qs[u % NQ][c]
                    if m == 1:
                        self.load(q.t[0:64, :], self.QT[h // 2, (h % 2) * 64:(h % 2) * 64 + 64, sl], [], q.b)
                    elif m == 2:
                        self.load(q.t[0:64, :], self.QT[h, c * 64:(c + 1) * 64, sl], [], q.b)
                    else:
                        self.load(q.t[:], self.QT[h, 0:96, sl], [], q.b)

            def prep_v(h):
                i = h % 2
                vt, va = vts[i], vaug[i]
                per = 1024 // dv
                for t0 in range(0, NT, per):
                    for j in range(per):
                        self.tr(self.psb.t[:, j * dv:(j + 1) * dv], vt.t[:, (t0 + j) * 128:(t0 + j + 1) * 128],
                                self.idb[0:dv, 0:dv], vt.b + self.cbf.b, self.psb.b)
                    self.cp(va.t[:, t0:t0 + per, 0:dv], self.psb.t[:, 0:per * dv].rearrange("p (t d) -> p t d", d=dv),
                            self.psb.b, va.b, q="dve")

            def stageA(i):
                u, g = steps[i]
                h, s = units[u]
                grp = sgrp[i % NSG]
                for j in range(2):
                    if diff:
                        t, c = g, j
                    else:
                        t, c = 2 * g + j, 0
                    kt = kts[h % 2][c]
                    q = qs[u % NQ][c]
                    self.mm(grp.t[:, j * ST:(j + 1) * ST], kt.t[:, t * 128:(t + 1) * 128], q.t[:], True, True,
                            kt.b + q.b, [grp.b[j]])

            def stageE(i):
                u, g = steps[i]
                h, s = units[u]
                grp = sgrp[i % NSG]
                p = pts[i % NP]
                t0 = g if diff else 2 * g
                qhalf = (s * ST) // NH
                if (t0 * 128) // NH == qhalf:
                    self.act(p.t[:], grp.t[:], AF.Exp, grp.b, p.b, scale=scale)
                else:
                    self.act(p.t[:], grp.t[:], AF.Exp, grp.b + self.xbias.b, p.b, scale=scale, bias=self.xbias.t[:, 1:2])

            def stageB(i):
                u, g = steps[i]
                h, s = units[u]
                p = pts[i % NP]
                va = vaug[h % 2]
                if not diff:
                    ob = self.ps[6]
                    for j in range(2):
                        t = 2 * g + j
                        self.mm(ob.t[0:dv + 1, :], va.t[:, t, :], p.t[:, j * ST:(j + 1) * ST], t == 0, t == NT - 1,
                                va.b + p.b, ob.b)
                else:
                    t = g
                    for c in range(2):
                        self.mm(obs[c].t[:], va.t[:, t, :], p.t[:, c * ST:(c + 1) * ST], t == 0, t == NT - 1,
                                va.b + p.b, obs[c].b)
                    if t % 2 == 1:
                        pprev = pts[(i - 1) % NP]
                        ai = (t // 2) % 2
                        tmp = tmps[ai]
                        self.tt(tmp.t[:], pprev.t[:], p.t[:], ALU.add, pprev.b + p.b, tmp.b)
                        eng = "pool" if ai else "dve"
                        if t // 2 < 2:
                            self.cp(accs[ai].t[:], tmp.t[:], tmp.b, accs[ai].b, q=eng)
                        else:
                            self.tt(accs[ai].t[:], accs[ai].t[:], tmp.t[:], ALU.add, tmp.b + accs[ai].b, accs[ai].b, q=eng)

            def epilogue(u):
                h, s = units[u]
                if not diff:
                    ob = self.ps[6]
                    self.recip(rec.t[64:65, :], ob.t[64:65, :], ob.b, rec.b)
                    bb = self.ps[7]
                    self.mm(bb.t[0:64, :], self.ones32[64:65, 0:64], rec.t[64:65, :], True, True, rec.b + self.c32.b, bb.b)
                    self.cp(bcs.t[0:64, :], bb.t[0:64, :], bb.b, bcs.b, q="act")
                    o = osb[u % 2]
                    self.tt(o.t[0:64, :], ob.t[0:64, :], bcs.t[0:64, :], ALU.mult, ob.b + bcs.b, o.b)
                    self.load(self.AT[h // 2, (h % 2) * 64:(h % 2) * 64 + 64, s * ST:(s + 1) * ST], o.t[0:64, :], o.b, [], q="pool")
                else:
                    self.tt(accs[0].t[:], accs[0].t[:], accs[1].t[:], ALU.add, accs[0].b + accs[1].b, accs[0].b)
                    for c, r_ in ((0, rec), (1, rec2)):
                        self.mm(dbank.t[:], self.ones32, accs[0].t[:, c * ST:(c + 1) * ST], True, True,
                                accs[0].b + self.c32.b, dbank.b)
                        self.recip(r_.t[:], dbank.t[:], dbank.b, r_.b)
                    self.tt(o32.t[:, 0, :], obs[0].t[:], rec.t[:], ALU.mult, obs[0].b + rec.b, o32.b)
                    self.tt(t32.t[:], obs[1].t[:], rec2.t[:], ALU.mult, obs[1].b + rec2.b, t32.b)
                    self.stt(o32.t[:, 0, :], t32.t[:], self.neglam.t[:], o32.t[:, 0, :], ALU.mult, ALU.add,
                             t32.b + o32.b + self.neglam.b, o32.b)
                    o = osb[u % 2]
                    ot = Tile(o.t[:].rearrange("p (o t) -> p o t", o=1)); ot.b = o.b
                    self.rms(o32, 1, 128, lambda kc: self.csubg.t[:], ot, osq, rstd, dbank)
                    self.load(self.AT[h, :, s * ST:(s + 1) * ST], o.t[:], o.b, [], q="pool")

            ld_head(0)
            ld_q(0)
            ld_q(1)
            LA = NSG - 1
            for i0 in range(LA):
                stageA(i0)
            for i, (u, g) in enumerate(steps):
                h, s = units[u]
                if g == 0:
                    if s == 0:
                        prep_v(h)
                        if h + 1 < nstream:
                            ld_head(h + 1)
                    if u + 2 < len(units):
                        ld_q(u + 2)
                if i + LA < len(steps):
                    stageA(i + LA)
                stageE(i)
                stageB(i)
                if g == NG - 1:
                    epilogue(u)
            self.S.emit_phase()

    def phase_attn_A(self):
        N, NT, NH, NST = self.N, self.NT, self.NH, self.NST
        scale = 64 ** -0.5
        with contextlib.ExitStack() as st:
            self.common_tiles(st)
            qkv = [[self.sb(st, "aa_%s%d" % (n, i), [128 if n != "v" else 64, N], BF16) for n in "qkv"] for i in range(2)]
            for i in range(2):
                for j in range(2):
                    self.memset(qkv[i][j].t[64:128, :], 0.0, qkv[i][j].b)
            vaug = [self.sb(st, "aa_va%d" % i, [128, NT, 65], BF16) for i in range(2)]
            for i in range(2):
                self.cp(vaug[i].t[:, :, 64:65], self.onesb[:, 0:NT].rearrange("p (t o) -> p t o", o=1), self.cbf.b, vaug[i].b)
            acc = self.sb(st, "aa_acc", [65, N], F32)
            NP = 3
            es = [self.sb(st, "aa_e%d" % i, [128, 3, 128], BF16) for i in range(NP)]
            pp = [self.sb(st, "aa_p%d" % i, [128, 3, 128], BF16) for i in range(NP)]
            rec = self.sb(st, "aa_rec", [65, ST], F32)
            osb = [self.sb(st, "aa_o%d" % i, [64, ST], BF16) for i in range(2)]
            accsync = Buf()
            combos = [(h, g) for h in range(8) for g in range(3)]

            def ld(ci):
                h, g = combos[ci]
                for j, src in enumerate((self.QT, self.KT, self.VT)):
                    t = qkv[ci % 2][j]
                    self.load(t.t[0:64, :], src[g * 4 + h // 2, (h % 2) * 64:(h % 2) * 64 + 64, :], [], t.b)

            def toks(d, s, r, i):
                start = s * NH + (128 * i) * d + r
                return slice(start, start + 127 * d + 1, d)

            ld(0)
            ui = 0
            for ci, (h, g) in enumerate(combos):
                if ci + 1 < len(combos):
                    ld(ci + 1)
                d = A_DILS[g]
                q, k, v = qkv[ci % 2]
                va = vaug[ci % 2]
                npp = NH // d // 128
                tiles = [(s, r, i) for s in range(2) for r in range(d) for i in range(npp)]
                tidx = {t: j for j, t in enumerate(tiles)}
                for t0 in range(0, NT, 16):
                    for j in range(16):
                        s, r, i = tiles[t0 + j]
                        self.tr(self.psb.t[:, j * 64:(j + 1) * 64], v.t[:, toks(d, s, r, i)], self.idb[0:64, 0:64],
                                v.b + self.cbf.b, self.psb.b)
                    self.cp(va.t[:, t0:t0 + 16, 0:64], self.psb.t[:, :].rearrange("p (t d) -> p t d", d=64),
                            self.psb.b, va.b, q="dve")
                first_evac = [True]

                def nbrs(s, r, i):
                    nb = []
                    if i > 0:
                        nb.append(((s, r, i - 1), 0))
                    elif s == 1:
                        nb.append(((0, r, npp - 1), 3))
                    nb.append(((s, r, i), 1))
                    if i < npp - 1:
                        nb.append(((s, r, i + 1), 2))
                    elif s == 0:
                        nb.append(((1, r, 0), 4))
                    return nb

                def stA(j_, u_):
                    s, r, i = tiles[j_]
                    sb_ = self.ps[u_ % 3]
                    for j, (kt_, mi) in enumerate(nbrs(s, r, i)):
                        self.mm(sb_.t[:, j * 128:(j + 1) * 128], k.t[:, toks(d, *kt_)], q.t[:, toks(d, s, r, i)],
                                True, True, k.b + q.b, sb_.b)

                def stEB(j_, u_):
                    s, r, i = tiles[j_]
                    nb = nbrs(s, r, i)
                    nn = len(nb)
                    sb_ = self.ps[u_ % 3]
                    e = es[u_ % NP]
                    p = pp[u_ % NP]
                    ob = self.ps[3 + (u_ % 2)]
                    self.act(e.t[:, 0:nn, :], sb_.t[:, 0:nn * 128].rearrange("p (j t) -> p j t", t=128), AF.Exp,
                             sb_.b, e.b, scale=scale)
                    for j, (kt_, mi) in enumerate(nb):
                        self.tt(p.t[:, j, :], e.t[:, j, :], self.maskA.t[:, mi, :], ALU.mult, e.b + self.maskA.b, p.b,
                                q=("pool" if j == 1 else "dve"))
                    for j, (kt_, mi) in enumerate(nb):
                        self.mm(ob.t[0:65, 0:128], va.t[:, tidx[kt_], :], p.t[:, j, :], j == 0, j == nn - 1, va.b + p.b, ob.b)
                    asl = acc.t[:, toks(d, s, r, i)]
                    rd = [accsync] if first_evac[0] else []
                    first_evac[0] = False
                    if g == 0:
                        self.cp(asl, ob.t[0:65, 0:128], ob.b + rd, [], q="dve")
                    else:
                        self.tt(asl, asl, ob.t[0:65, 0:128], ALU.add, ob.b + rd, [])

                stA(0, ui)
                for j_ in range(len(tiles)):
                    if j_ + 1 < len(tiles):
                        stA(j_ + 1, ui + 1)
                    stEB(j_, ui)
                    ui += 1
                self.memset(rec.t[0:1, 0:1], 0.0, [accsync])
                if g == 2:
                    for ck in range(NST):
                        sl = slice(ck * ST, (ck + 1) * ST)
                        self.recip(rec.t[64:65, :], acc.t[64:65, sl], [accsync] + rec.b, rec.b)
                        bb = self.ps[5]
                        self.mm(bb.t[0:64, :], self.ones32[64:65, 0:64], rec.t[64:65, :], True, True, rec.b + self.c32.b, bb.b)
                        o = osb[ck % 2]
                        self.tt(o.t[:], acc.t[0:64, sl], bb.t[0:64, :], ALU.mult, bb.b, o.b)
                        self.load(self.AT[h // 2, (h % 2) * 64:(h % 2) * 64 + 64, sl], o.t[:], o.b, [], q="pool")
                    self.memset(rec.t[0:1, 0:1], 0.0, [accsync])
            self.S.emit_phase()

    def phase_p2b(self, l, m):
        N, NST, NH = self.N, self.NST, self.NH
        last = (l == DEPTH - 1)
        woname = ("a_w_out", "b_w_out", "c_w_out", "d_w_out")[m]
        dvp = 128
        nh = WSHAPES[woname][1] // dvp
        xscale = 128 ** -0.5
        with contextlib.ExitStack() as st:
            self.common_tiles(st)
            kmT = self.sb(st, "pb_kmT", [128, 2, 4, 256], BF16)
            vm = self.sb(st, "pb_vm", [128, 2, 2, 512], BF16)
            ssbank = self.ps[4]
            with contextlib.ExitStack() as st2:
                wkv = self.sb(st2, "pb_wkv", [128, KC, D], BF16)
                self.load(wkv.t[:], self.w_bf["w_xkv"][l].rearrange("(kc p) n -> p kc n", p=128), [self.wB["w_xkv"]], wkv.b)
                mtok = self.sb(st2, "pb_mtok", [128, 2, D], F32)
                mT = self.sb(st2, "pb_mT", [128, KC, 256], F32)
                mTn = self.sb(st2, "pb_mTn", [128, KC, 256], BF16)
                msq = self.sb(st2, "pb_msq", [128, KC, 256], BF16)
                mrs = self.sb(st2, "pb_mrs", [128, ST], F32)
                gmem = self.gn["norm_mem"]
                for hf in range(2):
                    self.load(mtok.t[:], self.mem_in[hf].rearrange("(t p) d -> p t d", p=128), [], mtok.b)
                    for kc in range(KC):
                        bank = self.ps[kc % 4]
                        for t in range(2):
                            self.tr(bank.t[:, t * 128:(t + 1) * 128], mtok.t[:, t, kc * 128:(kc + 1) * 128], self.idf,
                                    mtok.b + self.c32.b, bank.b)
                        self.cp(mT.t[:, kc, :], bank.t[:, 0:256], bank.b, mT.b, q=("act" if kc % 2 else "dve"))
                    self.rms(mT, KC, D, lambda kc: gmem.t[:, l, kc:kc + 1], mTn, msq, mrs, ssbank, cols=256)
                    for hd in range(4):
                        bank = self.ps[hd % 4]
                        for kc in range(KC):
                            self.mm(bank.t[:, 0:256], wkv.t[:, kc, hd * 128:(hd + 1) * 128], mTn.t[:, kc, :], kc == 0,
                                    kc == KC - 1, wkv.b + mTn.b, bank.b)
                        self.cp(kmT.t[:, hf, hd, :], bank.t[:, 0:256], bank.b, kmT.b, q="act")
                    for t in range(2):
                        bank = self.ps[t % 4]
                        for kc in range(KC):
                            self.mm(bank.t[:], mTn.t[:, kc, t * 128:(t + 1) * 128], wkv.t[:, kc, 512:1024], kc == 0,
                                    kc == KC - 1, wkv.b + mTn.b, bank.b)
                        self.cp(vm.t[:, hf, t, :], bank.t[:], bank.b, vm.b, q="dve")
                self.S.emit_phase()
            wo = self.sb(st, "pb_wo", [dvp, nh, D], BF16)
            self.load(wo.t[:], self.w_bf[woname][0].rearrange("(h p) n -> p h n", p=dvp), [self.wB[woname]], wo.b)
            wxq = self.sb(st, "pb_wxq", [128, KC, 512], BF16)
            self.load(wxq.t[:], self.w_bf["w_xq"][l].rearrange("(kc p) n -> p kc n", p=128), [self.wB["w_xq"]], wxq.b)
            wxo = self.sb(st, "pb_wxo", [128, 4, D], BF16)
            self.load(wxo.t[:], self.w_bf["w_xo"][l].rearrange("(kc p) n -> p kc n", p=128), [self.wB["w_xo"]], wxo.b)
            NW1, NW2 = 2, 2
            w1 = [self.sb(st, "pb_w1_%d" % i, [128, KC, 512], BF16) for i in range(NW1)]
            w2 = [self.sb(st, "pb_w2_%d" % i, [128, 16, 256], BF16) for i in range(NW2)]
            xs = [self.sb(st, "pb_x%d" % i, [128, KC, ST], F32, nb=KC) for i in range(2)]
            ats = [self.sb(st, "pb_at%d" % i, [dvp, nh, ST], BF16) for i in range(2)]
            hT = self.sb(st, "pb_h", [128, KC, ST], BF16)
            sq = self.sb(st, "pb_sq", [128, KC, ST], BF16)
            rstd = self.sb(st, "pb_rstd", [128, ST], F32)
            qx = self.sb(st, "pb_qx", [128, 4, ST], BF16)
            px = [self.sb(st, "pb_px%d" % i, [128, ST], BF16) for i in range(2)]
            recx = self.sb(st, "pb_recx", [128, ST], F32)
            ox = self.sb(st, "pb_ox", [128, 4, ST], BF16)
            aT = self.sb(st, "pb_a", [128, 16, ST], BF16, nb=16)
            rl = [self.sb(st, "pb_rl%d" % i, [128, ST], F32) for i in range(2)]
            if last:
                ytok = [self.sb(st, "pb_ytok%d" % i, [128, D], F32) for i in range(2)]

            def ld(s):
                x = xs[s % 2]
                self.load(x.t[:], self.xT[:, :, s * ST:(s + 1) * ST].rearrange("kc p t -> p kc t"), [], x.all)
                a = ats[s % 2]
                self.load(a.t[:], self.AT[0:nh, 0:dvp, s * ST:(s + 1) * ST].rearrange("h p t -> p h t"), [], a.b)

            w1i, w2i, nb = [0], [0], [0]

            def ld_w1(j):
                t = w1[w1i[0] % NW1]; w1i[0] += 1
                self.load(t.t[:], self.w_bf["w_mlp_in"][l][:, j * 512:(j + 1) * 512].rearrange("(kc p) n -> p kc n", p=128),
                          [self.wB["w_mlp_in"]], t.b)
                return t

            def ld_w2(j):
                hf2, c = j // 4, j % 4
                t = w2[w2i[0] % NW2]; w2i[0] += 1
                self.load(t.t[:], self.w_bf["w_mlp_out"][l][hf2 * 2048:(hf2 + 1) * 2048, c * 256:(c + 1) * 256]
                          .rearrange("(fc p) n -> p fc n", p=128), [self.wB["w_mlp_out"]], t.b)
                return t

            def bank_():
                b = self.ps[nb[0] % 4]; nb[0] += 1
                return b

            ld(0)
            for s in range(NST):
                if s + 1 < NST:
                    ld(s + 1)
                x, a = xs[s % 2], ats[s % 2]
                hf = (s * ST) // NH
                w1q = [ld_w1(0), ld_w1(1)]
                for oc in range(KC):
                    bank = bank_()
                    for hh in range(nh):
                        self.mm(bank.t[:], wo.t[:, hh, oc * 128:(oc + 1) * 128], a.t[:, hh, :], hh == 0, hh == nh - 1,
                                wo.b + a.b, bank.b)
                    self.tt(x.t[:, oc, :], x.t[:, oc, :], bank.t[:], ALU.add, [x.b[oc]] + bank.b, [x.b[oc]])
                gx = self.gn["norm_x"]
                self.rms(x, KC, D, lambda kc: gx.t[:, l, kc:kc + 1], hT, sq, rstd, ssbank)
                for hd in range(4):
                    bank = bank_()
                    for kc in range(KC):
                        self.mm(bank.t[:], wxq.t[:, kc, hd * 128:(hd + 1) * 128], hT.t[:, kc, :], kc == 0, kc == KC - 1,
                                wxq.b + hT.b, bank.b)
                    self.cp(qx.t[:, hd, :], bank.t[:], bank.b, qx.b, q="act")
                for hd in range(4):
                    ob, db = self.ps[5], self.ps[6]
                    for t in range(2):
                        bank = bank_()
                        p = px[t]
                        self.mm(bank.t[:], kmT.t[:, hf, hd, t * 128:(t + 1) * 128], qx.t[:, hd, :], True, True,
                                kmT.b + qx.b, bank.b)
                        self.act(p.t[:], bank.t[:], AF.Exp, bank.b, p.b, scale=xscale)
                        self.mm(ob.t[:], vm.t[:, hf, t, hd * 128:(hd + 1) * 128], p.t[:], t == 0, t == 1, vm.b + p.b, ob.b)
                        self.mm(db.t[:], self.onesb, p.t[:], t == 0, t == 1, self.cbf.b + p.b, db.b)
                    self.recip(recx.t[:], db.t[:], db.b, recx.b)
                    self.tt(ox.t[:, hd, :], ob.t[:], recx.t[:], ALU.mult, ob.b + recx.b, ox.b)
                for oc in range(KC):
                    bank = bank_()
                    for hd in range(4):
                        self.mm(bank.t[:], wxo.t[:, hd, oc * 128:(oc + 1) * 128], ox.t[:, hd, :], hd == 0, hd == 3,
                                wxo.b + ox.b, bank.b)
                    self.tt(x.t[:, oc, :], x.t[:, oc, :], bank.t[:], ALU.add, [x.b[oc]] + bank.b, [x.b[oc]])
                gm = self.gn["norm_mlp"]
                self.rms(x, KC, D, lambda kc: gm.t[:, l, kc:kc + 1], hT, sq, rstd, ssbank)
                for hf2 in range(2):
                    w2q = [ld_w2(hf2 * 4 + 0)]
                    for j in range(4):
                        wt = w1q.pop(0)
                        for f in range(4):
                            fc = j * 4 + f
                            bank = bank_()
                            for kc in range(KC):
                                self.mm(bank.t[:], wt.t[:, kc, f * 128:(f + 1) * 128], hT.t[:, kc, :], kc == 0, kc == KC - 1,
                                        wt.b + hT.b, bank.b)
                            r = rl[fc % 2]
                            self.act(r.t[:], bank.t[:], AF.Relu, bank.b, r.b)
                            self.tt(aT.t[:, fc, :], r.t[:], r.t[:], ALU.mult, r.b, [aT.b[fc]], q="pool")
                        jj = hf2 * 4 + j
                        if jj + 2 < 8:
                            w1q.append(ld_w1(jj + 2))
                        if j == 2:
                            w2q.append(ld_w2(hf2 * 4 + 1))
                    for j in range(4):
                        wt = w2q.pop(0)
                        for o2 in range(2):
                            oc = j * 2 + o2
                            bank = bank_()
                            for fc in range(16):
                                self.mm(bank.t[:], wt.t[:, fc, o2 * 128:(o2 + 1) * 128], aT.t[:, fc, :], fc == 0, fc == 15,
                                        wt.b + [aT.b[fc]], bank.b)
                            self.tt(x.t[:, oc, :], x.t[:, oc, :], bank.t[:], ALU.add, [x.b[oc]] + bank.b, [x.b[oc]])
                        if j + 2 < 4:
                            w2q.append(ld_w2(hf2 * 4 + j + 2))
                if not last:
                    self.load(self.xT[:, :, s * ST:(s + 1) * ST].rearrange("kc p t -> p kc t"), x.t[:], x.all, [], q="pool")
                else:
                    gf = self.gn["final_norm"]
                    self.rms(x, KC, D, lambda kc: gf.t[:, kc:kc + 1], x, sq, rstd, ssbank)
                    yT = x
                    for tt_ in range(4):
                        yt = ytok[tt_ % 2]
                        for half in range(2):
                            bank = bank_()
                            for k4 in range(4):
                                kc = half * 4 + k4
                                self.tr(bank.t[:, k4 * 128:(k4 + 1) * 128], yT.t[:, kc, tt_ * 128:(tt_ + 1) * 128], self.idf,
                                        [yT.b[kc]] + self.c32.b, bank.b)
                            self.cp(yt.t[:, half * 512:(half + 1) * 512], bank.t[:], bank.b, yt.b, q=("act" if half else "dve"))
                        r0 = s * ST + tt_ * 128
                        self.load(self.y_out[r0:r0 + 128, :], yt.t[:], yt.b, [], q="pool")
            self.S.emit_phase()

    def phase_final_only(self):
        NST = self.NST
        with contextlib.ExitStack() as st:
            self.common_tiles(st)
            xs = [self.sb(st, "pf_x%d" % i, [128, KC, ST], F32) for i in range(2)]
            yT = self.sb(st, "pf_yT", [128, KC, ST], F32)
            sq = self.sb(st, "pf_sq", [128, KC, ST], BF16)
            rstd = self.sb(st, "pf_rstd", [128, ST], F32)
            ytok = [self.sb(st, "pf_ytok%d" % i, [128, D], F32) for i in range(2)]
            nb = 0
            for s in range(NST):
                x = xs[s % 2]
                self.load(x.t[:], self.xT[:, :, s * ST:(s + 1) * ST].rearrange("kc p t -> p kc t"), [], x.b)
                gf = self.gn["final_norm"]
                self.rms(x, KC, D, lambda kc: gf.t[:, kc:kc + 1], yT, sq, rstd, self.ps[4])
                for tt_ in range(4):
                    yt = ytok[tt_ % 2]
                    for half in range(2):
                        bank = self.ps[nb % 4]; nb += 1
                        for k4 in range(4):
                            kc = half * 4 + k4
                            self.tr(bank.t[:, k4 * 128:(k4 + 1) * 128], yT.t[:, kc, tt_ * 128:(tt_ + 1) * 128], self.idf,
                                    yT.b + self.c32.b, bank.b)
                        self.cp(yt.t[:, half * 512:(half + 1) * 512], bank.t[:], bank.b, yt.b, q=("act" if half else "dve"))
                    r0 = s * ST + tt_ * 128
                    self.load(self.y_out[r0:r0 + 128, :], yt.t[:], yt.b, [], q="pool")
            self.S.emit_phase()


def _rope_tab(pos, theta, rot):
    half = rot // 2
    inv = np.exp(np.arange(half, dtype=np.float32) * np.float32(-2.0 * math.log(theta) / rot)).astype(np.float32)
    ang = pos.astype(np.float32)[None, :] * inv[:, None]
    return np.cos(ang).astype(np.float32), np.sin(ang).astype(np.float32)


def host_tables(NH, is_pair):
    N = 2 * NH
    t = np.arange(N)
    pos = (t % NH) if is_pair else t
    out = {}
    c, s = _rope_tab(pos, 500000.0, 16)
    cosA = np.ones((128, N), np.float32)
    sinA = np.zeros((128, N), np.float32)
    swA = np.zeros((128, 128), np.float32)
    for u in range(2):
        for i in range(8):
            cosA[u * 64 + i] = c[i]; cosA[u * 64 + 8 + i] = c[i]
            sinA[u * 64 + i] = -s[i]; sinA[u * 64 + 8 + i] = s[i]
            swA[u * 64 + 8 + i, u * 64 + i] = 1.0
            swA[u * 64 + i, u * 64 + 8 + i] = 1.0
    out.update(tA_cos=cosA, tA_sin=sinA, swA=swA)
    rows = pos // 64
    cols = pos % 64
    cr, sr = _rope_tab(rows, 10000.0, 32)
    cc, sc = _rope_tab(cols, 10000.0, 32)
    cosB = np.ones((128, N), np.float32)
    sinB = np.zeros((128, N), np.float32)
    swB = np.zeros((128, 128), np.float32)
    for u in range(2):
        for off, (c_, s_) in ((0, (cr, sr)), (32, (cc, sc))):
            for i in range(16):
                a, b = u * 64 + off + i, u * 64 + off + 16 + i
                cosB[a] = c_[i]; cosB[b] = c_[i]
                sinB[a] = -s_[i]; sinB[b] = s_[i]
                swB[b, a] = 1.0
                swB[a, b] = 1.0
    out.update(tB_cos=cosB, tB_sin=sinB, swB=swB)
    c, s = _rope_tab(pos, 500000.0, 32)
    cosD = np.ones((96, N), np.float32)
    sinD = np.zeros((96, N), np.float32)
    swD = np.zeros((96, 96), np.float32)
    cosK = np.ones((32, N), np.float32)
    sinK = np.zeros((32, N), np.float32)
    swK = np.zeros((32, 32), np.float32)
    for i in range(16):
        cosD[64 + i] = c[i]; cosD[80 + i] = c[i]
        sinD[64 + i] = -s[i]; sinD[80 + i] = s[i]
        swD[80 + i, 64 + i] = 1.0
        swD[64 + i, 80 + i] = 1.0
        cosK[i] = c[i]; cosK[16 + i] = c[i]
        sinK[i] = -s[i]; sinK[16 + i] = s[i]
        swK[16 + i, i] = 1.0
        swK[i, 16 + i] = 1.0
    out.update(tD_cos=cosD, tD_sin=sinD, swD=swD, tDk_cos=cosK, tDk_sin=sinK, swDk=swK)
    kk = np.arange(128)[:, None]
    qq = np.arange(128)[None, :]
    mprev = (kk >= qq + 64).astype(np.float32)
    mcur = (np.abs(qq - kk) <= 64).astype(np.float32)
    mnext = (kk <= qq - 64).astype(np.float32)
    x = 0.0 if is_pair else 1.0
    out["maskA"] = np.ascontiguousarray(np.stack([mprev, mcur, mnext, mprev * x, mnext * x], axis=1))
    xb = np.zeros((128, 2), np.float32)
    xb[:, 1] = -30000.0 if is_pair else 0.0
    out["xbias"] = xb
    cst = np.zeros((128, 3, 128), np.float32)
    cst[:, 0, :] = np.eye(128, dtype=np.float32)
    cst[:, 1, :] = 1.0
    cst[0:64, 2, 0:64] = 1.0
    cst[64:128, 2, 64:128] = 1.0
    out["cst"] = cst
    return out


_PROG_CACHE = {}


def run_slots(slots, weights, NH, depth=DEPTH):
    key = (NH, depth)
    if key not in _PROG_CACHE:
        _PROG_CACHE[key] = Prog(NH, depth)
    prog = _PROG_CACHE[key]
    tabs = {True: host_tables(NH, True), False: host_tables(NH, False)}
    in_maps = []
    for sl in slots:
        mp = {"x_slot": np.ascontiguousarray(sl["x"], dtype=np.float32),
              "mem2": np.ascontiguousarray(sl["mem"], dtype=np.float32)}
        for n in WNAMES:
            mp[n] = weights[n]
        for n in GNAMES:
            mp[n] = weights[n]
        mp.update(tabs[bool(sl["pair"])])
        in_maps.append(mp)
    import os
    ncr = int(os.environ.get("MK_NCORES", "8"))
    res = run_bass_kernel_spmd(prog.nc, in_maps[:ncr], core_ids=list(range(ncr)))
    out = [r["y_slot"] for r in res.results]
    while len(out) < 8:
        out.append(out[-1])
    return out


def kernel(**inputs):
    NH = 4096
    xp = np.asarray(inputs["x_prompt"], dtype=np.float32)
    xs = np.asarray(inputs["x_sample"], dtype=np.float32)
    mp = np.asarray(inputs["mem_prompt"], dtype=np.float32)
    ms = np.asarray(inputs["mem_sample"], dtype=np.float32)
    weights = {n: np.ascontiguousarray(np.asarray(inputs[n], dtype=np.float32)) for n in list(WNAMES) + list(GNAMES)}
    slots = []
    for b in range(2):
        slots.append({"x": xp[b], "mem": np.stack([mp[b], mp[b]]), "pair": False})
    for j in range(4):
        slots.append({"x": xs[2 * j:2 * j + 2].reshape(2 * NH, D), "mem": ms[2 * j:2 * j + 2], "pair": True})
    for j in range(2):
        slots.append(slots[2 + j])
    ys = run_slots(slots, weights, NH)
    y_prompt = np.stack([ys[0], ys[1]]).astype(np.float32)
    y_sample = np.concatenate([ys[2 + j].reshape(2, NH, D) for j in range(4)], axis=0).astype(np.float32)
    return (y_prompt, y_sample)
```

```python
import math
import contextlib
import numpy as np
import concourse.bass as bass
import concourse.mybir as mybir
from concourse.bass_utils import run_bass_kernel_spmd

F32 = mybir.dt.float32
BF16 = mybir.dt.bfloat16
AF = mybir.ActivationFunctionType
ALU = mybir.AluOpType

D = 1024
KC = 8
ST = 512
DEPTH = 4
EPS = 1e-6
DMA_RING = 8


class Buf:
    __slots__ = ("last_w", "readers", "excl")

    def __init__(self, excl=False):
        self.last_w = None
        self.readers = {}
        self.excl = excl


class Op:
    __slots__ = ("q", "fn", "kind", "deps", "signal", "sigval", "slot", "rnd", "phase")


class Sched:
    ENGS = ("pe", "act", "dve", "pool", "sp")

    def __init__(self, nc, st):
        self.nc = nc
        self.ops = {q: [] for q in self.ENGS}
        self.ndma = {q: 0 for q in self.ENGS}
        self.sigcnt = {q: 0 for q in self.ENGS}
        self.phase = 0
        self.csem = {q: st.enter_context(nc.semaphore("c_" + q)) for q in self.ENGS}
        self.dsem = {q: [st.enter_context(nc.semaphore("d_%s_%d" % (q, i))) for i in range(DMA_RING)]
                     for q in ("sp", "pool")}
        self.waited = {q: {} for q in self.ENGS}

    def op(self, q, fn, reads=(), writes=(), kind="c"):
        o = Op()
        o.q = q
        o.fn = fn
        o.kind = kind
        o.signal = False
        o.sigval = 0
        o.phase = self.phase
        deps = {}
        xr = [b for b in reads if b.excl]
        if xr:
            reads = [b for b in reads if not b.excl]
            writes = list(writes) + xr
        for b in reads:
            w = b.last_w
            if w is not None:
                deps[id(w)] = w
        for b in writes:
            w = b.last_w
            if w is not None:
                deps[id(w)] = w
            for r in b.readers.values():
                deps[id(r)] = r
        for b in reads:
            if kind == "c":
                b.readers[q] = o
            else:
                b.readers[("d", q, self.ndma[q] % (4 * DMA_RING))] = o
        for b in writes:
            b.last_w = o
            b.readers = {}
        dl = []
        for d in deps.values():
            if d is o or d.phase != self.phase:
                continue
            if d.kind == "c":
                if d.q == q and q == "pe" and kind == "c":
                    continue
                d.signal = True
            dl.append(d)
        o.deps = dl
        if kind == "d":
            i = self.ndma[q]
            self.ndma[q] = i + 1
            o.slot = i % DMA_RING
            o.rnd = i // DMA_RING
        self.ops[q].append(o)
        return o

    def emit_phase(self):
        nc = self.nc
        for q in self.ENGS:
            c = self.sigcnt[q]
            for o in self.ops[q]:
                if o.kind == "c" and o.signal:
                    c += 1
                    o.sigval = c
            self.sigcnt[q] = c
        with nc.Block() as block:
            handles = {"pe": block.tensor, "act": block.scalar, "dve": block.vector,
                       "pool": block.gpsimd, "sp": block.sync}
            for q in self.ENGS:
                ops = self.ops[q]
                if not ops:
                    continue

                def body(eng, q=q, ops=ops):
                    waited = self.waited[q]

                    def wait(sem, key, val):
                        if waited.get(key, 0) >= val:
                            return
                        waited[key] = val
                        eng.wait_ge(sem, val)

                    for o in ops:
                        for d in o.deps:
                            if d.kind == "c":
                                wait(self.csem[d.q], ("c", d.q), d.sigval)
                            else:
                                wait(self.dsem[d.q][d.slot], ("d", d.q, d.slot), 16 * (d.rnd + 1))
                        if o.kind == "d":
                            if o.rnd > 0:
                                wait(self.dsem[q][o.slot], ("d", q, o.slot), 16 * o.rnd)
                            ins = o.fn(eng)
                            ins.then_inc(self.dsem[q][o.slot], 16)
                        else:
                            ins = o.fn(eng)
                            if o.signal:
                                ins.then_inc(self.csem[q], 1)
                    n = self.ndma[q]
                    if q in self.dsem and n > 0:
                        for s in range(min(DMA_RING, n)):
                            cnt = (n - 1 - s) // DMA_RING + 1
                            wait(self.dsem[q][s], ("d", q, s), 16 * cnt)

                handles[q](body)
        self.ops = {q: [] for q in self.ENGS}
        self.phase += 1


class Tile:
    def __init__(self, t, nb=1):
        self.t = t
        self.b = [Buf() for _ in range(nb)]

    @property
    def all(self):
        return self.b


WNAMES = ["w_xq", "w_xkv", "w_xo", "w_mlp_in", "w_mlp_out", "a_w_in", "a_w_out", "b_w_in", "b_w_out",
          "c_w_in", "c_w_out", "d_w_in", "d_w_uq", "d_w_ukv", "d_w_out"]
WSHAPES = {"w_xq": [4, 1024, 512], "w_xkv": [4, 1024, 1024], "w_xo": [4, 512, 1024],
           "w_mlp_in": [4, 1024, 4096], "w_mlp_out": [4, 4096, 1024], "a_w_in": [1, 1024, 4608],
           "a_w_out": [1, 512, 1024], "b_w_in": [1, 1024, 1536], "b_w_out": [1, 1024, 1024],
           "c_w_in": [1, 1024, 3072], "c_w_out": [1, 1024, 1024], "d_w_in": [1, 1024, 672],
           "d_w_uq": [1, 384, 1536], "d_w_ukv": [1, 256, 2048], "d_w_out": [1, 1024, 1024]}
GNAMES = {"norm_mix": [4, 1024], "norm_x": [4, 1024], "norm_mem": [4, 1024], "norm_mlp": [4, 1024],
          "final_norm": [1024], "b_q_norm": [1, 64], "b_k_norm": [1, 64], "c_lambda_q1": [1, 64],
          "c_lambda_k1": [1, 64], "c_lambda_q2": [1, 64], "c_lambda_k2": [1, 64], "c_sub_norm": [1, 128],
          "d_q_norm": [1, 384], "d_kv_norm": [1, 256]}
A_DILS = (1, 4, 16)


class Prog:
    def __init__(self, NH, depth=DEPTH):
        self.NH = NH
        self.N = 2 * NH
        self.NST = self.N // ST
        self.NT = self.N // 128
        self.depth = depth
        self.nc = bass.Bass("TRN2", target_bir_lowering=False)
        self.build()

    def dram_in(self, name, shape, dt=F32):
        return self.nc.dram_tensor(name, list(shape), dt, kind="ExternalInput").ap()

    def dram_tmp(self, name, shape, dt):
        return self.nc.dram_tensor(name, list(shape), dt, kind="Internal").ap()

    def sb(self, st, name, shape, dt, nb=1):
        self._uid = getattr(self, "_uid", 0) + 1
        return Tile(st.enter_context(self.nc.sbuf_tensor("s%d_%s" % (self._uid, name), list(shape), dt)), nb)

    def op(self, *a, **k):
        return self.S.op(*a, **k)

    def load(self, out_ap, in_ap, reads, writes, q="sp", slow=False):
        if slow:
            self.op(q, lambda e: e.dma_start(out=out_ap, in_=in_ap, allow_slow_non_contiguous=True),
                    reads=reads, writes=writes, kind="d")
        else:
            self.op(q, lambda e: e.dma_start(out=out_ap, in_=in_ap), reads=reads, writes=writes, kind="d")

    def mm(self, out_ap, lhsT, rhs, start, stop, reads, writes):
        self.op("pe", lambda e: e.matmul(out_ap, lhsT=lhsT, rhs=rhs, start=start, stop=stop),
                reads=reads, writes=writes)

    def tr(self, out_ap, in_ap, ident, reads, writes):
        self.op("pe", lambda e: e.transpose(out=out_ap, in_=in_ap, identity=ident), reads=reads, writes=writes)

    def act(self, out_ap, in_ap, func, reads, writes, scale=None, bias=None, accum=None):
        kw = {}
        if scale is not None:
            kw["scale"] = scale
        if bias is not None:
            kw["bias"] = bias
        if accum is not None:
            kw["accum_out"] = accum
        self.op("act", lambda e: e.activation(out=out_ap, in_=in_ap, func=func, **kw), reads=reads, writes=writes)

    def tt(self, out_ap, in0, in1, op, reads, writes, q="dve"):
        self.op(q, lambda e: e.tensor_tensor(out=out_ap, in0=in0, in1=in1, op=op), reads=reads, writes=writes)

    def stt(self, out_ap, in0, scalar, in1, op0, op1, reads, writes, q="dve"):
        self.op(q, lambda e: e.scalar_tensor_tensor(out=out_ap, in0=in0, scalar=scalar, in1=in1, op0=op0, op1=op1),
                reads=reads, writes=writes)

    def ts(self, out_ap, in0, s1, s2, op0, op1, reads, writes, q="dve"):
        self.op(q, lambda e: e.tensor_scalar(out=out_ap, in0=in0, scalar1=s1, scalar2=s2, op0=op0, op1=op1),
                reads=reads, writes=writes)

    def cp(self, out_ap, in_ap, reads, writes, q="dve"):
        if q == "act":
            self.act(out_ap, in_ap, AF.Copy, reads, writes)
        else:
            self.op(q, lambda e: e.tensor_copy(out=out_ap, in_=in_ap), reads=reads, writes=writes)

    def recip(self, out_ap, in_ap, reads, writes):
        self.op("dve", lambda e: e.reciprocal(out=out_ap, in_=in_ap), reads=reads, writes=writes)

    def memset(self, ap, val, writes, q="dve"):
        self.op(q, lambda e: e.memset(ap, val), writes=writes)

    def build(self):
        nc = self.nc
        N, NH, NST, NT = self.N, self.NH, self.NST, self.NT
        self.x_in = self.dram_in("x_slot", [N, D])
        self.mem_in = self.dram_in("mem2", [2, 256, D])
        self.w_in = {n: self.dram_in(n, WSHAPES[n]) for n in WNAMES}
        self.g_in = {n: self.dram_in(n, GNAMES[n]) for n in GNAMES}
        self.tabs = {}
        for n, r in (("tA", 128), ("tB", 128), ("tD", 96), ("tDk", 32)):
            self.tabs[n] = (self.dram_in(n + "_cos", [r, N]), self.dram_in(n + "_sin", [r, N]))
        self.sw_in = {n: self.dram_in(n, [r, r]) for n, r in (("swA", 128), ("swB", 128), ("swD", 96), ("swDk", 32))}
        self.maskA_in = self.dram_in("maskA", [128, 5, 128])
        self.xbias_in = self.dram_in("xbias", [128, 2])
        self.cst_in = self.dram_in("cst", [128, 3, 128])
        self.y_out = nc.dram_tensor("y_slot", [N, D], F32, kind="ExternalOutput").ap()
        self.w_bf = {n: self.dram_tmp(n + "_bf", WSHAPES[n], BF16) for n in WNAMES}
        self.xT = self.dram_tmp("xT", [KC, 128, N], F32)
        self.QT = self.dram_tmp("QT", [16, 128, N], BF16)
        self.KT = self.dram_tmp("KT", [16, 128, N], BF16)
        self.VT = self.dram_tmp("VT", [16, 128, N], BF16)
        self.KR = self.dram_tmp("KR", [32, N], BF16)
        self.AT = self.dram_tmp("AT", [16, 128, N], BF16)

        with contextlib.ExitStack() as gst:
            self.S = Sched(nc, gst)
            self.psbig = gst.enter_context(nc.psum_tensor("psbig", [128, 8 * 512], F32))
            self.ps = [Tile(self.psbig[:, i * 512:(i + 1) * 512]) for i in range(8)]
            self.psb = Tile(self.psbig[:, 7 * 512:8 * 512].bitcast(BF16))
            self.psb.b = self.ps[7].b
            for t in self.ps:
                t.b[0].excl = True
            c32 = self.sb(gst, "c32", [128, 3, 128], F32)
            self.load(c32.t[:], self.cst_in, [], c32.b)
            self.idf = c32.t[:, 0, :]
            self.ones32 = c32.t[:, 1, :]
            cbf = self.sb(gst, "cbf", [128, 3, 128], BF16)
            self.cp(cbf.t[:], c32.t[:], c32.b, cbf.b)
            self.c32, self.cbf = c32, cbf
            self.idb = cbf.t[:, 0, :]
            self.onesb = cbf.t[:, 1, :]
            self.blkb = cbf.t[:, 2, :]
            self.sw = {}
            for n, r in (("swA", 128), ("swB", 128), ("swD", 96), ("swDk", 32)):
                t32 = self.sb(gst, n + "32", [r, r], F32)
                tb = self.sb(gst, n + "b", [r, r], BF16)
                self.load(t32.t[:], self.sw_in[n], [], t32.b)
                self.cp(tb.t[:], t32.t[:], t32.b, tb.b)
                self.sw[n] = tb
            self.gn = {}
            for n in ("norm_mix", "norm_x", "norm_mem", "norm_mlp"):
                t = self.sb(gst, "g_" + n, [128, 4, KC], F32)
                for l in range(4):
                    self.load(t.t[:, l, :], self.g_in[n][l].rearrange("(kc p) -> p kc", p=128), [], t.b, slow=True)
                self.gn[n] = t
            t = self.sb(gst, "g_final", [128, KC], F32)
            self.load(t.t[:], self.g_in["final_norm"].rearrange("(kc p) -> p kc", p=128), [], t.b, slow=True)
            self.gn["final_norm"] = t
            t = self.sb(gst, "g_dq", [128, 3], F32)
            self.load(t.t[:], self.g_in["d_q_norm"][0].rearrange("(kc p) -> p kc", p=128), [], t.b, slow=True)
            self.gn["d_q_norm"] = t
            t = self.sb(gst, "g_dkv", [128, 2], F32)
            self.load(t.t[:], self.g_in["d_kv_norm"][0].rearrange("(kc p) -> p kc", p=128), [], t.b, slow=True)
            self.gn["d_kv_norm"] = t
            t = self.sb(gst, "g_bqk", [128, 2], F32)
            for j, n in enumerate(("b_q_norm", "b_k_norm")):
                for hh in range(2):
                    self.load(t.t[hh * 64:(hh + 1) * 64, j:j + 1], self.g_in[n][0].rearrange("(p o) -> p o", o=1),
                              [], t.b, slow=True)
            self.gn["b_qk"] = t
            t = self.sb(gst, "g_csub", [128, 1], F32)
            self.load(t.t[:], self.g_in["c_sub_norm"][0].rearrange("(p o) -> p o", o=1), [], t.b, slow=True)
            self.gn["c_sub"] = t
            mA32 = self.sb(gst, "mA32", [128, 5, 128], F32)
            self.load(mA32.t[:], self.maskA_in, [], mA32.b)
            self.maskA = self.sb(gst, "maskA", [128, 5, 128], BF16)
            self.cp(self.maskA.t[:], mA32.t[:], mA32.b, self.maskA.b)
            self.xbias = self.sb(gst, "xbias", [128, 2], F32)
            self.load(self.xbias.t[:], self.xbias_in, [], self.xbias.b)
            self.lam_init = 0.8 - 0.6 * math.exp(-0.3 * 2)
            lv = self.sb(gst, "lamv", [128, 4, 64], F32)
            for j, n in enumerate(("c_lambda_q1", "c_lambda_k1", "c_lambda_q2", "c_lambda_k2")):
                self.load(lv.t[:, j, :], self.g_in[n][0].partition_broadcast(128), [], lv.b)
            lp = self.sb(gst, "lamp", [128, 2, 64], F32)
            ls = self.sb(gst, "lams", [128, 4], F32)
            self.tt(lp.t[:, 0, :], lv.t[:, 0, :], lv.t[:, 1, :], ALU.mult, lv.b, lp.b)
            self.tt(lp.t[:, 1, :], lv.t[:, 2, :], lv.t[:, 3, :], ALU.mult, lv.b, lp.b)
            self.op("dve", lambda e: e.tensor_reduce(out=ls.t[:, 0:2], in_=lp.t[:], axis=mybir.AxisListType.X, op=ALU.add),
                    reads=lp.b, writes=ls.b)
            self.act(ls.t[:, 2:4], ls.t[:, 0:2], AF.Exp, ls.b, ls.b)
            self.neglam = self.sb(gst, "neglam", [128, 1], F32)
            self.stt(self.neglam.t[:], ls.t[:, 3:4], -self.lam_init, ls.t[:, 2:3], ALU.add, ALU.subtract, ls.b, self.neglam.b)
            self.csubg = self.sb(gst, "csubg", [128, 1], F32)
            self.act(self.csubg.t[:], self.gn["c_sub"].t[:], AF.Copy, self.gn["c_sub"].b, self.csubg.b,
                     scale=1.0 - self.lam_init)
            self.wB = {n: Buf() for n in WNAMES}
            for n in WNAMES:
                L, R, C = WSHAPES[n]
                rows = max(128, (min(R, (1 << 20) // C) // 16) * 16)
                for l in range(L):
                    for r0 in range(0, R, rows):
                        r1 = min(R, r0 + rows)
                        self.load(self.w_bf[n][l, r0:r1, :], self.w_in[n][l, r0:r1, :], [], [self.wB[n]], q="pool")

            import os
            maxph = int(os.environ.get("MK_MAXPH", "999"))
            steps = [self.phase_p0]
            for l in range(self.depth):
                m = l % 4
                steps.append(lambda l=l, m=m: self.phase_p1(l, m))
                steps.append((lambda: self.phase_attn_A()) if m == 0 else (lambda m=m: self.phase_attn(m)))
                steps.append(lambda l=l, m=m: self.phase_p2b(l, m))
            if self.depth < DEPTH:
                steps.append(self.phase_final_only)
            for i, f in enumerate(steps):
                if i >= maxph:
                    break
                f()

    def phase_p0(self):
        with contextlib.ExitStack() as st:
            xin = [self.sb(st, "p0_xin%d" % i, [128, 4, D], F32) for i in range(2)]
            xo = [self.sb(st, "p0_xo%d" % i, [128, KC, ST], F32, nb=KC) for i in range(2)]
            for s in range(self.NST):
                a, o = xin[s % 2], xo[s % 2]
                self.load(a.t[:], self.x_in[s * ST:(s + 1) * ST, :].rearrange("(tt p) d -> p tt d", p=128), [], a.b)
                for kc in range(KC):
                    bank = self.ps[kc % 4]
                    for tt_ in range(4):
                        self.tr(bank.t[:, tt_ * 128:(tt_ + 1) * 128], a.t[:, tt_, kc * 128:(kc + 1) * 128], self.idf,
                                a.b + self.c32.b, bank.b)
                    self.cp(o.t[:, kc, :], bank.t[:], bank.b, [o.b[kc]], q=("act" if kc % 2 else "dve"))
                self.load(self.xT[:, :, s * ST:(s + 1) * ST].rearrange("kc p t -> p kc t"), o.t[:], o.b, [], q="sp")
            self.S.emit_phase()

    def rms(self, src, nkc, nfeat, gain_ap_fn, out, sq, rstd, ssbank, ones_lhsT=None, ones_reads=None, cols=ST):
        if ones_lhsT is None:
            ones_lhsT, ones_reads = self.onesb, self.cbf.b
        for kc in range(nkc):
            self.act(sq.t[:, kc, :cols], src.t[:, kc, :cols], AF.Square, src.b if len(src.b) == 1 else [src.b[kc]],
                     sq.b if len(sq.b) == 1 else [sq.b[kc]])
        for kc in range(nkc):
            self.mm(ssbank.t[:, :cols], ones_lhsT, sq.t[:, kc, :cols], kc == 0, kc == nkc - 1,
                    (sq.b if len(sq.b) == 1 else [sq.b[kc]]) + ones_reads, ssbank.b)
        self.act(rstd.t[:, :cols], ssbank.t[:, :cols], AF.Sqrt, ssbank.b + self.epsb.b, rstd.b, scale=1.0 / nfeat, bias=self.epsb.t[:])
        self.recip(rstd.t[:, :cols], rstd.t[:, :cols], rstd.b, rstd.b)
        for kc in range(nkc):
            self.stt(out.t[:, kc, :cols], src.t[:, kc, :cols], gain_ap_fn(kc), rstd.t[:, :cols], ALU.mult, ALU.mult,
                     (src.b if len(src.b) == 1 else [src.b[kc]]) + rstd.b + self.gall,
                     out.b if len(out.b) == 1 else [out.b[kc]])

    def layer(self, l):
        m = l % 4
        self.phase_p1(l, m)
        if m == 0:
            self.phase_attn_A()
        else:
            self.phase_attn(m)
        self.phase_p2b(l, m)

    def common_tiles(self, st):
        self.epsb = self.sb(st, "epsb", [128, 1], F32)
        self.memset(self.epsb.t[:], EPS, self.epsb.b)
        self.gall = []
        for t in self.gn.values():
            self.gall += t.b

    def rope(self, st_tiles, psrc, rows, swname, cosT, sinT, s_idx, out_ap, out_b, tag):
        import os
        if os.environ.get("MK_NOROPE"):
            self.cp(out_ap, psrc.t[:rows, :], psrc.b, out_b, q="act")
            return
        xb, t1, t2, swbank = st_tiles
        lvl = int(os.environ.get("MK_ROPE", "9"))
        sw = self.sw[swname]
        self.cp(xb.t[:rows, :], psrc.t[:rows, :], psrc.b, xb.b, q="act")
        self.mm(swbank.t[:rows, :], sw.t[:], xb.t[:rows, :], True, True, xb.b + sw.b, swbank.b)
        if lvl == 1:
            self.cp(out_ap, swbank.t[:rows, :], swbank.b + psrc.b, out_b, q="act")
            return
        self.tt(t1.t[:rows, :], psrc.t[:rows, :], cosT.t[:rows, :], ALU.mult, psrc.b + cosT.b, t1.b)
        self.tt(t2.t[:rows, :], swbank.t[:rows, :], sinT.t[:rows, :], ALU.mult, swbank.b + sinT.b, t2.b)
        if lvl == 2:
            self.cp(out_ap, t2.t[:rows, :], t1.b + t2.b, out_b, q="act")
            return
        if lvl == 3:
            self.tt(out_ap, t1.t[:rows, :], t2.t[:rows, :], ALU.add, t1.b + t2.b, out_b, q="dve")
            return
        self.tt(out_ap, t1.t[:rows, :], t2.t[:rows, :], ALU.add, t1.b + t2.b, out_b, q="pool")

    def phase_p1(self, l, m):
        N, NST = self.N, self.NST
        wname = ("a_w_in", "b_w_in", "c_w_in", "d_w_in")[m]
        C = WSHAPES[wname][2]
        with contextlib.ExitStack() as st:
            self.common_tiles(st)
            w = self.sb(st, "p1_w", [128, KC, C], BF16)
            self.load(w.t[:], self.w_bf[wname][0].rearrange("(kc p) n -> p kc n", p=128), [self.wB[wname]], w.b)
            if m == 3:
                wuq = self.sb(st, "p1_wuq", [128, 3, 1536], BF16)
                self.load(wuq.t[:], self.w_bf["d_w_uq"][0].rearrange("(kc p) n -> p kc n", p=128), [self.wB["d_w_uq"]], wuq.b)
                wukv = self.sb(st, "p1_wukv", [128, 2, 2048], BF16)
                self.load(wukv.t[:], self.w_bf["d_w_ukv"][0].rearrange("(kc p) n -> p kc n", p=128), [self.wB["d_w_ukv"]], wukv.b)
            xs = [self.sb(st, "p1_x%d" % i, [128, KC, ST], F32) for i in range(2)]
            hT = self.sb(st, "p1_h", [128, KC, ST], BF16)
            sq = self.sb(st, "p1_sq", [128, KC, ST], BF16)
            rstd = self.sb(st, "p1_rstd", [128, ST], F32)
            tabn = ("tA", "tB", "tA", "tD")[m]
            trows = 96 if m == 3 else 128
            cosT = [self.sb(st, "p1_cos%d" % i, [trows, ST], F32) for i in range(2)]
            sinT = [self.sb(st, "p1_sin%d" % i, [trows, ST], F32) for i in range(2)]
            if m == 3:
                cosK = [self.sb(st, "p1_cosk%d" % i, [32, ST], F32) for i in range(2)]
                sinK = [self.sb(st, "p1_sink%d" % i, [32, ST], F32) for i in range(2)]
                cq = self.sb(st, "p1_cq", [128, 5, ST], F32)
                cqn = self.sb(st, "p1_cqn", [128, 5, ST], BF16)
                sq2 = self.sb(st, "p1_sq2", [128, 3, ST], BF16)
                rstd2 = self.sb(st, "p1_rstd2", [128, ST], F32)
            xb = self.sb(st, "p1_xb", [128, ST], BF16)
            t1 = self.sb(st, "p1_t1", [128, ST], F32)
            t2 = self.sb(st, "p1_t2", [128, ST], F32)
            qn = self.sb(st, "p1_qn", [128, ST], F32)
            qsq = self.sb(st, "p1_qsq", [128, 1, ST], BF16)
            NO = 4
            outs = [self.sb(st, "p1_o%d" % i, [128, ST], BF16) for i in range(NO)]
            self.p1_oi = 0
            swbank = self.ps[5]
            ssbank = self.ps[4]
            rope_tiles = (xb, t1, t2, swbank)

            def ld(s):
                x = xs[s % 2]
                self.load(x.t[:], self.xT[:, :, s * ST:(s + 1) * ST].rearrange("kc p t -> p kc t"), [], x.b)
                tc_, ts_ = self.tabs[tabn]
                self.load(cosT[s % 2].t[:], tc_[:, s * ST:(s + 1) * ST], [], cosT[s % 2].b)
                self.load(sinT[s % 2].t[:], ts_[:, s * ST:(s + 1) * ST], [], sinT[s % 2].b)
                if m == 3:
                    tc_, ts_ = self.tabs["tDk"]
                    self.load(cosK[s % 2].t[:], tc_[:, s * ST:(s + 1) * ST], [], cosK[s % 2].b)
                    self.load(sinK[s % 2].t[:], ts_[:, s * ST:(s + 1) * ST], [], sinK[s % 2].b)

            def nxt_out():
                o = outs[self.p1_oi % NO]
                self.p1_oi += 1
                return o

            def proj(bank, wt, nk, c0, cn, src):
                for kc in range(nk):
                    self.mm(bank.t[:cn, :], wt.t[:, kc, c0:c0 + cn], src.t[:, kc, :], kc == 0, kc == nk - 1,
                            wt.b + src.b, bank.b)

            def store(dst_ap, o, rows, p0=0):
                self.load(dst_ap, o.t[p0:p0 + rows, :], o.b, [], q="pool")

            ld(0)
            for s in range(NST):
                if s + 1 < NST:
                    ld(s + 1)
                x = xs[s % 2]
                cs, sn = cosT[s % 2], sinT[s % 2]
                sl = slice(s * ST, (s + 1) * ST)
                gm = self.gn["norm_mix"]
                self.rms(x, KC, D, lambda kc: gm.t[:, l, kc:kc + 1], hT, sq, rstd, ssbank)
                nb = 0
                if m == 0:
                    for c in range(36):
                        g, typ, hp = c // 12, (c % 12) // 4, c % 4
                        bank = self.ps[nb % 4]; nb += 1
                        proj(bank, w, KC, c * 128, 128, hT)
                        o = nxt_out()
                        if typ < 2:
                            self.rope(rope_tiles, bank, 128, "swA", cs, sn, s, o.t[:], o.b, "a")
                        else:
                            self.cp(o.t[:], bank.t[:], bank.b, o.b, q="act")
                        dst = (self.QT, self.KT, self.VT)[typ]
                        store(dst[g * 4 + hp, :, sl], o, 128)
                elif m == 1:
                    gq = self.gn["b_qk"]
                    for c in range(12):
                        bank = self.ps[nb % 4]; nb += 1
                        proj(bank, w, KC, c * 128, 128, hT)
                        o = nxt_out()
                        if c < 10:
                            self.cp(qn.t[:], bank.t[:], bank.b, qn.b, q="act")
                            qt = Tile(qn.t[:].rearrange("p (o t) -> p o t", o=1))
                            qt.b = qn.b
                            gcol = 0 if c < 8 else 1
                            nbk = self.ps[6]
                            self.rms(qt, 1, 64, lambda kc: gq.t[:, gcol:gcol + 1], qt, qsq, rstd, ssbank,
                                     ones_lhsT=self.blkb, ones_reads=self.cbf.b)
                            self.rope(rope_tiles, qn, 128, "swB", cs, sn, s, o.t[:], o.b, "b")
                            dst = self.QT[c] if c < 8 else self.KT[c - 8]
                        else:
                            self.cp(o.t[:], bank.t[:], bank.b, o.b, q="act")
                            dst = self.VT[c - 10]
                        store(dst[:, sl], o, 128)
                elif m == 2:
                    for c in range(24):
                        typ, hh = c // 8, c % 8
                        bank = self.ps[nb % 4]; nb += 1
                        proj(bank, w, KC, c * 128, 128, hT)
                        o = nxt_out()
                        if typ < 2:
                            self.rope(rope_tiles, bank, 128, "swA", cs, sn, s, o.t[:], o.b, "c")
                        else:
                            self.cp(o.t[:], bank.t[:], bank.b, o.b, q="act")
                        dst = (self.QT, self.KT, self.VT)[typ]
                        store(dst[hh, :, sl], o, 128)
                else:
                    for c in range(5):
                        bank = self.ps[nb % 4]; nb += 1
                        proj(bank, w, KC, c * 128, 128, hT)
                        self.cp(cq.t[:, c, :], bank.t[:], bank.b, cq.b, q="act")
                    bank = self.ps[nb % 4]; nb += 1
                    proj(bank, w, KC, 640, 32, hT)
                    o = nxt_out()
                    self.rope(rope_tiles, bank, 32, "swDk", cosK[s % 2], sinK[s % 2], s, o.t[:32, :], o.b, "dk")
                    store(self.KR[:, sl], o, 32)
                    gq_, gkv_ = self.gn["d_q_norm"], self.gn["d_kv_norm"]
                    cq_q = Tile(cq.t[:, 0:3, :]); cq_q.b = cq.b
                    cqn_q = Tile(cqn.t[:, 0:3, :]); cqn_q.b = cqn.b
                    self.rms(cq_q, 3, 384, lambda kc: gq_.t[:, kc:kc + 1], cqn_q, sq2, rstd2, ssbank)
                    cq_k = Tile(cq.t[:, 3:5, :]); cq_k.b = cq.b
                    cqn_k = Tile(cqn.t[:, 3:5, :]); cqn_k.b = cqn.b
                    self.rms(cq_k, 2, 256, lambda kc: gkv_.t[:, kc:kc + 1], cqn_k, sq2, rstd2, ssbank)
                    for hh in range(16):
                        bank = self.ps[nb % 4]; nb += 1
                        proj(bank, wuq, 3, hh * 96, 96, cqn_q)
                        o = nxt_out()
                        self.rope(rope_tiles, bank, 96, "swD", cs, sn, s, o.t[:96, :], o.b, "d")
                        store(self.QT[hh, 0:96, sl], o, 96)
                    for hh in range(16):
                        bank = self.ps[nb % 4]; nb += 1
                        proj(bank, wukv, 2, hh * 128, 128, cqn_k)
                        o = nxt_out()
                        self.cp(o.t[:], bank.t[:], bank.b, o.b, q="act")
                        store(self.KT[hh, 0:64, sl], o, 64, 0)
                        store(self.VT[hh, 0:64, sl], o, 64, 64)
            self.S.emit_phase()

    def pspair(self, i):
        t = Tile(self.psbig[:, i * 512:(i + 2) * 512])
        t.b = self.ps[i].b + self.ps[i + 1].b
        return t

    def phase_attn(self, m):
        N, NST, NT, NH = self.N, self.NST, self.NT, self.NH
        diff = (m == 2)
        if m == 1:
            nstream, dqk, dv, scale = 16, 64, 64, 64 ** -0.5
        elif m == 2:
            nstream, dqk, dv, scale = 8, 64, 128, 64 ** -0.5
        else:
            nstream, dqk, dv, scale = 16, 96, 64, 96 ** -0.5
        with contextlib.ExitStack() as st:
            self.common_tiles(st)
            ncomp = 2 if diff else 1
            dqp = 128 if dqk == 64 else dqk
            kts = [[self.sb(st, "at_k%d_%d" % (i, c), [dqp, N], BF16) for c in range(ncomp)] for i in range(2)]
            vts = [self.sb(st, "at_v%d" % i, [dv, N], BF16) for i in range(2)]
            dva = dv if diff else dv + 1
            vaug = [self.sb(st, "at_va%d" % i, [128, NT, dva], BF16) for i in range(2)]
            if not diff:
                for i in range(2):
                    self.cp(vaug[i].t[:, :, dv:dv + 1], self.onesb[:, 0:NT].rearrange("p (t o) -> p t o", o=1),
                            self.cbf.b, vaug[i].b)
            NQ = 3
            qs = [[self.sb(st, "at_q%d_%d" % (i, c), [dqp, ST], BF16) for c in range(ncomp)] for i in range(NQ)]
            if dqp != dqk:
                for grp_ in (kts, qs):
                    for row in grp_:
                        for t_ in row:
                            self.memset(t_.t[dqk:dqp, :], 0.0, t_.b)
            NP = 5 if diff else 4
            pts = [self.sb(st, "at_p%d" % i, [128, 2 * ST], BF16) for i in range(NP)]
            rec = self.sb(st, "at_rec", [128, ST], F32)
            rec2 = self.sb(st, "at_rec2", [128, ST], F32)
            bcs = self.sb(st, "at_bcs", [128, ST], F32)
            osb = [self.sb(st, "at_o%d" % i, [128, ST], BF16) for i in range(2)]
            sgrp = [self.pspair(0), self.pspair(2)] if diff else [self.pspair(0), self.pspair(2), self.pspair(4)]
            NSG = len(sgrp)
            if diff:
                o32 = self.sb(st, "at_o32", [128, 1, ST], F32)
                t32 = self.sb(st, "at_t32", [128, ST], F32)
                osq = self.sb(st, "at_osq", [128, 1, ST], BF16)
                rstd = self.sb(st, "at_rstd", [128, ST], F32)
                accs = [self.sb(st, "at_acc%d" % i, [128, 2 * ST], F32) for i in range(2)]
                tmps = [self.sb(st, "at_tmp%d" % i, [128, 2 * ST], BF16) for i in range(2)]
                obs = [self.ps[4], self.ps[5]]
                dbank = self.ps[6]
            units = [(h, s) for h in range(nstream) for s in range(NST)]
            NG = NT if diff else NT // 2
            steps = [(u, g) for u in range(len(units)) for g in range(NG)]

            def ld_head(h):
                i = h % 2
                for c in range(ncomp):
                    kt = kts[i][c]
                    if m == 1:
                        src = self.KT[h // 8, ((h // 4) % 2) * 64:((h // 4) % 2) * 64 + 64, :]
                        self.load(kt.t[0:64, :], src, [], kt.b)
                    elif m == 2:
                        self.load(kt.t[0:64, :], self.KT[h, c * 64:(c + 1) * 64, :], [], kt.b)
                    else:
                        self.load(kt.t[0:64, :], self.KT[h, 0:64, :], [], kt.b)
                        self.load(kt.t[64:96, :], self.KR[:, :], [], kt.b)
                vt = vts[i]
                if m == 1:
                    self.load(vt.t[:], self.VT[h // 8, ((h // 4) % 2) * 64:((h // 4) % 2) * 64 + 64, :], [], vt.b)
                elif m == 2:
                    self.load(vt.t[:], self.VT[h, :, :], [], vt.b)
                else:
                    self.load(vt.t[:], self.VT[h, 0:64, :], [], vt.b)

            def ld_q(u):
                h, s = units[u]
                sl = slice(s * ST, (s + 1) * ST)
                for c in range(ncomp):
                    q = qs[u % NQ][c]
                    if m == 1:
                        self.load(q.t[0:64, :], self.QT[h // 2, (h % 2) * 64:(h % 2) * 64 + 64, sl], [], q.b)
                    elif m == 2:
                        self.load(q.t[0:64, :], self.QT[h, c * 64:(c + 1) * 64, sl], [], q.b)
                    else:
                        self.load(q.t[:], self.QT[h, 0:96, sl], [], q.b)

            def prep_v(h):
                i = h % 2
                vt, va = vts[i], vaug[i]
                per = 1024 // dv
                for t0 in range(0, NT, per):
                    for j in range(per):
                        self.tr(self.psb.t[:, j * dv:(j + 1) * dv], vt.t[:, (t0 + j) * 128:(t0 + j + 1) * 128],
                                self.idb[0:dv, 0:dv], vt.b + self.cbf.b, self.psb.b)
                    self.cp(va.t[:, t0:t0 + per, 0:dv], self.psb.t[:, 0:per * dv].rearrange("p (t d) -> p t d", d=dv),
                            self.psb.b, va.b, q="dve")

            def stageA(i):
                u, g = steps[i]
                h, s = units[u]
                grp = sgrp[i % NSG]
                for j in range(2):
                    if diff:
                        t, c = g, j
                    else:
                        t, c = 2 * g + j, 0
                    kt = kts[h % 2][c]
                    q = qs[u % NQ][c]
                    self.mm(grp.t[:, j * ST:(j + 1) * ST], kt.t[:, t * 128:(t + 1) * 128], q.t[:], True, True,
                            kt.b + q.b, [grp.b[j]])

            def stageE(i):
                u, g = steps[i]
                h, s = units[u]
                grp = sgrp[i % NSG]
                p = pts[i % NP]
                t0 = g if diff else 2 * g
                qhalf = (s * ST) // NH
                if (t0 * 128) // NH == qhalf:
                    self.act(p.t[:], grp.t[:], AF.Exp, grp.b, p.b, scale=scale)
                else:
                    self.act(p.t[:], grp.t[:], AF.Exp, grp.b + self.xbias.b, p.b, scale=scale, bias=self.xbias.t[:, 1:2])

            def stageB(i):
                u, g = steps[i]
                h, s = units[u]
                p = pts[i % NP]
                va = vaug[h % 2]
                if not diff:
                    ob = self.ps[6]
                    for j in range(2):
                        t = 2 * g + j
                        self.mm(ob.t[0:dv + 1, :], va.t[:, t, :], p.t[:, j * ST:(j + 1) * ST], t == 0, t == NT - 1,
                                va.b + p.b, ob.b)
                else:
                    t = g
                    for c in range(2):
                        self.mm(obs[c].t[:], va.t[:, t, :], p.t[:, c * ST:(c + 1) * ST], t == 0, t == NT - 1,
                                va.b + p.b, obs[c].b)
                    if t % 2 == 1:
                        pprev = pts[(i - 1) % NP]
                        ai = (t // 2) % 2
                        tmp = tmps[ai]
                        self.tt(tmp.t[:], pprev.t[:], p.t[:], ALU.add, pprev.b + p.b, tmp.b)
                        eng = "pool" if ai else "dve"
                        if t // 2 < 2:
                            self.cp(accs[ai].t[:], tmp.t[:], tmp.b, accs[ai].b, q=eng)
                        else:
                            self.tt(accs[ai].t[:], accs[ai].t[:], tmp.t[:], ALU.add, tmp.b + accs[ai].b, accs[ai].b, q=eng)

            def epilogue(u):
                h, s = units[u]
                if not diff:
                    ob = self.ps[6]
                    self.recip(rec.t[64:65, :], ob.t[64:65, :], ob.b, rec.b)
                    bb = self.ps[7]
                    self.mm(bb.t[0:64, :], self.ones32[64:65, 0:64], rec.t[64:65, :], True, True, rec.b + self.c32.b, bb.b)
                    self.cp(bcs.t[0:64, :], bb.t[0:64, :], bb.b, bcs.b, q="act")
                    o = osb[u % 2]
                    self.tt(o.t[0:64, :], ob.t[0:64, :], bcs.t[0:64, :], ALU.mult, ob.b + bcs.b, o.b)
                    self.load(self.AT[h // 2, (h % 2) * 64:(h % 2) * 64 + 64, s * ST:(s + 1) * ST], o.t[0:64, :], o.b, [], q="pool")
                else:
                    self.tt(accs[0].t[:], accs[0].t[:], accs[1].t[:], ALU.add, accs[0].b + accs[1].b, accs[0].b)
                    for c, r_ in ((0, rec), (1, rec2)):
                        self.mm(dbank.t[:], self.ones32, accs[0].t[:, c * ST:(c + 1) * ST], True, True,
                                accs[0].b + self.c32.b, dbank.b)
                        self.recip(r_.t[:], dbank.t[:], dbank.b, r_.b)
                    self.tt(o32.t[:, 0, :], obs[0].t[:], rec.t[:], ALU.mult, obs[0].b + rec.b, o32.b)
                    self.tt(t32.t[:], obs[1].t[:], rec2.t[:], ALU.mult, obs[1].b + rec2.b, t32.b)
                    self.stt(o32.t[:, 0, :], t32.t[:], self.neglam.t[:], o32.t[:, 0, :], ALU.mult, ALU.add,
                             t32.b + o32.b + self.neglam.b, o32.b)
                    o = osb[u % 2]
                    ot = Tile(o.t[:].rearrange("p (o t) -> p o t", o=1)); ot.b = o.b
                    self.rms(o32, 1, 128, lambda kc: self.csubg.t[:], ot, osq, rstd, dbank)
                    self.load(self.AT[h, :, s * ST:(s + 1) * ST], o.t[:], o.b, [], q="pool")

            ld_head(0)
            ld_q(0)
            ld_q(1)
            LA = NSG - 1
            for i0 in range(LA):
                stageA(i0)
            for i, (u, g) in enumerate(steps):
                h, s = units[u]
                if g == 0:
                    if s == 0:
                        prep_v(h)
                        if h + 1 < nstream:
                            ld_head(h + 1)
                    if u + 2 < len(units):
                        ld_q(u + 2)
                if i + LA < len(steps):
                    stageA(i + LA)
                stageE(i)
                stageB(i)
                if g == NG - 1:
                    epilogue(u)
            self.S.emit_phase()

    def phase_attn_A(self):
        N, NT, NH, NST = self.N, self.NT, self.NH, self.NST
        scale = 64 ** -0.5
        with contextlib.ExitStack() as st:
            self.common_tiles(st)
            qkv = [[self.sb(st, "aa_%s%d" % (n, i), [128 if n != "v" else 64, N], BF16) for n in "qkv"] for i in range(2)]
            for i in range(2):
                for j in range(2):
                    self.memset(qkv[i][j].t[64:128, :], 0.0, qkv[i][j].b)
            vaug = [self.sb(st, "aa_va%d" % i, [128, NT, 65], BF16) for i in range(2)]
            for i in range(2):
                self.cp(vaug[i].t[:, :, 64:65], self.onesb[:, 0:NT].rearrange("p (t o) -> p t o", o=1), self.cbf.b, vaug[i].b)
            acc = self.sb(st, "aa_acc", [65, N], F32)
            NP = 3
            es = [self.sb(st, "aa_e%d" % i, [128, 3, 128], BF16) for i in range(NP)]
            pp = [self.sb(st, "aa_p%d" % i, [128, 3, 128], BF16) for i in range(NP)]
            rec = self.sb(st, "aa_rec", [65, ST], F32)
            osb = [self.sb(st, "aa_o%d" % i, [64, ST], BF16) for i in range(2)]
            accsync = Buf()
            combos = [(h, g) for h in range(8) for g in range(3)]

            def ld(ci):
                h, g = combos[ci]
                for j, src in enumerate((self.QT, self.KT, self.VT)):
                    t = qkv[ci % 2][j]
                    self.load(t.t[0:64, :], src[g * 4 + h // 2, (h % 2) * 64:(h % 2) * 64 + 64, :], [], t.b)

            def toks(d, s, r, i):
                start = s * NH + (128 * i) * d + r
                return slice(start, start + 127 * d + 1, d)

            ld(0)
            ui = 0
            for ci, (h, g) in enumerate(combos):
                if ci + 1 < len(combos):
                    ld(ci + 1)
                d = A_DILS[g]
                q, k, v = qkv[ci % 2]
                va = vaug[ci % 2]
                npp = NH // d // 128
                tiles = [(s, r, i) for s in range(2) for r in range(d) for i in range(npp)]
                tidx = {t: j for j, t in enumerate(tiles)}
                for t0 in range(0, NT, 16):
                    for j in range(16):
                        s, r, i = tiles[t0 + j]
                        self.tr(self.psb.t[:, j * 64:(j + 1) * 64], v.t[:, toks(d, s, r, i)], self.idb[0:64, 0:64],
                                v.b + self.cbf.b, self.psb.b)
                    self.cp(va.t[:, t0:t0 + 16, 0:64], self.psb.t[:, :].rearrange("p (t d) -> p t d", d=64),
                            self.psb.b, va.b, q="dve")
                first_evac = [True]

                def nbrs(s, r, i):
                    nb = []
                    if i > 0:
                        nb.append(((s, r, i - 1), 0))
                    elif s == 1:
                        nb.append(((0, r, npp - 1), 3))
                    nb.append(((s, r, i), 1))
                    if i < npp - 1:
                        nb.append(((s, r, i + 1), 2))
                    elif s == 0:
                        nb.append(((1, r, 0), 4))
                    return nb

                def stA(j_, u_):
                    s, r, i = tiles[j_]
                    sb_ = self.ps[u_ % 3]
                    for j, (kt_, mi) in enumerate(nbrs(s, r, i)):
                        self.mm(sb_.t[:, j * 128:(j + 1) * 128], k.t[:, toks(d, *kt_)], q.t[:, toks(d, s, r, i)],
                                True, True, k.b + q.b, sb_.b)

                def stEB(j_, u_):
                    s, r, i = tiles[j_]
                    nb = nbrs(s, r, i)
                    nn = len(nb)
                    sb_ = self.ps[u_ % 3]
                    e = es[u_ % NP]
                    p = pp[u_ % NP]
                    ob = self.ps[3 + (u_ % 2)]
                    self.act(e.t[:, 0:nn, :], sb_.t[:, 0:nn * 128].rearrange("p (j t) -> p j t", t=128), AF.Exp,
                             sb_.b, e.b, scale=scale)
                    for j, (kt_, mi) in enumerate(nb):
                        self.tt(p.t[:, j, :], e.t[:, j, :], self.maskA.t[:, mi, :], ALU.mult, e.b + self.maskA.b, p.b,
                                q=("pool" if j == 1 else "dve"))
                    for j, (kt_, mi) in enumerate(nb):
                        self.mm(ob.t[0:65, 0:128], va.t[:, tidx[kt_], :], p.t[:, j, :], j == 0, j == nn - 1, va.b + p.b, ob.b)
                    asl = acc.t[:, toks(d, s, r, i)]
                    rd = [accsync] if first_evac[0] else []
                    first_evac[0] = False
                    if g == 0:
                        self.cp(asl, ob.t[0:65, 0:128], ob.b + rd, [], q="dve")
                    else:
                        self.tt(asl, asl, ob.t[0:65, 0:128], ALU.add, ob.b + rd, [])

                stA(0, ui)
                for j_ in range(len(tiles)):
                    if j_ + 1 < len(tiles):
                        stA(j_ + 1, ui + 1)
                    stEB(j_, ui)
                    ui += 1
                self.memset(rec.t[0:1, 0:1], 0.0, [accsync])
                if g == 2:
                    for ck in range(NST):
                        sl = slice(ck * ST, (ck + 1) * ST)
                        self.recip(rec.t[64:65, :], acc.t[64:65, sl], [accsync] + rec.b, rec.b)
                        bb = self.ps[5]
                        self.mm(bb.t[0:64, :], self.ones32[64:65, 0:64], rec.t[64:65, :], True, True, rec.b + self.c32.b, bb.b)
                        o = osb[ck % 2]
                        self.tt(o.t[:], acc.t[0:64, sl], bb.t[0:64, :], ALU.mult, bb.b, o.b)
                        self.load(self.AT[h // 2, (h % 2) * 64:(h % 2) * 64 + 64, sl], o.t[:], o.b, [], q="pool")
                    self.memset(rec.t[0:1, 0:1], 0.0, [accsync])
            self.S.emit_phase()

    def phase_p2b(self, l, m):
        N, NST, NH = self.N, self.NST, self.NH
        last = (l == DEPTH - 1)
        woname = ("a_w_out", "b_w_out", "c_w_out", "d_w_out")[m]
        dvp = 128
        nh = WSHAPES[woname][1] // dvp
        xscale = 128 ** -0.5
        with contextlib.ExitStack() as st:
            self.common_tiles(st)
            kmT = self.sb(st, "pb_kmT", [128, 2, 4, 256], BF16)
            vm = self.sb(st, "pb_vm", [128, 2, 2, 512], BF16)
            ssbank = self.ps[4]
            with contextlib.ExitStack() as st2:
                wkv = self.sb(st2, "pb_wkv", [128, KC, D], BF16)
                self.load(wkv.t[:], self.w_bf["w_xkv"][l].rearrange("(kc p) n -> p kc n", p=128), [self.wB["w_xkv"]], wkv.b)
                mtok = self.sb(st2, "pb_mtok", [128, 2, D], F32)
                mT = self.sb(st2, "pb_mT", [128, KC, 256], F32)
                mTn = self.sb(st2, "pb_mTn", [128, KC, 256], BF16)
                msq = self.sb(st2, "pb_msq", [128, KC, 256], BF16)
                mrs = self.sb(st2, "pb_mrs", [128, ST], F32)
                gmem = self.gn["norm_mem"]
                for hf in range(2):
                    self.load(mtok.t[:], self.mem_in[hf].rearrange("(t p) d -> p t d", p=128), [], mtok.b)
                    for kc in range(KC):
                        bank = self.ps[kc % 4]
                        for t in range(2):
                            self.tr(bank.t[:, t * 128:(t + 1) * 128], mtok.t[:, t, kc * 128:(kc + 1) * 128], self.idf,
                                    mtok.b + self.c32.b, bank.b)
                        self.cp(mT.t[:, kc, :], bank.t[:, 0:256], bank.b, mT.b, q=("act" if kc % 2 else "dve"))
                    self.rms(mT, KC, D, lambda kc: gmem.t[:, l, kc:kc + 1], mTn, msq, mrs, ssbank, cols=256)
                    for hd in range(4):
                        bank = self.ps[hd % 4]
                        for kc in range(KC):
                            self.mm(bank.t[:, 0:256], wkv.t[:, kc, hd * 128:(hd + 1) * 128], mTn.t[:, kc, :], kc == 0,
                                    kc == KC - 1, wkv.b + mTn.b, bank.b)
                        self.cp(kmT.t[:, hf, hd, :], bank.t[:, 0:256], bank.b, kmT.b, q="act")
                    for t in range(2):
                        bank = self.ps[t % 4]
                        for kc in range(KC):
                            self.mm(bank.t[:], mTn.t[:, kc, t * 128:(t + 1) * 128], wkv.t[:, kc, 512:1024], kc == 0,
                                    kc == KC - 1, wkv.b + mTn.b, bank.b)
                        self.cp(vm.t[:, hf, t, :], bank.t[:], bank.b, vm.b, q="dve")
                self.S.emit_phase()
            wo = self.sb(st, "pb_wo", [dvp, nh, D], BF16)
            self.load(wo.t[:], self.w_bf[woname][0].rearrange("(h p) n -> p h n", p=dvp), [self.wB[woname]], wo.b)
            wxq = self.sb(st, "pb_wxq", [128, KC, 512], BF16)
            self.load(wxq.t[:], self.w_bf["w_xq"][l].rearrange("(kc p) n -> p kc n", p=128), [self.wB["w_xq"]], wxq.b)
            wxo = self.sb(st, "pb_wxo", [128, 4, D], BF16)
            self.load(wxo.t[:], self.w_bf["w_xo"][l].rearrange("(kc p) n -> p kc n", p=128), [self.wB["w_xo"]], wxo.b)
            NW1, NW2 = 2, 2
            w1 = [self.sb(st, "pb_w1_%d" % i, [128, KC, 512], BF16) for i in range(NW1)]
            w2 = [self.sb(st, "pb_w2_%d" % i, [128, 16, 256], BF16) for i in range(NW2)]
            xs = [self.sb(st, "pb_x%d" % i, [128, KC, ST], F32, nb=KC) for i in range(2)]
            ats = [self.sb(st, "pb_at%d" % i, [dvp, nh, ST], BF16) for i in range(2)]
            hT = self.sb(st, "pb_h", [128, KC, ST], BF16)
            sq = self.sb(st, "pb_sq", [128, KC, ST], BF16)
            rstd = self.sb(st, "pb_rstd", [128, ST], F32)
            qx = self.sb(st, "pb_qx", [128, 4, ST], BF16)
            px = [self.sb(st, "pb_px%d" % i, [128, ST], BF16) for i in range(2)]
            recx = self.sb(st, "pb_recx", [128, ST], F32)
            ox = self.sb(st, "pb_ox", [128, 4, ST], BF16)
            aT = self.sb(st, "pb_a", [128, 16, ST], BF16, nb=16)
            rl = [self.sb(st, "pb_rl%d" % i, [128, ST], F32) for i in range(2)]
            if last:
                ytok = [self.sb(st, "pb_ytok%d" % i, [128, D], F32) for i in range(2)]

            def ld(s):
                x = xs[s % 2]
                self.load(x.t[:], self.xT[:, :, s * ST:(s + 1) * ST].rearrange("kc p t -> p kc t"), [], x.all)
                a = ats[s % 2]
                self.load(a.t[:], self.AT[0:nh, 0:dvp, s * ST:(s + 1) * ST].rearrange("h p t -> p h t"), [], a.b)

            w1i, w2i, nb = [0], [0], [0]

            def ld_w1(j):
                t = w1[w1i[0] % NW1]; w1i[0] += 1
                self.load(t.t[:], self.w_bf["w_mlp_in"][l][:, j * 512:(j + 1) * 512].rearrange("(kc p) n -> p kc n", p=128),
                          [self.wB["w_mlp_in"]], t.b)
                return t

            def ld_w2(j):
                hf2, c = j // 4, j % 4
                t = w2[w2i[0] % NW2]; w2i[0] += 1
                self.load(t.t[:], self.w_bf["w_mlp_out"][l][hf2 * 2048:(hf2 + 1) * 2048, c * 256:(c + 1) * 256]
                          .rearrange("(fc p) n -> p fc n", p=128), [self.wB["w_mlp_out"]], t.b)
                return t

            def bank_():
                b = self.ps[nb[0] % 4]; nb[0] += 1
                return b

            ld(0)
            for s in range(NST):
                if s + 1 < NST:
                    ld(s + 1)
                x, a = xs[s % 2], ats[s % 2]
                hf = (s * ST) // NH
                w1q = [ld_w1(0), ld_w1(1)]
                for oc in range(KC):
                    bank = bank_()
                    for hh in range(nh):
                        self.mm(bank.t[:], wo.t[:, hh, oc * 128:(oc + 1) * 128], a.t[:, hh, :], hh == 0, hh == nh - 1,
                                wo.b + a.b, bank.b)
                    self.tt(x.t[:, oc, :], x.t[:, oc, :], bank.t[:], ALU.add, [x.b[oc]] + bank.b, [x.b[oc]])
                gx = self.gn["norm_x"]
                self.rms(x, KC, D, lambda kc: gx.t[:, l, kc:kc + 1], hT, sq, rstd, ssbank)
                for hd in range(4):
                    bank = bank_()
                    for kc in range(KC):
                        self.mm(bank.t[:], wxq.t[:, kc, hd * 128:(hd + 1) * 128], hT.t[:, kc, :], kc == 0, kc == KC - 1,
                                wxq.b + hT.b, bank.b)
                    self.cp(qx.t[:, hd, :], bank.t[:], bank.b, qx.b, q="act")
                for hd in range(4):
                    ob, db = self.ps[5], self.ps[6]
                    for t in range(2):
                        bank = bank_()
                        p = px[t]
                        self.mm(bank.t[:], kmT.t[:, hf, hd, t * 128:(t + 1) * 128], qx.t[:, hd, :], True, True,
                                kmT.b + qx.b, bank.b)
                        self.act(p.t[:], bank.t[:], AF.Exp, bank.b, p.b, scale=xscale)
                        self.mm(ob.t[:], vm.t[:, hf, t, hd * 128:(hd + 1) * 128], p.t[:], t == 0, t == 1, vm.b + p.b, ob.b)
                        self.mm(db.t[:], self.onesb, p.t[:], t == 0, t == 1, self.cbf.b + p.b, db.b)
                    self.recip(recx.t[:], db.t[:], db.b, recx.b)
                    self.tt(ox.t[:, hd, :], ob.t[:], recx.t[:], ALU.mult, ob.b + recx.b, ox.b)
                for oc in range(KC):
                    bank = bank_()
                    for hd in range(4):
                        self.mm(bank.t[:], wxo.t[:, hd, oc * 128:(oc + 1) * 128], ox.t[:, hd, :], hd == 0, hd == 3,
                                wxo.b + ox.b, bank.b)
                    self.tt(x.t[:, oc, :], x.t[:, oc, :], bank.t[:], ALU.add, [x.b[oc]] + bank.b, [x.b[oc]])
                gm = self.gn["norm_mlp"]
                self.rms(x, KC, D, lambda kc: gm.t[:, l, kc:kc + 1], hT, sq, rstd, ssbank)
                for hf2 in range(2):
                    w2q = [ld_w2(hf2 * 4 + 0)]
                    for j in range(4):
                        wt = w1q.pop(0)
                        for f in range(4):
                            fc = j * 4 + f
                            bank = bank_()
                            for kc in range(KC):
                                self.mm(bank.t[:], wt.t[:, kc, f * 128:(f + 1) * 128], hT.t[:, kc, :], kc == 0, kc == KC - 1,
                                        wt.b + hT.b, bank.b)
                            r = rl[fc % 2]
                            self.act(r.t[:], bank.t[:], AF.Relu, bank.b, r.b)
                            self.tt(aT.t[:, fc, :], r.t[:], r.t[:], ALU.mult, r.b, [aT.b[fc]], q="pool")
                        jj = hf2 * 4 + j
                        if jj + 2 < 8:
                            w1q.append(ld_w1(jj + 2))
                        if j == 2:
                            w2q.append(ld_w2(hf2 * 4 + 1))
                    for j in range(4):
                        wt = w2q.pop(0)
                        for o2 in range(2):
                            oc = j * 2 + o2
                            bank = bank_()
                            for fc in range(16):
                                self.mm(bank.t[:], wt.t[:, fc, o2 * 128:(o2 + 1) * 128], aT.t[:, fc, :], fc == 0, fc == 15,
                                        wt.b + [aT.b[fc]], bank.b)
                            self.tt(x.t[:, oc, :], x.t[:, oc, :], bank.t[:], ALU.add, [x.b[oc]] + bank.b, [x.b[oc]])
                        if j + 2 < 4:
                            w2q.append(ld_w2(hf2 * 4 + j + 2))
                if not last:
                    self.load(self.xT[:, :, s * ST:(s + 1) * ST].rearrange("kc p t -> p kc t"), x.t[:], x.all, [], q="pool")
                else:
                    gf = self.gn["final_norm"]
                    self.rms(x, KC, D, lambda kc: gf.t[:, kc:kc + 1], x, sq, rstd, ssbank)
                    yT = x
                    for tt_ in range(4):
                        yt = ytok[tt_ % 2]
                        for half in range(2):
                            bank = bank_()
                            for k4 in range(4):
                                kc = half * 4 + k4
                                self.tr(bank.t[:, k4 * 128:(k4 + 1) * 128], yT.t[:, kc, tt_ * 128:(tt_ + 1) * 128], self.idf,
                                        [yT.b[kc]] + self.c32.b, bank.b)
                            self.cp(yt.t[:, half * 512:(half + 1) * 512], bank.t[:], bank.b, yt.b, q=("act" if half else "dve"))
                        r0 = s * ST + tt_ * 128
                        self.load(self.y_out[r0:r0 + 128, :], yt.t[:], yt.b, [], q="pool")
            self.S.emit_phase()

    def phase_final_only(self):
        NST = self.NST
        with contextlib.ExitStack() as st:
            self.common_tiles(st)
            xs = [self.sb(st, "pf_x%d" % i, [128, KC, ST], F32) for i in range(2)]
            yT = self.sb(st, "pf_yT", [128, KC, ST], F32)
            sq = self.sb(st, "pf_sq", [128, KC, ST], BF16)
            rstd = self.sb(st, "pf_rstd", [128, ST], F32)
            ytok = [self.sb(st, "pf_ytok%d" % i, [128, D], F32) for i in range(2)]
            nb = 0
            for s in range(NST):
                x = xs[s % 2]
                self.load(x.t[:], self.xT[:, :, s * ST:(s + 1) * ST].rearrange("kc p t -> p kc t"), [], x.b)
                gf = self.gn["final_norm"]
                self.rms(x, KC, D, lambda kc: gf.t[:, kc:kc + 1], yT, sq, rstd, self.ps[4])
                for tt_ in range(4):
                    yt = ytok[tt_ % 2]
                    for half in range(2):
                        bank = self.ps[nb % 4]; nb += 1
                        for k4 in range(4):
                            kc = half * 4 + k4
                            self.tr(bank.t[:, k4 * 128:(k4 + 1) * 128], yT.t[:, kc, tt_ * 128:(tt_ + 1) * 128], self.idf,
                                    yT.b + self.c32.b, bank.b)
                        self.cp(yt.t[:, half * 512:(half + 1) * 512], bank.t[:], bank.b, yt.b, q=("act" if half else "dve"))
                    r0 = s * ST + tt_ * 128
                    self.load(self.y_out[r0:r0 + 128, :], yt.t[:], yt.b, [], q="pool")
            self.S.emit_phase()


def _rope_tab(pos, theta, rot):
    half = rot // 2
    inv = np.exp(np.arange(half, dtype=np.float32) * np.float32(-2.0 * math.log(theta) / rot)).astype(np.float32)
    ang = pos.astype(np.float32)[None, :] * inv[:, None]
    return np.cos(ang).astype(np.float32), np.sin(ang).astype(np.float32)


def host_tables(NH, is_pair):
    N = 2 * NH
    t = np.arange(N)
    pos = (t % NH) if is_pair else t
    out = {}
    c, s = _rope_tab(pos, 500000.0, 16)
    cosA = np.ones((128, N), np.float32)
    sinA = np.zeros((128, N), np.float32)
    swA = np.zeros((128, 128), np.float32)
    for u in range(2):
        for i in range(8):
            cosA[u * 64 + i] = c[i]; cosA[u * 64 + 8 + i] = c[i]
            sinA[u * 64 + i] = -s[i]; sinA[u * 64 + 8 + i] = s[i]
            swA[u * 64 + 8 + i, u * 64 + i] = 1.0
            swA[u * 64 + i, u * 64 + 8 + i] = 1.0
    out.update(tA_cos=cosA, tA_sin=sinA, swA=swA)
    rows = pos // 64
    cols = pos % 64
    cr, sr = _rope_tab(rows, 10000.0, 32)
    cc, sc = _rope_tab(cols, 10000.0, 32)
    cosB = np.ones((128, N), np.float32)
    sinB = np.zeros((128, N), np.float32)
    swB = np.zeros((128, 128), np.float32)
    for u in range(2):
        for off, (c_, s_) in ((0, (cr, sr)), (32, (cc, sc))):
            for i in range(16):
                a, b = u * 64 + off + i, u * 64 + off + 16 + i
                cosB[a] = c_[i]; cosB[b] = c_[i]
                sinB[a] = -s_[i]; sinB[b] = s_[i]
                swB[b, a] = 1.0
                swB[a, b] = 1.0
    out.update(tB_cos=cosB, tB_sin=sinB, swB=swB)
    c, s = _rope_tab(pos, 500000.0, 32)
    cosD = np.ones((96, N), np.float32)
    sinD = np.zeros((96, N), np.float32)
    swD = np.zeros((96, 96), np.float32)
    cosK = np.ones((32, N), np.float32)
    sinK = np.zeros((32, N), np.float32)
    swK = np.zeros((32, 32), np.float32)
    for i in range(16):
        cosD[64 + i] = c[i]; cosD[80 + i] = c[i]
        sinD[64 + i] = -s[i]; sinD[80 + i] = s[i]
        swD[80 + i, 64 + i] = 1.0
        swD[64 + i, 80 + i] = 1.0
        cosK[i] = c[i]; cosK[16 + i] = c[i]
        sinK[i] = -s[i]; sinK[16 + i] = s[i]
        swK[16 + i, i] = 1.0
        swK[i, 16 + i] = 1.0
    out.update(tD_cos=cosD, tD_sin=sinD, swD=swD, tDk_cos=cosK, tDk_sin=sinK, swDk=swK)
    kk = np.arange(128)[:, None]
    qq = np.arange(128)[None, :]
    mprev = (kk >= qq + 64).astype(np.float32)
    mcur = (np.abs(qq - kk) <= 64).astype(np.float32)
    mnext = (kk <= qq - 64).astype(np.float32)
    x = 0.0 if is_pair else 1.0
    out["maskA"] = np.ascontiguousarray(np.stack([mprev, mcur, mnext, mprev * x, mnext * x], axis=1))
    xb = np.zeros((128, 2), np.float32)
    xb[:, 1] = -30000.0 if is_pair else 0.0
    out["xbias"] = xb
    cst = np.zeros((128, 3, 128), np.float32)
    cst[:, 0, :] = np.eye(128, dtype=np.float32)
    cst[:, 1, :] = 1.0
    cst[0:64, 2, 0:64] = 1.0
    cst[64:128, 2, 64:128] = 1.0
    out["cst"] = cst
    return out


_PROG_CACHE = {}


def run_slots(slots, weights, NH, depth=DEPTH):
    key = (NH, depth)
    if key not in _PROG_CACHE:
        _PROG_CACHE[key] = Prog(NH, depth)
    prog = _PROG_CACHE[key]
    tabs = {True: host_tables(NH, True), False: host_tables(NH, False)}
    in_maps = []
    for sl in slots:
        mp = {"x_slot": np.ascontiguousarray(sl["x"], dtype=np.float32),
              "mem2": np.ascontiguousarray(sl["mem"], dtype=np.float32)}
        for n in WNAMES:
            mp[n] = weights[n]
        for n in GNAMES:
            mp[n] = weights[n]
        mp.update(tabs[bool(sl["pair"])])
        in_maps.append(mp)
    import os
    ncr = int(os.environ.get("MK_NCORES", "8"))
    res = run_bass_kernel_spmd(prog.nc, in_maps[:ncr], core_ids=list(range(ncr)))
    out = [r["y_slot"] for r in res.results]
    while len(out) < 8:
        out.append(out[-1])
    return out


def kernel(**inputs):
    NH = 4096
    xp = np.asarray(inputs["x_prompt"], dtype=np.float32)
    xs = np.asarray(inputs["x_sample"], dtype=np.float32)
    mp = np.asarray(inputs["mem_prompt"], dtype=np.float32)
    ms = np.asarray(inputs["mem_sample"], dtype=np.float32)
    weights = {n: np.ascontiguousarray(np.asarray(inputs[n], dtype=np.float32)) for n in list(WNAMES) + list(GNAMES)}
    slots = []
    for b in range(2):
        slots.append({"x": xp[b], "mem": np.stack([mp[b], mp[b]]), "pair": False})
    for j in range(4):
        slots.append({"x": xs[2 * j:2 * j + 2].reshape(2 * NH, D), "mem": ms[2 * j:2 * j + 2], "pair": True})
    for j in range(2):
        slots.append(slots[2 + j])
    ys = run_slots(slots, weights, NH)
    y_prompt = np.stack([ys[0], ys[1]]).astype(np.float32)
    y_sample = np.concatenate([ys[2 + j].reshape(2, NH, D) for j in range(4)], axis=0).astype(np.float32)
    return (y_prompt, y_sample)
```
